# Optimizing a Trainium2 kernel written in Bass

```python
import math
import jax, jax.numpy as jnp
from jax import lax
import numpy as np

D_MODEL = 1024
BATCH = 2
SEQ = 8192
DEPTH = 2

MEM_LEN = 256
EPS = 1e-6
MLA_HEADS = 8
Q_LORA = 384
KV_LORA = 256
D_NOPE = 64
D_ROPE = 32
D_QK = D_NOPE + D_ROPE
D_V = 64
MLA_WIDTH = MLA_HEADS * D_V
ROPE_THETA = 10000.0
Q_BLOCK = 128
SSM_GROUPS = 32
SSM_GROUP_CH = 16
SSM_WIDTH = SSM_GROUPS * SSM_GROUP_CH
SSM_STATE = 64
DT_MIN = 1e-3
DT_MAX = 1e-1
X_HEADS = 4
X_HEAD_DIM = 128
X_WIDTH = X_HEADS * X_HEAD_DIM
N_BRANCH = 3
D_FF = 2816
CONV_WIDTH = 3
IN_WIDTH = Q_LORA + KV_LORA + D_ROPE + SSM_WIDTH + X_WIDTH + N_BRANCH * D_MODEL

kernel_name = "hybrid_mla_s5_memxattn_convffn"


def rmsnorm(x, g):
    xf = x.astype(jnp.float32)
    y = xf * lax.rsqrt(jnp.mean(xf * xf, axis=-1, keepdims=True) + EPS)
    return (y * g.astype(jnp.float32)).astype(x.dtype)


def rope_tables(positions):
    inv_freq = ROPE_THETA ** (-jnp.arange(0, D_ROPE, 2, dtype=jnp.float32) / D_ROPE)
    ang = positions.astype(jnp.float32)[..., None] * inv_freq
    return jnp.cos(ang), jnp.sin(ang)


def apply_rope(x, cos, sin):
    xf = x.astype(jnp.float32)
    x1, x2 = jnp.split(xf, 2, axis=-1)
    return jnp.concatenate([x1 * cos - x2 * sin, x1 * sin + x2 * cos], axis=-1).astype(x.dtype)


def split_combined(proj):
    sizes = (Q_LORA, KV_LORA, D_ROPE, SSM_WIDTH, X_WIDTH)
    idx = [int(v) for v in np.cumsum(sizes)]
    return jnp.split(proj, idx, axis=-1)


def causal_block_attention(q, k, v):
    b, l, h, dq = q.shape
    nb = l // Q_BLOCK
    scale = dq ** -0.5
    qb = jnp.moveaxis(q.reshape(b, nb, Q_BLOCK, h, dq), 1, 0)
    kpos = jnp.arange(l)

    def one_block(args):
        i, qi = args
        s = jnp.einsum('bqhd,bkhd->bhqk', qi, k, preferred_element_type=jnp.float32) * scale
        qpos = i * Q_BLOCK + jnp.arange(Q_BLOCK)
        s = jnp.where(kpos[None, :] <= qpos[:, None], s, -jnp.inf)
        p = jax.nn.softmax(s, axis=-1)
        return jnp.einsum('bhqk,bkhd->bqhd', p.astype(v.dtype), v)

    out = lax.map(one_block, (jnp.arange(nb), qb))
    return jnp.moveaxis(out, 0, 1).reshape(b, l, h * v.shape[-1])


def mla_branch(c_q, c_kv, k_r, cos, sin, q_a_norm_g, w_q_b, kv_a_norm_g, w_kv_b, q_norm_g, k_norm_g):
    b, l, _ = c_q.shape
    q = (rmsnorm(c_q, q_a_norm_g) @ w_q_b).reshape(b, l, MLA_HEADS, D_QK)
    kv = (rmsnorm(c_kv, kv_a_norm_g) @ w_kv_b).reshape(b, l, MLA_HEADS, D_NOPE + D_V)
    k_nope, v = kv[..., :D_NOPE], kv[..., D_NOPE:]
    k_rope = jnp.broadcast_to(k_r[:, :, None, :], (b, l, MLA_HEADS, D_ROPE))
    k = jnp.concatenate([k_nope, k_rope], axis=-1)
    q = rmsnorm(q, q_norm_g)
    k = rmsnorm(k, k_norm_g)
    c4, s4 = cos[:, :, None, :], sin[:, :, None, :]
    q = jnp.concatenate([q[..., :D_NOPE], apply_rope(q[..., D_NOPE:], c4, s4)], axis=-1)
    k = jnp.concatenate([k[..., :D_NOPE], apply_rope(k[..., D_NOPE:], c4, s4)], axis=-1)
    return causal_block_attention(q, k, v)


def _ssm_combine(left, right):
    a1r, a1i, b1r, b1i = left
    a2r, a2i, b2r, b2i = right
    return (a2r * a1r - a2i * a1i,
            a2r * a1i + a2i * a1r,
            a2r * b1r - a2i * b1i + b2r,
            a2r * b1i + a2i * b1r + b2i)


def s5_branch(u, lam_re, lam_im, log_dt, b_re, b_im, c_re, c_im, d_skip, w_glu, b_glu):
    f32 = jnp.float32
    bsz, l, _ = u.shape
    uf = u.astype(f32).reshape(bsz, l, SSM_GROUPS, SSM_GROUP_CH)
    dt = jnp.exp(log_dt.astype(f32))[:, None]
    lr, li = lam_re.astype(f32), lam_im.astype(f32)
    mag = jnp.exp(lr * dt)
    a_re, a_im = mag * jnp.cos(li * dt), mag * jnp.sin(li * dt)
    den = lr * lr + li * li
    e_re, e_im = a_re - 1.0, a_im
    f_re = ((e_re * lr + e_im * li) / den)[..., None]
    f_im = ((e_im * lr - e_re * li) / den)[..., None]
    br, bi = b_re.astype(f32), b_im.astype(f32)
    bb_re = f_re * br - f_im * bi
    bb_im = f_re * bi + f_im * br
    bu_re = jnp.einsum('blgc,gnc->blgn', uf, bb_re)
    bu_im = jnp.einsum('blgc,gnc->blgn', uf, bb_im)
    a_re_t = jnp.broadcast_to(a_re, bu_re.shape)
    a_im_t = jnp.broadcast_to(a_im, bu_im.shape)
    _, _, s_re, s_im = lax.associative_scan(_ssm_combine, (a_re_t, a_im_t, bu_re, bu_im), axis=1)
    y = (jnp.einsum('blgn,gcn->blgc', s_re, c_re.astype(f32))
         - jnp.einsum('blgn,gcn->blgc', s_im, c_im.astype(f32))
         + d_skip.astype(f32) * uf)
    y = jax.nn.gelu(y.reshape(bsz, l, SSM_WIDTH)).astype(u.dtype)
    return y * jax.nn.sigmoid(y @ w_glu + b_glu)


def cross_branch(x_q, mem, mem_norm_g, w_mem_kv, xq_norm_g, xk_norm_g):
    b, l, _ = x_q.shape
    kv = rmsnorm(mem, mem_norm_g) @ w_mem_kv
    k = kv[..., :X_WIDTH].reshape(b, MEM_LEN, X_HEADS, X_HEAD_DIM)
    v = kv[..., X_WIDTH:].reshape(b, MEM_LEN, X_HEADS, X_HEAD_DIM)
    q = rmsnorm(x_q.reshape(b, l, X_HEADS, X_HEAD_DIM), xq_norm_g)
    k = rmsnorm(k, xk_norm_g)
    s = jnp.einsum('blhd,bmhd->bhlm', q, k, preferred_element_type=jnp.float32) * (X_HEAD_DIM ** -0.5)
    p = jax.nn.softmax(s, axis=-1)
    return jnp.einsum('bhlm,bmhd->blhd', p.astype(v.dtype), v).reshape(b, l, X_WIDTH)


def causal_dwconv(x, w, bias):
    c = x.shape[-1]
    y = lax.conv_general_dilated(x, w[:, None, :].astype(x.dtype), window_strides=(1,),
                                 padding=((CONV_WIDTH - 1, 0),),
                                 dimension_numbers=('NWC', 'WIO', 'NWC'),
                                 feature_group_count=c)
    return y + bias


def setup_inputs(seed: int = 0) -> dict:
    key = jax.random.key(seed)
    ks = iter(jax.random.split(key, 48))
    f32 = jnp.float32
    L = DEPTH

    def nrm(shape, fan_in):
        return jax.random.normal(next(ks), shape, f32) * (fan_in ** -0.5)

    def gain(shape):
        return 1.0 + 0.02 * jax.random.normal(next(ks), shape, f32)

    def small(shape):
        return 0.01 * jax.random.normal(next(ks), shape, f32)

    x = jax.random.normal(next(ks), (BATCH, SEQ, D_MODEL), f32)
    mem = jax.random.normal(next(ks), (BATCH, MEM_LEN, D_MODEL), f32)
    offset = jax.random.randint(next(ks), (BATCH, 1), 0, 1024, dtype=jnp.int32)
    positions = offset + jnp.arange(SEQ, dtype=jnp.int32)[None, :]
    n_idx = jnp.arange(SSM_STATE, dtype=f32)
    lam_re = -0.5 + small((L, SSM_GROUPS, SSM_STATE))
    lam_im = math.pi * n_idx + small((L, SSM_GROUPS, SSM_STATE))
    log_dt = jax.random.uniform(next(ks), (L, SSM_GROUPS), f32, math.log(DT_MIN), math.log(DT_MAX))
    return {
        "x": x,
        "mem": mem,
        "positions": positions,
        "norm_mix_g": gain((L, D_MODEL)),
        "w_in": nrm((L, D_MODEL, IN_WIDTH), D_MODEL),
        "q_a_norm_g": gain((L, Q_LORA)),
        "w_q_b": nrm((L, Q_LORA, MLA_HEADS * D_QK), Q_LORA),
        "kv_a_norm_g": gain((L, KV_LORA)),
        "w_kv_b": nrm((L, KV_LORA, MLA_HEADS * (D_NOPE + D_V)), KV_LORA),
        "q_norm_g": gain((L, D_QK)),
        "k_norm_g": gain((L, D_QK)),
        "w_o_mla": nrm((L, MLA_WIDTH, D_MODEL), MLA_WIDTH),
        "ssm_lambda_re": lam_re,
        "ssm_lambda_im": lam_im,
        "ssm_log_dt": log_dt,
        "ssm_b_re": nrm((L, SSM_GROUPS, SSM_STATE, SSM_GROUP_CH), 2 * SSM_GROUP_CH),
        "ssm_b_im": nrm((L, SSM_GROUPS, SSM_STATE, SSM_GROUP_CH), 2 * SSM_GROUP_CH),
        "ssm_c_re": nrm((L, SSM_GROUPS, SSM_GROUP_CH, SSM_STATE), SSM_STATE),
        "ssm_c_im": nrm((L, SSM_GROUPS, SSM_GROUP_CH, SSM_STATE), SSM_STATE),
        "ssm_d": jax.random.normal(next(ks), (L, SSM_GROUPS, SSM_GROUP_CH), f32),
        "w_glu": nrm((L, SSM_WIDTH, SSM_WIDTH), SSM_WIDTH),
        "b_glu": small((L, SSM_WIDTH)),
        "w_o_ssm": nrm((L, SSM_WIDTH, D_MODEL), SSM_WIDTH),
        "mem_norm_g": gain((L, D_MODEL)),
        "w_mem_kv": nrm((L, D_MODEL, 2 * X_WIDTH), D_MODEL),
        "xq_norm_g": gain((L, X_HEAD_DIM)),
        "xk_norm_g": gain((L, X_HEAD_DIM)),
        "w_o_cross": nrm((L, X_WIDTH, D_MODEL), X_WIDTH),
        "b_gate": small((L, N_BRANCH * D_MODEL)),
        "w_out": nrm((L, D_MODEL, D_MODEL), D_MODEL),
        "norm_ffn_g": gain((L, D_MODEL)),
        "w_up": nrm((L, D_MODEL, 2 * D_FF), D_MODEL),
        "conv_w": nrm((L, CONV_WIDTH, 2 * D_FF), CONV_WIDTH),
        "conv_b": small((L, 2 * D_FF)),
        "w_down": nrm((L, D_FF, D_MODEL), D_FF),
    }


def reference(x, mem, positions, norm_mix_g, w_in, q_a_norm_g, w_q_b, kv_a_norm_g, w_kv_b,
              q_norm_g, k_norm_g, w_o_mla, ssm_lambda_re, ssm_lambda_im, ssm_log_dt,
              ssm_b_re, ssm_b_im, ssm_c_re, ssm_c_im, ssm_d, w_glu, b_glu, w_o_ssm,
              mem_norm_g, w_mem_kv, xq_norm_g, xk_norm_g, w_o_cross, b_gate, w_out,
              norm_ffn_g, w_up, conv_w, conv_b, w_down):
    bsz, l, _ = x.shape
    cos, sin = rope_tables(positions)
    for i in range(DEPTH):
        h = rmsnorm(x, norm_mix_g[i])
        c_q, c_kv, k_r, u_ssm, x_q, gate_logits = split_combined(h @ w_in[i])
        y_a = mla_branch(c_q, c_kv, k_r, cos, sin, q_a_norm_g[i], w_q_b[i], kv_a_norm_g[i],
                         w_kv_b[i], q_norm_g[i], k_norm_g[i]) @ w_o_mla[i]
        y_b = s5_branch(u_ssm, ssm_lambda_re[i], ssm_lambda_im[i], ssm_log_dt[i], ssm_b_re[i],
                        ssm_b_im[i], ssm_c_re[i], ssm_c_im[i], ssm_d[i], w_glu[i], b_glu[i]) @ w_o_ssm[i]
        y_c = cross_branch(x_q, mem, mem_norm_g[i], w_mem_kv[i], xq_norm_g[i], xk_norm_g[i]) @ w_o_cross[i]
        gates = jax.nn.sigmoid(gate_logits + b_gate[i]).reshape(bsz, l, N_BRANCH, D_MODEL)
        merged = gates[:, :, 0] * y_a + gates[:, :, 1] * y_b + gates[:, :, 2] * y_c
        x = x + merged @ w_out[i]
        h2 = rmsnorm(x, norm_ffn_g[i])
        up = causal_dwconv(h2 @ w_up[i], conv_w[i], conv_b[i])
        g_ff, v_ff = up[..., :D_FF], up[..., D_FF:]
        x = x + (jax.nn.silu(g_ff) * v_ff) @ w_down[i]
    return x
```

```python
import math
from contextlib import ExitStack
import numpy as np
import concourse.bass as bass
import concourse.mybir as mybir
from concourse.bass_utils import run_bass_kernel_spmd

F32 = mybir.dt.float32
BF16 = mybir.dt.bfloat16
I32 = mybir.dt.int32
AF = mybir.ActivationFunctionType
ALU = mybir.AluOpType

CE = ["pe", "act", "dve", "pool", "sp"]
NLANES = 24
RG = [[0, 1, 2, 3], [4, 5, 6, 7]]

T = 2048
NB = 4
D = 1024
DFF = 2816
EPS = 1e-6
TWO_PI = 2.0 * math.pi
CW1 = 6.28125
CW2 = TWO_PI - CW1


class Buf:
    __slots__ = ("name", "w", "r")

    def __init__(self, name=""):
        self.name = name
        self.w = None
        self.r = {}


class Sched:
    def __init__(self, nc):
        self.nc = nc
        self.ccu = ["cc", "ccq", "cck", "ccv", "cca"]
        self.units = CE + ["d%d" % i for i in range(NLANES)] + self.ccu
        self.prog = {e: [] for e in CE}
        self.cnt = {u: 0 for u in self.units}
        self.seen = {u: {} for u in self.units}
        self.snaps = {u: [None] for u in self.units}
        self.lane_rr = 0
        self.sems = {}
        self.rank_cache = None

    def _need(self, reads, writes):
        need = {}
        for b in reads:
            if b.w is not None:
                u, c = b.w
                if need.get(u, 0) < c:
                    need[u] = c
        for b in writes:
            if b.w is not None:
                u, c = b.w
                if need.get(u, 0) < c:
                    need[u] = c
            for u, c in b.r.items():
                if need.get(u, 0) < c:
                    need[u] = c
        return need

    def _absorb(self, me, u, c):
        s = self.seen[me]
        if s.get(u, 0) < c:
            s[u] = c
        snap = self.snaps[u][c]
        if snap:
            for k, v in snap.items():
                if s.get(k, 0) < v:
                    s[k] = v

    def _waits(self, eng, need, skip_self=False):
        waits = []
        for u, c in sorted(need.items(), key=lambda kv: -kv[1]):
            if u == eng and skip_self:
                continue
            if self.seen[eng].get(u, 0) >= c:
                continue
            waits.append((u, c))
            self._absorb(eng, u, c)
        return waits

    def add(self, eng, fn, reads=(), writes=()):
        need = self._need(reads, writes)
        waits = self._waits(eng, need, skip_self=(eng == "pe"))
        self.cnt[eng] += 1
        n = self.cnt[eng]
        self.snaps[eng].append(dict(self.seen[eng]))
        self.prog[eng].append((waits, fn, (eng, 1)))
        for b in reads:
            b.r[eng] = n
        for b in writes:
            b.w = (eng, n)
            b.r = {}
        return n

    def dma(self, q, out, in_, reads=(), writes=(), **kw):
        def fn(e, out=out, in_=in_, kw=kw):
            return e.dma_start(out=out, in_=in_, **kw)
        return self.dma_fn(q, fn, reads, writes)

    def dma_fn(self, q, fn, reads=(), writes=()):
        lane = "d%d" % self.lane_rr
        self.lane_rr = (self.lane_rr + 1) % NLANES
        need = self._need(reads, writes)
        prev = self.cnt[lane]
        if prev > 0:
            need[lane] = max(need.get(lane, 0), prev)
        waits = self._waits(q, need)
        self.cnt[lane] += 1
        n = self.cnt[lane]
        self.snaps[lane].append(dict(self.seen[q]))
        self.prog[q].append((waits, fn, (lane, 16)))
        for b in reads:
            b.r[lane] = n
        for b in writes:
            b.w = (lane, n)
            b.r = {}
        return lane, n

    def cc(self, fn, reads=(), writes=(), unit="cc"):
        need = self._need(reads, writes)
        waits = self._waits("pool", need)
        self.cnt[unit] += 1
        n = self.cnt[unit]
        self.snaps[unit].append(dict(self.seen["pool"]))
        self.prog["pool"].append((waits, fn, (unit, 1)))
        for b in reads:
            b.r[unit] = n
        for b in writes:
            b.w = (unit, n)
            b.r = {}
        return n

    def alloc_sems(self, stack):
        for u in self.units:
            self.sems[u] = stack.enter_context(self.nc.semaphore("s_" + u))

    def barrier(self, final=False):
        for eng in CE:
            waits = []
            for u in self.units:
                c = self.cnt[u]
                if u == eng or (u in self.ccu and not final):
                    continue
                if c > 0 and self.seen[eng].get(u, 0) < c:
                    waits.append((u, c))
                    self.seen[eng][u] = c
            if waits:
                self.prog[eng].append((waits, None, None))

    def flush(self, stack, barrier=True):
        if barrier:
            self.barrier()
        nc = self.nc
        block = stack.enter_context(nc.Block())
        mult = {u: (1 if (u in CE or u in self.ccu) else 16) for u in self.units}
        prog = self.prog

        def run(engobj, items):
            self.rank_cache = None
            for waits, fn, inc in items:
                for u, c in waits:
                    engobj.wait_ge(self.sems[u], c * mult[u])
                if fn is not None:
                    ins = fn(engobj)
                    ins.then_inc(self.sems[inc[0]], inc[1])

        @block.tensor
        def _(e):
            run(e, prog["pe"])

        @block.scalar
        def _(e):
            run(e, prog["act"])

        @block.vector
        def _(e):
            run(e, prog["dve"])

        @block.gpsimd
        def _(e):
            run(e, prog["pool"])

        @block.sync
        def _(e):
            run(e, prog["sp"])

        self.prog = {e: [] for e in CE}


VC = {}
_o = 0
for _n, _w in [("mixg", 8), ("qag", 3), ("kvag", 2), ("qg", 1), ("qgsw", 1), ("kg", 1), ("kgsw", 1),
               ("xqg", 1), ("xkg", 1), ("memg", 8), ("bgate", 24), ("ffng", 8), ("cw0", 44), ("cw1", 44),
               ("cw2", 44), ("cb", 44), ("bglu", 4), ("ssmd", 4), ("lre", 16), ("lim", 16), ("ldt", 16)]:
    VC[_n] = _o
    _o += _w
NV = _o


def _chunks(v, n):
    return np.ascontiguousarray(np.asarray(v, np.float32).reshape(n, 128).T)


def pack_vec(inp, l):
    v = np.zeros((128, NV), np.float32)

    def put(name, arr):
        arr = np.asarray(arr, np.float32)
        v[: arr.shape[0], VC[name]: VC[name] + arr.shape[1]] = arr

    put("mixg", _chunks(inp["norm_mix_g"][l], 8))
    put("qag", _chunks(inp["q_a_norm_g"][l], 3))
    put("kvag", _chunks(inp["kv_a_norm_g"][l], 2))
    sw = np.concatenate([np.arange(80, 96), np.arange(64, 80)])
    for nm, g in (("q", inp["q_norm_g"][l]), ("k", inp["k_norm_g"][l])):
        g = np.asarray(g, np.float32)
        put(nm + "g", g[:, None])
        gs = np.zeros((96, 1), np.float32)
        gs[64:96, 0] = g[sw]
        put(nm + "gsw", gs)
    put("xqg", np.asarray(inp["xq_norm_g"][l])[:, None])
    put("xkg", np.asarray(inp["xk_norm_g"][l])[:, None])
    put("memg", _chunks(inp["mem_norm_g"][l], 8))
    put("bgate", _chunks(inp["b_gate"][l], 24))
    put("ffng", _chunks(inp["norm_ffn_g"][l], 8))
    for j in range(3):
        put("cw%d" % j, _chunks(inp["conv_w"][l][j], 44))
    put("cb", _chunks(inp["conv_b"][l], 44))
    put("bglu", _chunks(inp["b_glu"][l], 4))
    put("ssmd", _chunks(np.asarray(inp["ssm_d"][l]).reshape(-1), 4))
    lre = np.asarray(inp["ssm_lambda_re"][l], np.float32).reshape(16, 128).T
    lim = np.asarray(inp["ssm_lambda_im"][l], np.float32).reshape(16, 128).T
    ldt = np.repeat(np.asarray(inp["ssm_log_dt"][l], np.float32)[:, None], 64, 1).reshape(16, 128).T
    put("lre", lre)
    put("lim", lim)
    put("ldt", ldt)
    return v


def pack_bc(inp, l):
    bb = np.zeros((128, 2, 16, 32), np.float32)
    cb = np.zeros((128, 2, 16, 32), np.float32)
    for ri, (bn, cn) in enumerate((("ssm_b_re", "ssm_c_re"), ("ssm_b_im", "ssm_c_im"))):
        B = np.asarray(inp[bn][l], np.float32)
        C = np.asarray(inp[cn][l], np.float32)
        for i in range(16):
            for g2 in range(2):
                g = 2 * i + g2
                bb[g2 * 64:(g2 + 1) * 64, ri, i, g2 * 16:(g2 + 1) * 16] = B[g]
                cb[g2 * 64:(g2 + 1) * 64, ri, i, g2 * 16:(g2 + 1) * 16] = C[g].T
    return bb, cb


class _Stop(Exception):
    pass


def build(dbg=(), stop=None):
    nc = bass.Bass("TRN2", target_bir_lowering=False)
    dbg = set(dbg)

    def chk(name):
        if stop == name:
            raise _Stop()

    def ext_in(name, shape, dt=F32):
        return nc.dram_tensor(name, list(shape), dt, kind="ExternalInput").ap()

    xT_in = ext_in("xT", [D, T])
    memT_in = ext_in("memT", [D, 256])
    pos_in = ext_in("pos", [1, T], I32)
    ident_in = ext_in("ident", [128, 128])
    masks_in = ext_in("masks", [128, 4, 512])
    invf_in = ext_in("invf", [128, 1])
    W = {}
    for nm, shp in [("win", [D, 4800]), ("wqb", [384, 1024]), ("wkvk", [256, 512]), ("wkvv", [256, 512]),
                    ("womla", [512, D]), ("wossm", [512, D]), ("wocross", [512, D]), ("wglu", [512, 512]),
                    ("wmemkv", [D, D]), ("wout", [D, D]), ("wup", [D, 2 * DFF]), ("wdown", [DFF, D]),
                    ("vec", [128, NV]), ("bblk", [128, 2 * 16 * 32]), ("cblk", [128, 2 * 16 * 32])]:
        W[nm] = [ext_in("%s%d" % (nm, l), shp) for l in range(2)]
    yT_out = nc.dram_tensor("yT", [D, T], F32, kind="ExternalOutput").ap()

    scr = {}
    scr_tok = {}

    def scratch(name, shape, dt, internal=False):
        kind = "Internal"
        if name in dbg and not internal:
            kind = "ExternalOutput"
        t = nc.dram_tensor(name, list(shape), dt, kind=kind)
        scr[name] = t.ap()
        scr_tok[name] = Buf(name)
        return t.ap()

    xs = [scratch("xs0", [D, T], F32), scratch("xs1", [D, T], F32)]
    hT_d = scratch("hT", [D, T], BF16)
    cqT_d = scratch("cqT", [384, T], BF16)
    ckvT_d = scratch("ckvT", [256, T], BF16)
    krT_d = scratch("krT", [64, T], F32)
    uT_d = scratch("uT", [512, T], BF16)
    xqT_d = scratch("xqT", [512, T], BF16)
    ropeC_d = scratch("ropeC", [32, T], F32)
    ropeS_d = scratch("ropeS", [32, T], F32)
    q_own = scratch("q_own", [768, T], BF16, True)
    k_own = scratch("k_own", [768, T], BF16, True)
    v_own = scratch("v_own", [4 * T, 128], BF16, True)
    qk_all = scratch("qk_all", [2 * 4 * 768, T], BF16, True)
    q_all = qk_all[0:4 * 768, :]
    k_all = qk_all[4 * 768:8 * 768, :]
    for _n, _ap in (("q_all", q_all), ("k_all", k_all)):
        scr[_n] = _ap
        scr_tok[_n] = Buf(_n)
    v_all = scratch("v_all", [16 * T, 128], BF16, True)
    qk_mine = scratch("qk_mine", [2 * 4 * 192, T], BF16)
    q_mine = qk_mine[0:768, :]
    k_mine = qk_mine[768:1536, :]
    for _n in ("q_mine", "k_mine"):
        scr_tok[_n] = Buf(_n)
    v_mine = scratch("v_mine", [4 * T, 128], BF16)
    a_send = scratch("a_send", [4 * 128, T], BF16, True)
    a_all = scratch("a_all", [16 * 128, T], BF16, True)
    attnT_d = scratch("attnT", [512, T], BF16)
    xaT_d = scratch("xaT", [512, T], BF16)
    gsT_d = scratch("gsT", [512, T], BF16)
    e_own = scratch("e_own", [128, 32], F32, True)
    e_all = scratch("e_all", [4 * 128, 32], F32, True)
    f_scr = scratch("f_scr", [4 * 128, 32], F32)
    xh_own = scratch("xh_own", [D, 2], F32, True)
    xh_all = scratch("xh_all", [4 * D, 2], F32, True)
    xh_ext = scratch("xh_ext", [5 * D, 2], F32)
    x2_d = scratch("x2", [D, T], F32)
    h2T_d = scratch("h2T", [D, T + 2], BF16)
    actT_d = scratch("actT", [DFF, T], BF16)
    qdbg = scratch("qdbg", [192, 4 * T], BF16)
    scratch("dq", [768, T], BF16)
    scratch("dk", [768, T], BF16)
    scratch("dv", [4 * T, 128], BF16)

    def kview(ap, p=128):
        return ap.rearrange("(k p) t -> p k t", p=p)

    with ExitStack() as top:
        S = Sched(nc)
        S.alloc_sems(top)

        def rank_of(e):
            if S.rank_cache is None:
                S.rank_cache = e.partition_id() % 4
            return S.rank_cache

        ident_b = top.enter_context(nc.sbuf_tensor("ident_b", [128, 128], BF16))
        ones_b = top.enter_context(nc.sbuf_tensor("ones_b", [128, 128], BF16))
        Bc = Buf("consts")
        S.dma("pool", ident_b[:], ident_in, writes=[Bc])
        S.add("dve", lambda e: e.memset(ones_b[:], 1.0), writes=[Bc])
        eps_t = top.enter_context(nc.sbuf_tensor("eps_t", [128, 1], F32))
        S.add("dve", lambda e: e.memset(eps_t[:], EPS), writes=[Bc])

        class Stage:
            _ctr = [0]

            def __init__(self):
                self.st = ExitStack()

            def sb(self, shape, dt, name=None):
                Stage._ctr[0] += 1
                t = self.st.enter_context(nc.sbuf_tensor("t%d" % Stage._ctr[0], list(shape), dt))
                return t, Buf()

            def ps(self, shape=(128, 512), dt=F32):
                Stage._ctr[0] += 1
                t = self.st.enter_context(nc.psum_tensor("p%d" % Stage._ctr[0], list(shape), dt))
                return t, Buf()

            def done(self, barrier=True):
                S.flush(self.st, barrier=barrier)
                self.st.close()

        def load_vec(stg, l):
            vt, vb = stg.sb([128, NV], F32)
            S.dma("sp", vt[:], W["vec"][l], writes=[vb])
            return vt, vb

        def rstd_from(ps_ap, out_ap, n, reads, writes, tmp_ap, tmpb):
            np_ = out_ap.shape[0]
            S.add("act", lambda e: e.activation(out=tmp_ap, in_=ps_ap, func=AF.Ln, scale=1.0 / n, bias=eps_t[0:np_, 0:1]), reads=list(reads) + [Bc], writes=[tmpb])
            S.add("act", lambda e: e.activation(out=out_ap, in_=tmp_ap, func=AF.Exp, scale=-0.5), reads=[tmpb], writes=writes)

        def norm_T(stg, src, srcb, kch, ncols, gcol, vt, vb, dst, dstb, nblk=None, bw=512, psum=None):
            nblk = nblk if nblk is not None else ncols // bw
            sq, sqb = stg.sb([128, kch, bw], BF16)
            pss, pssb = psum if psum is not None else stg.ps()
            rs, rsb = stg.sb([128, bw], F32)
            tm, tmb = stg.sb([128, bw], F32)
            for tb in range(nblk):
                sl = slice(tb * bw, (tb + 1) * bw)
                for k in range(kch):
                    S.add("act", lambda e, k=k, sl=sl: e.activation(out=sq[:, k, :], in_=src[:, k, sl], func=AF.Square),
                          reads=[srcb(tb) if callable(srcb) else srcb], writes=[sqb])
                for k in range(kch):
                    S.add("pe", lambda e, k=k: e.matmul(pss[:, 0:bw], ones_b[:], sq[:, k, :], start=(k == 0), stop=(k == kch - 1)),
                          reads=[sqb, Bc], writes=[pssb])
                rstd_from(pss[:, 0:bw], rs[:], 128 * kch, [pssb], [rsb], tm[:], tmb)
                for k in range(kch):
                    S.add("dve", lambda e, k=k, sl=sl: e.scalar_tensor_tensor(
                        dst[:, k, sl], src[:, k, sl], vt[:, gcol + k:gcol + k + 1], rs[:], ALU.mult, ALU.mult),
                        reads=[srcb(tb) if callable(srcb) else srcb, rsb, vb], writes=[dstb(tb) if callable(dstb) else dstb])

        def linear_T(stg, inT, inb, kch, wview, col0, ncols, consumer, nblk=NB, bw=512, wq="pool", psums=None):
            wsp = 2 if kch >= 16 else 1
            wb = [(stg.sb([128, kch, 128], BF16)[0], [Buf() for _ in range(wsp)]) for _ in range(2)]
            kcut = [(i_ * kch) // wsp for i_ in range(wsp + 1)]
            pss = psums if psums is not None else [stg.ps() for _ in range(3)]
            npz = len(pss)
            noc = (ncols + 127) // 128
            it = 0
            for oc in range(noc):
                m = min(128, ncols - oc * 128)
                wt, wtbs = wb[oc % 2]
                for i_ in range(wsp):
                    S.dma(wq, wt[:, kcut[i_]:kcut[i_ + 1], 0:m], wview[:, kcut[i_]:kcut[i_ + 1], col0 + oc * 128: col0 + oc * 128 + m], writes=[wtbs[i_]])
                for tb in range(nblk):
                    ps, psb = pss[it % npz]
                    it += 1
                    for k in range(kch):
                        S.add("pe", lambda e, k=k, ps=ps, wt=wt, m=m, tb=tb: e.matmul(
                            ps[0:m, 0:bw], wt[:, k, 0:m], inT[:, k, tb * bw:(tb + 1) * bw], start=(k == 0), stop=(k == kch - 1)),
                            reads=[wtbs[min(wsp - 1, (k * wsp) // kch)], (inb(k, tb) if callable(inb) else inb)], writes=[psb])
                    consumer(oc, m, tb, ps, psb)

        stg = Stage()
        posi, posib = stg.sb([32, T], I32)
        posf, posfb = stg.sb([32, T], F32)
        invf, invfb = stg.sb([32, 1], F32)
        S.dma("sp", posi[:], pos_in.partition_broadcast(32), writes=[posib])
        S.dma("sp", invf[:], invf_in[0:32, :], writes=[invfb])
        S.add("dve", lambda e: e.tensor_copy(posf[:], posi[:]), reads=[posib], writes=[posfb])
        ang, angb = stg.sb([32, T], F32)
        qn_, qnb = stg.sb([32, T], F32)
        ni, nib = stg.sb([32, T], I32)
        nf, nfb = stg.sb([32, T], F32)
        rr, rrb = stg.sb([32, T], F32)
        mk, mkb = stg.sb([32, T], F32)
        sn, snb = stg.sb([32, T], F32)
        cs, csb = stg.sb([32, T], F32)
        S.add("dve", lambda e: e.tensor_scalar(ang[:], posf[:], invf[:, 0:1], None, ALU.mult), reads=[posfb, invfb], writes=[angb])
        S.add("dve", lambda e: e.tensor_scalar(qn_[:], ang[:], 1.0 / TWO_PI, None, ALU.mult), reads=[angb], writes=[qnb])
        S.add("dve", lambda e: e.tensor_copy(ni[:], qn_[:]), reads=[qnb], writes=[nib])
        S.add("dve", lambda e: e.tensor_copy(nf[:], ni[:]), reads=[nib], writes=[nfb])
        S.add("dve", lambda e: e.scalar_tensor_tensor(rr[:], nf[:], -CW1, ang[:], ALU.mult, ALU.add), reads=[nfb, angb], writes=[rrb])
        S.add("dve", lambda e: e.scalar_tensor_tensor(rr[:], nf[:], -CW2, rr[:], ALU.mult, ALU.add), reads=[nfb, rrb], writes=[rrb])
        S.add("dve", lambda e: e.tensor_scalar(rr[:], rr[:], math.pi, -math.pi, ALU.min, ALU.max), reads=[rrb], writes=[rrb])
        S.add("act", lambda e: e.activation(out=sn[:], in_=rr[:], func=AF.Sin), reads=[rrb], writes=[snb])
        S.add("dve", lambda e: e.tensor_scalar(qn_[:], rr[:], math.pi / 2, None, ALU.add), reads=[rrb], writes=[qnb])
        S.add("dve", lambda e: e.tensor_scalar(mk[:], qn_[:], math.pi, -TWO_PI, ALU.is_gt, ALU.mult), reads=[qnb], writes=[mkb])
        S.add("dve", lambda e: e.tensor_tensor(qn_[:], qn_[:], mk[:], ALU.add), reads=[qnb, mkb], writes=[qnb])
        S.add("dve", lambda e: e.tensor_scalar(qn_[:], qn_[:], math.pi, -math.pi, ALU.min, ALU.max), reads=[qnb], writes=[qnb])
        S.add("act", lambda e: e.activation(out=cs[:], in_=qn_[:], func=AF.Sin), reads=[qnb], writes=[csb])
        sgn, sgnb = stg.sb([32, 1], F32)
        S.dma("sp", sgn[:], invf_in[32:64, :], writes=[sgnb])
        S.add("dve", lambda e: e.tensor_scalar(sn[:], sn[:], sgn[:, 0:1], None, ALU.mult), reads=[snb, sgnb], writes=[snb])
        S.dma("sp", ropeC_d, cs[:], reads=[csb], writes=[scr_tok["ropeC"]])
        S.dma("sp", ropeS_d, sn[:], reads=[snb], writes=[scr_tok["ropeS"]])
        stg.done()

        try:
          for l in (range(2) if stop != "s0" else []):
              x_cur = xs[l % 2] if l > 0 else xT_in
              xb_cur = scr_tok["xs%d" % (l % 2)] if l > 0 else Buf("x_in")
              x_nxt = xs[(l + 1) % 2] if l == 0 else yT_out
              xb_nxt = scr_tok["xs%d" % ((l + 1) % 2)] if l == 0 else Buf("yout")
              win_v = kview(W["win"][l])
              stg = Stage()
              vt, vb = load_vec(stg, l)
              xt, xtb = stg.sb([128, 8, T], F32)
              hT, hTb = stg.sb([128, 8, T], BF16)
              xtbs = [Buf("xt%d" % i_) for i_ in range(NB)]
              hTbs = [Buf("hT%d" % i_) for i_ in range(NB)]
              for i_ in range(NB):
                  S.dma("sp", xt[:, :, i_ * 512:(i_ + 1) * 512], kview(x_cur)[:, :, i_ * 512:(i_ + 1) * 512], reads=[xb_cur], writes=[xtbs[i_]])
              norm_T(stg, xt, lambda tb_: xtbs[tb_], 8, T, VC["mixg"], vt, vb, hT, lambda tb_: hTbs[tb_])
              S.dma("sp", kview(hT_d), hT[:], reads=hTbs, writes=[scr_tok["hT"]])
              hTb = lambda k_, tb_: hTbs[tb_]
              stage_o = [stg.sb([128, 512], BF16) for _ in range(3)]
              stage_f = [stg.sb([128, 512], F32) for _ in range(2)]
              lps = [stg.ps() for _ in range(4)]
              cnt = [0]

              def cons_bf(dst_ap_fn, tokname):
                  def c(oc, m, tb, ps, psb):
                      so, sob = stage_o[cnt[0] % 3]
                      cnt[0] += 1
                      eng = "act" if cnt[0] % 2 else "dve"
                      if eng == "act":
                          S.add("act", lambda e: e.activation(out=so[0:m, :], in_=ps[0:m, :], func=AF.Copy), reads=[psb], writes=[sob])
                      else:
                          S.add("dve", lambda e: e.tensor_copy(so[0:m, :], ps[0:m, :]), reads=[psb], writes=[sob])
                      S.dma("sp", dst_ap_fn(oc, m, tb), so[0:m, :], reads=[sob], writes=[scr_tok[tokname]])
                  return c

              linear_T(stg, hT, hTb, 8, win_v, 0, 384, cons_bf(lambda oc, m, tb: cqT_d[oc * 128:oc * 128 + m, tb * 512:(tb + 1) * 512], "cqT"), psums=lps)
              linear_T(stg, hT, hTb, 8, win_v, 384, 256, cons_bf(lambda oc, m, tb: ckvT_d[oc * 128:oc * 128 + m, tb * 512:(tb + 1) * 512], "ckvT"), psums=lps)
              linear_T(stg, hT, hTb, 8, win_v, 672, 512, cons_bf(lambda oc, m, tb: uT_d[oc * 128:oc * 128 + m, tb * 512:(tb + 1) * 512], "uT"), psums=lps)
              linear_T(stg, hT, hTb, 8, win_v, 1184, 512, cons_bf(lambda oc, m, tb: xqT_d[oc * 128:oc * 128 + m, tb * 512:(tb + 1) * 512], "xqT"), psums=lps)

              def cons_kr(half):
                  def c(oc, m, tb, ps, psb):
                      so, sob = stage_f[cnt[0] % 2]
                      cnt[0] += 1
                      S.add("act", lambda e: e.activation(out=so[0:32, :], in_=ps[0:32, :], func=AF.Copy), reads=[psb], writes=[sob])
                      S.dma("sp", krT_d[half * 32:(half + 1) * 32, tb * 512:(tb + 1) * 512], so[0:32, :], reads=[sob], writes=[scr_tok["krT"]])
                  return c
              linear_T(stg, hT, hTb, 8, win_v, 640, 32, cons_kr(0), psums=lps)
              linear_T(stg, hT, hTb, 8, win_v, 4768, 32, cons_kr(1), psums=lps)
              stg.done()

              chk("A1%d" % l)
              stg = Stage()
              vt, vb = load_vec(stg, l)
              cq, cqb = stg.sb([128, 3, T], BF16)
              ckv, ckvb = stg.sb([128, 2, T], BF16)
              cqn, cqnb = stg.sb([128, 3, T], BF16)
              ckvn, ckvnb = stg.sb([128, 2, T], BF16)
              S.dma("sp", cq[:], kview(cqT_d), reads=[scr_tok["cqT"]], writes=[cqb])
              S.dma("sp", ckv[:], kview(ckvT_d), reads=[scr_tok["ckvT"]], writes=[ckvb])
              npsA2 = stg.ps()
              norm_T(stg, cq, cqb, 3, T, VC["qag"], vt, vb, cqn, cqnb, psum=npsA2)
              norm_T(stg, ckv, ckvb, 2, T, VC["kvag"], vt, vb, ckvn, ckvnb, psum=npsA2)
              rc, rcb = stg.sb([96, T], F32)
              rsn, rsnb = stg.sb([96, T], F32)
              kr, krb = stg.sb([96, T], F32)
              krs, krsb = stg.sb([96, T], F32)
              S.dma("sp", rc[64:96, :], ropeC_d, reads=[scr_tok["ropeC"]], writes=[rcb])
              S.dma("sp", rsn[64:96, :], ropeS_d, reads=[scr_tok["ropeS"]], writes=[rsnb])
              S.dma("sp", kr[64:96, :], krT_d[0:32, :], reads=[scr_tok["krT"]], writes=[krb])
              S.dma("sp", krs[64:96, :], krT_d[32:64, :], reads=[scr_tok["krT"]], writes=[krsb])
              wq, wqb_ = stg.sb([128, 3, 1024], BF16)
              wkk, wkkb = stg.sb([128, 2, 512], BF16)
              wkv, wkvb = stg.sb([128, 2, 512], BF16)
              S.dma("pool", wq[:], kview(W["wqb"][l]), writes=[wqb_])
              S.dma("pool", wkk[:], kview(W["wkvk"][l]), writes=[wkkb])
              S.dma("pool", wkv[:], kview(W["wkvv"][l]), writes=[wkvb])
              krr, krrb = stg.sb([96, T], F32)
              t1, t1b = stg.sb([96, T], F32)
              R = slice(64, 96)
              gk = VC["kg"]
              S.add("dve", lambda e: e.scalar_tensor_tensor(krr[R, :], kr[R, :], vt[R, gk:gk + 1], rc[R, :], ALU.mult, ALU.mult),
                    reads=[krb, rcb, vb], writes=[krrb])
              S.add("dve", lambda e: e.scalar_tensor_tensor(t1[R, :], krs[R, :], vt[R, gk + 1:gk + 2], rsn[R, :], ALU.mult, ALU.mult),
                    reads=[krsb, rsnb, vb], writes=[t1b])
              S.add("dve", lambda e: e.tensor_tensor(krr[R, :], krr[R, :], t1[R, :], ALU.add), reads=[krrb, t1b], writes=[krrb])
              krsq, krsqb = stg.sb([96, T], BF16)
              S.add("act", lambda e: e.activation(out=krsq[R, :], in_=kr[R, :], func=AF.Square), reads=[krb], writes=[krsqb])

              psq = [stg.ps() for _ in range(3)]
              psw = [stg.ps() for _ in range(2)]
              pss2 = [stg.ps() for _ in range(2)]
              sqs = [stg.sb([96, 512], BF16) for _ in range(3)]
              rss = [stg.sb([96, 512], F32) for _ in range(3)]
              tms = [stg.sb([96, 512], F32) for _ in range(2)]
              qf = [stg.sb([96, 512], F32) for _ in range(2)]
              qsf = [stg.sb([96, 512], F32) for _ in range(2)]
              ob = [stg.sb([96, 512], BF16) for _ in range(3)]
              gq = VC["qg"]
              GCq, GCqb = stg.sb([96, T], F32)
              GSq, GSqb = stg.sb([96, T], F32)
              S.add("dve", lambda e: e.tensor_scalar(GCq[R, :], rc[R, :], vt[R, gq:gq + 1], None, ALU.mult), reads=[rcb, vb], writes=[GCqb])
              S.add("dve", lambda e: e.tensor_scalar(GSq[R, :], rsn[R, :], vt[R, gq + 1:gq + 2], None, ALU.mult), reads=[rsnb, vb], writes=[GSqb])
              items = []
              for h in range(8):
                  for tb in range(NB):
                      items.append(("q", h, tb))
                      items.append(("k", h, tb))

              def a2_stage1(n):
                  kind, h, tb = items[n]
                  sl = slice(tb * 512, (tb + 1) * 512)
                  pq, pqb = psq[n % 3]
                  sq, sqb = sqs[n % 3]
                  if kind == "q":
                      pw, pwb = psw[(n // 2) % 2]
                      for k in range(3):
                          S.add("pe", lambda e, k=k: e.matmul(pq[0:96, :], wq[:, k, h * 96:(h + 1) * 96], cqn[:, k, sl],
                                                              start=(k == 0), stop=(k == 2)), reads=[wqb_, cqnb], writes=[pqb])
                      for k in range(3):
                          S.add("pe", lambda e, k=k: e.matmul(pw[64:96, :], wq[:, k, 768 + h * 32:768 + (h + 1) * 32], cqn[:, k, sl],
                                                              start=(k == 0), stop=(k == 2), tile_position=(0, 64)), reads=[wqb_, cqnb], writes=[pwb])
                      S.add("act", lambda e: e.activation(out=sq[0:96, :], in_=pq[0:96, :], func=AF.Square), reads=[pqb], writes=[sqb])
                  else:
                      for k in range(2):
                          S.add("pe", lambda e, k=k: e.matmul(pq[0:64, :], wkk[:, k, h * 64:(h + 1) * 64], ckvn[:, k, sl],
                                                              start=(k == 0), stop=(k == 1)), reads=[wkkb, ckvnb], writes=[pqb])
                      S.add("act", lambda e: e.activation(out=sq[0:64, :], in_=pq[0:64, :], func=AF.Square), reads=[pqb], writes=[sqb])

              def a2_stage2(n):
                  kind, h, tb = items[n]
                  sl = slice(tb * 512, (tb + 1) * 512)
                  sq, sqb = sqs[n % 3]
                  p2, p2b = pss2[n % 2]
                  rs, rsb = rss[n % 3]
                  tm, tmb = tms[n % 2]
                  if kind == "q":
                      S.add("pe", lambda e: e.matmul(p2[0:96, :], ones_b[0:96, 0:96], sq[0:96, :], start=True, stop=True),
                            reads=[sqb, Bc], writes=[p2b])
                  else:
                      S.add("pe", lambda e: e.matmul(p2[0:96, :], ones_b[0:64, 0:96], sq[0:64, :], start=True, stop=False, tile_position=(0, 0)),
                            reads=[sqb, Bc], writes=[p2b])
                      S.add("pe", lambda e: e.matmul(p2[0:96, :], ones_b[64:96, 0:96], krsq[64:96, sl], start=False, stop=True, tile_position=(64, 0)),
                            reads=[krsqb, Bc], writes=[p2b])
                  rstd_from(p2[0:96, :], rs[0:96, :], 96, [p2b], [rsb], tm[0:96, :], tmb)

              def a2_stage3(n):
                  kind, h, tb = items[n]
                  sl = slice(tb * 512, (tb + 1) * 512)
                  pq, pqb = psq[n % 3]
                  rs, rsb = rss[n % 3]
                  o, obf = ob[n % 3]
                  if kind == "q":
                      pw, pwb = psw[(n // 2) % 2]
                      q_f, q_fb = qf[(n // 2) % 2]
                      qs_f, qs_fb = qsf[(n // 2) % 2]
                      S.add("dve", lambda e: e.tensor_tensor(q_f[R, :], pq[R, :], GCq[R, sl], ALU.mult), reads=[pqb, GCqb], writes=[q_fb])
                      S.add("dve", lambda e: e.tensor_tensor(qs_f[R, :], pw[R, :], GSq[R, sl], ALU.mult), reads=[pwb, GSqb], writes=[qs_fb])
                      S.add("dve", lambda e: e.scalar_tensor_tensor(o[0:64, :], pq[0:64, :], vt[0:64, gq:gq + 1], rs[0:64, :], ALU.mult, ALU.mult),
                            reads=[pqb, rsb, vb], writes=[obf])
                      S.add("dve", lambda e: e.tensor_tensor(q_f[R, :], q_f[R, :], qs_f[R, :], ALU.add), reads=[q_fb, qs_fb], writes=[q_fb])
                      S.add("dve", lambda e: e.tensor_tensor(o[R, :], q_f[R, :], rs[R, :], ALU.mult), reads=[q_fb, rsb], writes=[obf])
                      S.dma("sp", q_own[h * 96:(h + 1) * 96, sl], o[0:96, :], reads=[obf], writes=[scr_tok["q_own"]])
                  else:
                      S.add("dve", lambda e: e.scalar_tensor_tensor(o[0:64, :], pq[0:64, :], vt[0:64, gk:gk + 1], rs[0:64, :], ALU.mult, ALU.mult),
                            reads=[pqb, rsb, vb], writes=[obf])
                      S.add("dve", lambda e: e.tensor_tensor(o[R, :], krr[R, sl], rs[R, :], ALU.mult), reads=[krrb, rsb], writes=[obf])
                      S.dma("sp", k_own[h * 96:(h + 1) * 96, sl], o[0:96, :], reads=[obf], writes=[scr_tok["k_own"]])

              NI = len(items)
              for n in range(NI + 2):
                  if n < NI:
                      a2_stage1(n)
                  if 0 <= n - 1 < NI:
                      a2_stage2(n - 1)
                  if n - 2 >= 0:
                      a2_stage3(n - 2)
              pv = psq[0:2]
              vo = [stg.sb([128, 512], BF16) for _ in range(2)]
              for tt in range(16):
                  p, pb = pv[tt % 2]
                  o, obf = vo[tt % 2]
                  for k in range(2):
                      S.add("pe", lambda e, k=k, p=p, tt=tt: e.matmul(p[:, :], ckvn[:, k, tt * 128:(tt + 1) * 128], wkv[:, k, :], start=(k == 0), stop=(k == 1)),
                            reads=[ckvnb, wkvb], writes=[pb])
                  S.add("act", lambda e, o=o, p=p: e.activation(out=o[:], in_=p[:], func=AF.Copy), reads=[pb], writes=[obf])
                  S.dma("sp", v_own.rearrange("(j t) c -> t j c", j=4)[tt * 128:(tt + 1) * 128, :, :], o[:].rearrange("p (j c) -> p j c", j=4),
                        reads=[obf], writes=[scr_tok["v_own"]])
              stg.done()
              if "dq" in dbg and l == 0:
                  for (sname, dname) in (("q_own", "dq"), ("k_own", "dk"), ("v_own", "dv")):
                      S.dma("sp", scr[dname], scr[sname], reads=[scr_tok[sname]], writes=[scr_tok[dname]])
                  with ExitStack() as tmpst:
                      S.flush(tmpst, barrier=True)
              chk("A2%d" % l)
              stg = Stage()
              vt, vb = load_vec(stg, l)
              mt_, mtb = stg.sb([128, 8, 256], F32)
              mn, mnb = stg.sb([128, 8, 256], BF16)
              S.dma("sp", mt_[:], kview(memT_in), writes=[mtb])
              pssx, pssxb = stg.ps()
              norm_T(stg, mt_, mtb, 8, 256, VC["memg"], vt, vb, mn, mnb, nblk=1, bw=256, psum=(pssx, pssxb))
              kxT, kxTb = stg.sb([128, 4, 256], BF16)
              vx, vxb = stg.sb([128, 2, 512], BF16)
              wmv = kview(W["wmemkv"][l])
              sqx, sqxb = stg.sb([128, 512], BF16)
              rsx, rsxb = stg.sb([128, 512], F32)
              tmx, tmxb = stg.sb([128, 512], F32)
              xkg = VC["xkg"]

              def cons_kx(oc, m, tb, ps, psb):
                  S.add("act", lambda e: e.activation(out=sqx[:, 0:256], in_=ps[:, 0:256], func=AF.Square), reads=[psb], writes=[sqxb])
                  S.add("pe", lambda e: e.matmul(pssx[:, 0:256], ones_b[:], sqx[:, 0:256], start=True, stop=True), reads=[sqxb, Bc], writes=[pssxb])
                  rstd_from(pssx[:, 0:256], rsx[:, 0:256], 128, [pssxb], [rsxb], tmx[:, 0:256], tmxb)
                  S.add("dve", lambda e: e.scalar_tensor_tensor(kxT[:, oc, :], ps[:, 0:256], vt[:, xkg:xkg + 1], rsx[:, 0:256], ALU.mult, ALU.mult),
                        reads=[psb, rsxb, vb], writes=[kxTb])
              pvx = [stg.ps() for _ in range(2)]
              linear_T(stg, mn, mnb, 8, wmv, 0, 512, cons_kx, nblk=1, bw=256, psums=pvx)
              wv_, wv_b = stg.sb([128, 8, 512], BF16)
              S.dma("pool", wv_[:], wmv[:, :, 512:1024], writes=[wv_b])
              for mt in range(2):
                  p, pb = pvx[mt]
                  for k in range(8):
                      S.add("pe", lambda e, k=k, p=p, mt=mt: e.matmul(p[:, :], mn[:, k, mt * 128:(mt + 1) * 128], wv_[:, k, :], start=(k == 0), stop=(k == 7)),
                            reads=[mnb, wv_b], writes=[pb])
                  S.add("act", lambda e, p=p, mt=mt: e.activation(out=vx[:, mt, :], in_=p[:], func=AF.Copy), reads=[pb], writes=[vxb])
              xq, xqb = stg.sb([128, 4, T], BF16)
              S.dma("sp", xq[:], kview(xqT_d), reads=[scr_tok["xqT"]], writes=[xqb])
              sqx2 = [(sqx, sqxb), stg.sb([128, 512], BF16)]
              rsx2 = [(rsx, rsxb), stg.sb([128, 512], F32)]
              tmx2 = [(tmx, tmxb), stg.sb([128, 512], F32)]
              pssx2 = [(pssx, pssxb), stg.ps()]
              rd2 = [stg.sb([128, 512], F32) for _ in range(2)]
              qxn = [stg.sb([128, 512], BF16) for _ in range(2)]
              pT = [stg.sb([128, 512], BF16) for _ in range(3)]
              pS = [stg.ps() for _ in range(2)]
              pO = [stg.ps() for _ in range(1)]
              pD = [stg.ps() for _ in range(1)]
              xo = [stg.sb([128, 512], BF16) for _ in range(2)]
              xqg = VC["xqg"]
              it = 0
              ip = 0
              qxn = qxn + [stg.sb([128, 512], BF16)]
              pT = pT + [stg.sb([128, 512], BF16)]
              xitems = [(hx, tb) for hx in range(4) for tb in range(NB)]
              pO2 = [pO[0], pvx[0]]
              pD2 = [pD[0], pvx[1]]
              rt2 = [stg.sb([128, 512], F32) for _ in range(2)]

              def x_s1(n):
                  hx, tb = xitems[n]
                  sl = slice(tb * 512, (tb + 1) * 512)
                  qx, qxb = qxn[n % 3]
                  sqx_, sqx_b = sqx2[n % 2]
                  rsx_, rsx_b = rsx2[n % 2]
                  tmx_, tmx_b = tmx2[n % 2]
                  pssx_, pssx_b = pssx2[n % 2]
                  S.add("act", lambda e: e.activation(out=sqx_[:], in_=xq[:, hx, sl], func=AF.Square), reads=[xqb], writes=[sqx_b])
                  S.add("pe", lambda e: e.matmul(pssx_[:], ones_b[:], sqx_[:], start=True, stop=True), reads=[sqx_b, Bc], writes=[pssx_b])
                  rstd_from(pssx_[:], rsx_[:], 128, [pssx_b], [rsx_b], tmx_[:], tmx_b)
                  S.add("dve", lambda e: e.scalar_tensor_tensor(qx[:], xq[:, hx, sl], vt[:, xqg:xqg + 1], rsx_[:], ALU.mult, ALU.mult),
                        reads=[xqb, rsx_b, vb], writes=[qxb])

              def x_s2(n):
                  hx, tb = xitems[n]
                  qx, qxb = qxn[n % 3]
                  for mt in range(2):
                      ps_, psb_ = pS[mt]
                      pt, ptb = pT[(2 * n + mt) % 4]
                      S.add("pe", lambda e, ps_=ps_, mt=mt: e.matmul(ps_[:], kxT[:, hx, mt * 128:(mt + 1) * 128], qx[:], start=True, stop=True),
                            reads=[kxTb, qxb], writes=[psb_])
                      S.add("act", lambda e, pt=pt, ps_=ps_: e.activation(out=pt[:], in_=ps_[:], func=AF.Exp, scale=128 ** -0.5, bias=-6.0),
                            reads=[psb_], writes=[ptb])

              def x_s3(n):
                  hx, tb = xitems[n]
                  sl = slice(tb * 512, (tb + 1) * 512)
                  po, pob = pO2[n % 2]
                  pd, pdb = pD2[n % 2]
                  o, obf = xo[n % 2]
                  rd, rdb = rd2[n % 2]
                  rt, rtb = rt2[n % 2]
                  for mt in range(2):
                      pt, ptb = pT[(2 * n + mt) % 4]
                      S.add("pe", lambda e, mt=mt, pt=pt: e.matmul(po[:], vx[:, mt, hx * 128:(hx + 1) * 128], pt[:], start=(mt == 0), stop=(mt == 1)),
                            reads=[vxb, ptb], writes=[pob])
                  for mt in range(2):
                      pt, ptb = pT[(2 * n + mt) % 4]
                      S.add("pe", lambda e, mt=mt, pt=pt: e.matmul(pd[:], ones_b[:], pt[:], start=(mt == 0), stop=(mt == 1)),
                            reads=[Bc, ptb], writes=[pdb])
                  S.add("act", lambda e: e.activation(out=rt[:], in_=pd[:], func=AF.Ln), reads=[pdb], writes=[rtb])
                  S.add("act", lambda e: e.activation(out=rd[:], in_=rt[:], func=AF.Exp, scale=-1.0), reads=[rtb], writes=[rdb])
                  S.add("dve", lambda e: e.tensor_tensor(o[:], po[:], rd[:], ALU.mult), reads=[pob, rdb], writes=[obf])
                  S.dma("sp", xaT_d[hx * 128:(hx + 1) * 128, sl], o[:], reads=[obf], writes=[scr_tok["xaT"]])

              NXI = len(xitems)
              for n in range(NXI + 2):
                  if n < NXI:
                      x_s1(n)
                  if 0 <= n - 1 < NXI:
                      x_s2(n - 1)
                  if n - 2 >= 0:
                      x_s3(n - 2)
              for a, b, rows in ((("q_own", "q_all", 192), ("k_own", "k_all", 192), ("v_own", "v_all", T)) if "nocc" not in dbg else ()):
                  for j in range(4):
                      S.cc(lambda e, a=a, b=b, rows=rows, j=j: e.collective_compute(
                          "AllGather", ALU.bypass, replica_groups=RG, ins=[scr[a][j * rows:(j + 1) * rows, :].opt()],
                          outs=[scr[b][j * 4 * rows:(j + 1) * 4 * rows, :].opt()]), reads=[scr_tok[a]], writes=[scr_tok[b]], unit="cc" + a[0])

              stg.done()

              chk("A3%d" % l)
              stg = Stage()
              vt, vb = load_vec(stg, l)
              P16 = [128, 16]

              def v16(name):
                  return vt[:, VC[name]:VC[name] + 16]
              sm = {}
              smb = Buf("ssm_small")

              def small(name, shape=P16, dt=F32):
                  t, _ = stg.sb(shape, dt)
                  sm[name] = t
                  return t

              def sop(fn, eng="dve"):
                  S.add(eng, fn, reads=[smb, vb], writes=[smb])

              for nmz in ["dt", "lrdt", "th", "mag", "qn", "nf", "r", "rc_", "mk", "sin", "cos", "are", "aim", "ere", "den", "t1", "t2", "fre", "fim"]:
                  small(nmz)
              small("ni", P16, I32)
              sop(lambda e: e.activation(out=sm["dt"][:], in_=v16("ldt"), func=AF.Exp), "act")
              sop(lambda e: e.tensor_tensor(sm["lrdt"][:], v16("lre"), sm["dt"][:], ALU.mult))
              sop(lambda e: e.tensor_tensor(sm["th"][:], v16("lim"), sm["dt"][:], ALU.mult))
              sop(lambda e: e.activation(out=sm["mag"][:], in_=sm["lrdt"][:], func=AF.Exp), "act")
              sop(lambda e: e.tensor_scalar(sm["qn"][:], sm["th"][:], 1.0 / TWO_PI, None, ALU.mult))
              sop(lambda e: e.tensor_copy(sm["ni"][:], sm["qn"][:]))
              sop(lambda e: e.tensor_copy(sm["nf"][:], sm["ni"][:]))
              sop(lambda e: e.scalar_tensor_tensor(sm["r"][:], sm["nf"][:], -CW1, sm["th"][:], ALU.mult, ALU.add))
              sop(lambda e: e.scalar_tensor_tensor(sm["r"][:], sm["nf"][:], -CW2, sm["r"][:], ALU.mult, ALU.add))
              sop(lambda e: e.tensor_scalar(sm["r"][:], sm["r"][:], math.pi, -math.pi, ALU.min, ALU.max))
              sop(lambda e: e.activation(out=sm["sin"][:], in_=sm["r"][:], func=AF.Sin), "act")
              sop(lambda e: e.tensor_scalar(sm["rc_"][:], sm["r"][:], math.pi / 2, None, ALU.add))
              sop(lambda e: e.tensor_scalar(sm["mk"][:], sm["rc_"][:], math.pi, -TWO_PI, ALU.is_gt, ALU.mult))
              sop(lambda e: e.tensor_tensor(sm["rc_"][:], sm["rc_"][:], sm["mk"][:], ALU.add))
              sop(lambda e: e.tensor_scalar(sm["rc_"][:], sm["rc_"][:], math.pi, -math.pi, ALU.min, ALU.max))
              sop(lambda e: e.activation(out=sm["cos"][:], in_=sm["rc_"][:], func=AF.Sin), "act")
              sop(lambda e: e.tensor_tensor(sm["are"][:], sm["mag"][:], sm["cos"][:], ALU.mult))
              sop(lambda e: e.tensor_tensor(sm["aim"][:], sm["mag"][:], sm["sin"][:], ALU.mult))
              sop(lambda e: e.tensor_scalar(sm["ere"][:], sm["are"][:], -1.0, None, ALU.add))
              sop(lambda e: e.tensor_tensor(sm["den"][:], v16("lre"), v16("lre"), ALU.mult))
              sop(lambda e: e.tensor_tensor(sm["t1"][:], v16("lim"), v16("lim"), ALU.mult))
              sop(lambda e: e.tensor_tensor(sm["den"][:], sm["den"][:], sm["t1"][:], ALU.add))
              sop(lambda e: e.reciprocal(sm["den"][:], sm["den"][:]))
              sop(lambda e: e.tensor_tensor(sm["t1"][:], sm["ere"][:], v16("lre"), ALU.mult))
              sop(lambda e: e.tensor_tensor(sm["t2"][:], sm["aim"][:], v16("lim"), ALU.mult))
              sop(lambda e: e.tensor_tensor(sm["t1"][:], sm["t1"][:], sm["t2"][:], ALU.add))
              sop(lambda e: e.tensor_tensor(sm["fre"][:], sm["t1"][:], sm["den"][:], ALU.mult))
              sop(lambda e: e.tensor_tensor(sm["t1"][:], sm["aim"][:], v16("lre"), ALU.mult))
              sop(lambda e: e.tensor_tensor(sm["t2"][:], sm["ere"][:], v16("lim"), ALU.mult))
              sop(lambda e: e.tensor_tensor(sm["t1"][:], sm["t1"][:], sm["t2"][:], ALU.subtract))
              sop(lambda e: e.tensor_tensor(sm["fim"][:], sm["t1"][:], sm["den"][:], ALU.mult))
              RS = 8
              NK = T // RS
              TL = NK.bit_length() - 1
              apw = small("apw", [128, RS, 2, 16])
              tA = small("tA")
              tB = small("tB")

              def cmul(ore, oim, xre, xim, yre, yim):
                  sop(lambda e: e.tensor_tensor(tA[:], xre, yre, ALU.mult))
                  sop(lambda e: e.tensor_tensor(tB[:], xim, yim, ALU.mult))
                  sop(lambda e: e.tensor_tensor(ore, tA[:], tB[:], ALU.subtract))
                  sop(lambda e: e.tensor_tensor(tA[:], xre, yim, ALU.mult))
                  sop(lambda e: e.tensor_tensor(tB[:], xim, yre, ALU.mult))
                  sop(lambda e: e.tensor_tensor(oim, tA[:], tB[:], ALU.add))
              sop(lambda e: e.tensor_copy(apw[:, 0, 0, :], sm["are"][:]))
              sop(lambda e: e.tensor_copy(apw[:, 0, 1, :], sm["aim"][:]))
              for j in range(1, RS):
                  cmul(apw[:, j, 0, :], apw[:, j, 1, :], apw[:, j - 1, 0, :], apw[:, j - 1, 1, :], sm["are"][:], sm["aim"][:])
              NLV = TL + 1
              ald = small("ald", [128, NLV, 3, 16])
              sop(lambda e: e.tensor_copy(ald[:, 0, 0, :], apw[:, RS - 1, 0, :]))
              sop(lambda e: e.tensor_copy(ald[:, 0, 1, :], apw[:, RS - 1, 1, :]))
              for d in range(1, NLV):
                  cmul(ald[:, d, 0, :], ald[:, d, 1, :], ald[:, d - 1, 0, :], ald[:, d - 1, 1, :], ald[:, d - 1, 0, :], ald[:, d - 1, 1, :])
              for d in range(NLV):
                  sop(lambda e, d=d: e.tensor_scalar(ald[:, d, 2, :], ald[:, d, 1, :], -1.0, None, ALU.mult))
              bblk, bblkb = stg.sb([128, 2, 16, 32], F32)
              cblk, cblkb = stg.sb([128, 2, 16, 32], F32)
              S.dma("sp", bblk[:], W["bblk"][l].rearrange("p (a b c) -> p a b c", a=2, b=16), writes=[smb])
              S.dma("sp", cblk[:], W["cblk"][l].rearrange("p (a b c) -> p a b c", a=2, b=16), writes=[smb])
              Lf = small("Lf", [128, 2, 2, 16, 32])
              Lb = small("Lb", [128, RS, 2, 16, 32], BF16)
              tL1 = small("tL1", [128, 16, 32])
              tL2 = small("tL2", [128, 16, 32])

              def bc(ap16):
                  return ap16.unsqueeze(2).broadcast_to([128, 16, 32])

              def cmul_b(ore, oim, xre, xim, sre, sim):
                  sop(lambda e: e.tensor_tensor(tL1[:], xre, bc(sre), ALU.mult))
                  sop(lambda e: e.tensor_tensor(tL2[:], xim, bc(sim), ALU.mult))
                  sop(lambda e: e.tensor_tensor(ore, tL1[:], tL2[:], ALU.subtract))
                  sop(lambda e: e.tensor_tensor(tL1[:], xre, bc(sim), ALU.mult))
                  sop(lambda e: e.tensor_tensor(tL2[:], xim, bc(sre), ALU.mult))
                  sop(lambda e: e.tensor_tensor(oim, tL1[:], tL2[:], ALU.add))
              cmul_b(Lf[:, 0, 0], Lf[:, 0, 1], bblk[:, 0], bblk[:, 1], sm["fre"][:], sm["fim"][:])
              sop(lambda e: e.tensor_copy(Lb[:, 0], Lf[:, 0]))
              for tau in range(1, RS):
                  cmul_b(Lf[:, tau % 2, 0], Lf[:, tau % 2, 1], Lf[:, (tau - 1) % 2, 0], Lf[:, (tau - 1) % 2, 1], sm["are"][:], sm["aim"][:])
                  sop(lambda e, tau=tau: e.tensor_copy(Lb[:, tau], Lf[:, tau % 2]))
              Cb = small("Cb", [128, 2, 16, 32], BF16)
              sop(lambda e: e.tensor_copy(Cb[:, 0], cblk[:, 0]))
              sop(lambda e: e.tensor_scalar(Cb[:, 1], cblk[:, 1], -1.0, None, ALU.mult))
              Kt = small("Kt", [128, 4, RS, 128], BF16)
              LT = small("LT", [128, 4, RS, 2, 128], BF16)
              tokK = Buf("tokK")
              tokLT = Buf("tokLT")
              S.add("dve", lambda e: e.memset(Kt[:], 0.0), writes=[tokK])
              pK, pKb = stg.ps([128, 4, 128])
              pKbm = [Buf("pK%d" % m_) for m_ in range(4)]
              pL = [stg.ps([128, 4, 128]) for _ in range(2)]
              pX = stg.ps([128, 4, 128])
              for q in range(4):
                for th in range(RS // 4):
                  for m in range(4):
                      i = 4 * q + m
                      ms = slice(m * 32, (m + 1) * 32)
                      for t4 in range(4):
                          tau = th * 4 + t4
                          S.add("pe", lambda e, tau=tau, t4=t4, i=i, ms=ms: e.matmul(pK[ms, t4, ms], Lb[:, tau, 0, i, :], Cb[:, 0, i, :], start=True, stop=False, tile_position=(0, ms.start)),
                                reads=[smb], writes=[pKbm[m]])
                          S.add("pe", lambda e, tau=tau, t4=t4, i=i, ms=ms: e.matmul(pK[ms, t4, ms], Lb[:, tau, 1, i, :], Cb[:, 1, i, :], start=False, stop=True, tile_position=(0, ms.start)),
                                reads=[smb], writes=[pKbm[m]])
                          for ri in range(2):
                              S.add("pe", lambda e, tau=tau, t4=t4, i=i, ms=ms, ri=ri: e.matmul(pL[ri][0][ms, t4, :], Lb[:, tau, ri, i, :], ident_b[:], start=True, stop=True, tile_position=(0, ms.start)),
                                    reads=[smb, Bc], writes=[pL[ri][1]])
                      S.add("act", lambda e, q=q, ms=ms, th=th: e.activation(out=Kt[ms, q, th * 4:(th + 1) * 4, ms], in_=pK[ms, :, ms], func=AF.Copy), reads=[pKbm[m]], writes=[tokK])
                  for ri in range(2):
                      S.add("act", lambda e, q=q, ri=ri, th=th: e.activation(out=LT[:, q, th * 4:(th + 1) * 4, ri, :], in_=pL[ri][0][:], func=AF.Copy), reads=[pL[ri][1]], writes=[tokLT])
              Gf = small("Gf", [128, 2, 16, 32])
              Gb = small("Gb", [128, RS, 2, 16, 32], BF16)
              tG1 = small("tG1", [128, 16, 32])
              tG2 = small("tG2", [128, 16, 32])
              tokG = Buf("tokG")

              def gop(fn):
                  S.add("dve", fn, reads=[smb, tokG], writes=[tokG])
              for j in range(RS):
                  sre, sim = apw[:, j, 0, :], apw[:, j, 1, :]
                  gop(lambda e, sre=sre: e.tensor_tensor(tG1[:], cblk[:, 0], bc(sre), ALU.mult))
                  gop(lambda e, sim=sim: e.tensor_tensor(tG2[:], cblk[:, 1], bc(sim), ALU.mult))
                  gop(lambda e: e.tensor_tensor(Gf[:, 0], tG1[:], tG2[:], ALU.subtract))
                  gop(lambda e, sim=sim: e.tensor_tensor(tG1[:], cblk[:, 0], bc(sim), ALU.mult))
                  gop(lambda e, sre=sre: e.tensor_tensor(tG2[:], cblk[:, 1], bc(sre), ALU.mult))
                  gop(lambda e: e.tensor_tensor(Gf[:, 1], tG1[:], tG2[:], ALU.add))
                  gop(lambda e, j=j: e.tensor_copy(Gb[:, j, 0], Gf[:, 0]))
                  gop(lambda e, j=j: e.tensor_scalar(Gb[:, j, 1], Gf[:, 1], -1.0, None, ALU.mult))

              dcol = VC["ssmd"]
              for q in range(4):
                  S.add("dve", lambda e, q=q: e.scalar_tensor_tensor(Kt[:, q, 0, :], ident_b[:], vt[:, dcol + q:dcol + q + 1], Kt[:, q, 0, :], ALU.mult, ALU.add),
                        reads=[tokK, vb, Bc], writes=[tokK])

              uT, uTb = stg.sb([128, 4, T], BF16)
              S.dma("sp", uT[:], kview(uT_d), reads=[scr_tok["uT"]], writes=[uTb])
              def fqk(e):
                  r = rank_of(e)
                  return e.dma_start(out=qk_mine.rearrange("(s j) t -> s j t", s=2),
                                     in_=qk_all.rearrange("(s j) t -> s j t", s=2)[:, bass.ds(r * 768, 768), :])
              S.dma_fn("sp", fqk, reads=[scr_tok["q_all"], scr_tok["k_all"]], writes=[scr_tok["q_mine"], scr_tok["k_mine"]])

              def fv(e):
                  r = rank_of(e)
                  return e.dma_start(out=v_mine, in_=v_all[bass.ds(r * 4 * T, 4 * T), :])
              S.dma_fn("pool", fv, reads=[scr_tok["v_all"]], writes=[scr_tok["v_mine"]])
              Vst = [stg.sb([128, 2, NK + 8], F32) for _ in range(2)]
              Eo, Eob = stg.sb([128, 2, 16], F32)
              pV = pL + [pX]
              Ssb, Ssbb = stg.sb([128, 16, 2, NK + 8], BF16)
              ra = [stg.sb([128, 2, 256], F32) for _ in range(2)]

              def compute_V(i, dst, dstb):
                  q, m = i // 4, i % 4
                  ms = slice(m * 32, (m + 1) * 32)
                  for ri in range(2):
                      p, pb = pV[(i * 2 + ri) % 3]
                      for j in range(RS):
                          S.add("pe", lambda e, p=p, j=j, ri=ri, q=q, ms=ms, m=m: e.matmul(
                              p[:].rearrange("p a b -> p (a b)")[:, 0:NK], LT[ms, q, RS - 1 - j, ri, :], uT[ms, q, j:T:RS],
                              start=(j == 0), stop=(j == RS - 1), tile_position=(m * 32, 0)), reads=[tokLT, uTb], writes=[pb])
                      S.add("act", lambda e, p=p, ri=ri, dst=dst: e.activation(out=dst[:, ri, 1:1 + NK], in_=p[:].rearrange("p a b -> p (a b)")[:, 0:NK], func=AF.Copy), reads=[pb], writes=[dstb])

              def interleave(lists):
                  n = max(len(x) for x in lists)
                  for k in range(n):
                      for x in lists:
                          if k < len(x):
                              eng, fn, rd_, wr_ = x[k]
                              S.add(eng, fn, reads=rd_, writes=wr_)

              def tree_ops(i, vs, vsb, ra_):
                  ops = []
                  src, srcb = vs, vsb
                  n = NK
                  off = 1
                  for d in range(TL):
                      n2 = n // 2
                      dstt, dsttb = ra_[d % 2]
                      ar = ald[:, d, 0, i:i + 1]
                      ai = ald[:, d, 1, i:i + 1]
                      nai = ald[:, d, 2, i:i + 1]
                      ev = lambda ri, src=src, off=off, n=n: src[:, ri, off:off + n:2]
                      od = lambda ri, src=src, off=off, n=n: src[:, ri, off + 1:off + n:2]
                      out_re = dstt[:, 0, 0:n2] if d < TL - 1 else Eo[:, 0, i:i + 1]
                      out_im = dstt[:, 1, 0:n2] if d < TL - 1 else Eo[:, 1, i:i + 1]
                      wtok = dsttb if d < TL - 1 else Eob
                      ops.append(("dve", lambda e, o=out_re, a=ev(0), b=od(0), ar=ar: e.scalar_tensor_tensor(o, a, ar, b, ALU.mult, ALU.add), [srcb, smb], [wtok]))
                      ops.append(("dve", lambda e, o=out_im, a=ev(0), b=od(1), ai=ai: e.scalar_tensor_tensor(o, a, ai, b, ALU.mult, ALU.add), [srcb, smb, wtok], [wtok]))
                      ops.append(("dve", lambda e, o=out_re, a=ev(1), nai=nai: e.scalar_tensor_tensor(o, a, nai, o, ALU.mult, ALU.add), [srcb, smb, wtok], [wtok]))
                      ops.append(("dve", lambda e, o=out_im, a=ev(1), ar=ar: e.scalar_tensor_tensor(o, a, ar, o, ALU.mult, ALU.add), [srcb, smb, wtok], [wtok]))
                      src, srcb = dstt, dsttb
                      n = n2
                      off = 0
                  return ops

              ra4 = [ra, [stg.sb([128, 2, 256], F32) for _ in range(2)]]
              for ip_ in range(8):
                  lists = []
                  for t_ in range(2):
                      i = 2 * ip_ + t_
                      vs, vsb = Vst[t_]
                      compute_V(i, vs, vsb)
                      lists.append(tree_ops(i, vs, vsb, ra4[t_]))
                  interleave(lists)
              S.dma("sp", e_own, Eo[:].rearrange("p a b -> p (a b)"), reads=[Eob], writes=[scr_tok["e_own"]])
              S.cc(lambda e: e.collective_compute("AllGather", ALU.bypass, replica_groups=RG, ins=[e_own.opt()], outs=[e_all.opt()]),
                   reads=[scr_tok["e_own"]], writes=[scr_tok["e_all"]])
              Ea, Eab = stg.sb([128, 4, 2, 16], F32)
              Fa, Fab = stg.sb([128, 4, 2, 16], F32)
              S.dma("sp", Ea[:], e_all.rearrange("(r p) (a b) -> p r a b", p=128, a=2), reads=[scr_tok["e_all"]], writes=[Eab])
              S.add("dve", lambda e: e.memset(Fa[:], 0.0), writes=[Fab])
              A9r, A9i = ald[:, NLV - 1, 0, :], ald[:, NLV - 1, 1, :]
              for rnk in range(1, 4):
                  S.add("dve", lambda e, rnk=rnk: e.tensor_tensor(tA[:], Fa[:, rnk - 1, 0, :], A9r, ALU.mult), reads=[Fab, smb], writes=[smb])
                  S.add("dve", lambda e, rnk=rnk: e.tensor_tensor(tB[:], Fa[:, rnk - 1, 1, :], A9i, ALU.mult), reads=[Fab, smb], writes=[smb])
                  S.add("dve", lambda e: e.tensor_tensor(tA[:], tA[:], tB[:], ALU.subtract), reads=[smb], writes=[smb])
                  S.add("dve", lambda e, rnk=rnk: e.tensor_tensor(Fa[:, rnk, 0, :], tA[:], Ea[:, rnk - 1, 0, :], ALU.add), reads=[smb, Eab], writes=[Fab])
                  S.add("dve", lambda e, rnk=rnk: e.tensor_tensor(tA[:], Fa[:, rnk - 1, 0, :], A9i, ALU.mult), reads=[Fab, smb], writes=[smb])
                  S.add("dve", lambda e, rnk=rnk: e.tensor_tensor(tB[:], Fa[:, rnk - 1, 1, :], A9r, ALU.mult), reads=[Fab, smb], writes=[smb])
                  S.add("dve", lambda e: e.tensor_tensor(tA[:], tA[:], tB[:], ALU.add), reads=[smb], writes=[smb])
                  S.add("dve", lambda e, rnk=rnk: e.tensor_tensor(Fa[:, rnk, 1, :], tA[:], Ea[:, rnk - 1, 1, :], ALU.add), reads=[smb, Eab], writes=[Fab])
              S.dma("sp", f_scr.rearrange("(r p) (a b) -> p r a b", p=128, a=2), Fa[:], reads=[Fab], writes=[scr_tok["f_scr"]])
              Fo, Fob = stg.sb([128, 2, 16], F32)

              def ff(e):
                  r = rank_of(e)
                  return e.dma_start(out=Fo[:].rearrange("p a b -> p (a b)"), in_=f_scr[bass.ds(r * 128, 128), :])
              S.dma_fn("pool", ff, reads=[scr_tok["f_scr"]], writes=[Fob])

              NX = NK + 1
              pp4 = [[stg.sb([128, 2, NK + 8], F32) for _ in range(2)] for _ in range(2)]

              def scan_ops(i, vs, vsb, pp):
                  ops = []
                  src, srcbs = vs, [vsb]
                  for d in range(NLV):
                      s_ = 1 << d
                      dstt, dsttb = pp[d % 2]
                      dpre = pp_pre[id(dsttb)]
                      ar = ald[:, d, 0, i:i + 1]
                      ai = ald[:, d, 1, i:i + 1]
                      nai = ald[:, d, 2, i:i + 1]
                      lo = lambda ri, src=src, s_=s_: src[:, ri, 0:NX - s_]
                      hi = lambda ri, src=src, s_=s_: src[:, ri, s_:NX]
                      o_re = dstt[:, 0, s_:NX]
                      o_im = dstt[:, 1, s_:NX]
                      ops.append(("pool", lambda e, dstt=dstt, src=src, s_=s_: e.tensor_copy(dstt[:, :, 0:s_], src[:, :, 0:s_]), list(srcbs), [dpre]))
                      ops.append(("dve", lambda e, o=o_re, a=lo(0), b=hi(0), ar=ar: e.scalar_tensor_tensor(o, a, ar, b, ALU.mult, ALU.add), srcbs + [smb], [dsttb]))
                      ops.append(("dve", lambda e, o=o_im, a=lo(0), b=hi(1), ai=ai: e.scalar_tensor_tensor(o, a, ai, b, ALU.mult, ALU.add), srcbs + [smb, dsttb], [dsttb]))
                      ops.append(("dve", lambda e, o=o_re, a=lo(1), nai=nai: e.scalar_tensor_tensor(o, a, nai, o, ALU.mult, ALU.add), srcbs + [smb, dsttb], [dsttb]))
                      ops.append(("dve", lambda e, o=o_im, a=lo(1), ar=ar: e.scalar_tensor_tensor(o, a, ar, o, ALU.mult, ALU.add), srcbs + [smb, dsttb], [dsttb]))
                      src, srcbs = dstt, [dsttb, dpre]
                  ops.append(("act", lambda e, i=i, src=src: e.activation(out=Ssb[:, i, :, 0:NX], in_=src[:, :, 0:NX], func=AF.Copy), list(srcbs), [Ssbb]))
                  return ops

              pp_pre = {}
              for pl in pp4:
                  for (_t, _b) in pl:
                      pp_pre[id(_b)] = Buf("pre")

              yg, ygb = stg.sb([128, 4, T], BF16)
              pY = [stg.ps() for _ in range(2)]
              glups = [stg.ps() for _ in range(2)]
              iy = [0]
              def emit_out(q):
                  for j in range(RS):
                      p, pb = pY[iy[0] % 2]
                      iy[0] += 1
                      first = True
                      for tau in range(j + 1):
                          S.add("pe", lambda e, p=p, q=q, tau=tau, j=j, first=first: e.matmul(p[:, 0:NK], Kt[:, q, tau, :], uT[:, q, (j - tau):T:RS], start=first, stop=False),
                                reads=[tokK, uTb], writes=[pb])
                          first = False
                      for m in range(4):
                          i = 4 * q + m
                          ms = slice(m * 32, (m + 1) * 32)
                          for ri in range(2):
                              lastmm = (m == 3 and ri == 1)
                              S.add("pe", lambda e, p=p, ms=ms, j=j, ri=ri, i=i, lastmm=lastmm: e.matmul(p[ms, 0:NK], Gb[:, j, ri, i, :], Ssb[:, i, ri, 0:NK], start=False, stop=lastmm, tile_position=(0, ms.start)),
                                    reads=[tokG, Ssbb], writes=[pb])
                      S.add("act", lambda e, p=p, q=q, j=j: e.activation(out=yg[:, q, j:T:RS], in_=p[:, 0:NK], func=AF.Gelu_apprx_tanh), reads=[pb], writes=[ygb])
              for ip_ in range(8):
                  lists = []
                  for t_ in range(2):
                      i = 2 * ip_ + t_
                      vs, vsb = Vst[t_]
                      compute_V(i, vs, vsb)
                      for ri in range(2):
                          S.add("pool", lambda e, vs=vs, ri=ri, i=i: e.tensor_copy(vs[:, ri, 0:1], Fo[:, ri, i:i + 1]), reads=[Fob], writes=[vsb])
                  if ip_ >= 2 and ip_ % 2 == 0:
                      emit_out((ip_ - 2) // 2)
                  for t_ in range(2):
                      i = 2 * ip_ + t_
                      vs, vsb = Vst[t_]
                      lists.append(scan_ops(i, vs, vsb, pp4[t_]))
                  interleave(lists)
              emit_out(3)

              gs, gsb = stg.sb([128, 4, T], BF16)
              sg = [stg.sb([128, 512], BF16) for _ in range(2)]
              gi = [0]
              bgl = VC["bglu"]

              def cons_glu(oc, m, tb, ps, psb):
                  s_, s_b = sg[gi[0] % 2]
                  gi[0] += 1
                  S.add("act", lambda e: e.activation(out=s_[:], in_=ps[:], func=AF.Sigmoid, bias=vt[:, bgl + oc:bgl + oc + 1]), reads=[psb, vb], writes=[s_b])
                  S.add("dve", lambda e: e.tensor_tensor(gs[:, oc, tb * 512:(tb + 1) * 512], yg[:, oc, tb * 512:(tb + 1) * 512], s_[:], ALU.mult),
                        reads=[s_b, ygb], writes=[gsb])
              linear_T(stg, yg, ygb, 4, kview(W["wglu"][l]), 0, 512, cons_glu, psums=glups)
              S.dma("sp", kview(gsT_d), gs[:], reads=[gsb], writes=[scr_tok["gsT"]])
              stg.done()

              chk("SSM%d" % l)
              stg = Stage()
              QT, QTb = stg.sb([96, 2, 4 * T], BF16)
              KT, KTb = stg.sb([96, 2, 4 * T], BF16)
              VA, VAb = stg.sb([128, 2, 64, 128], BF16)
              mskf, mskfb = stg.sb([128, 4, 512], F32)
              msk, mskb = stg.sb([128, 4, 512], BF16)
              S.dma("sp", mskf[:], masks_in, writes=[mskfb])
              S.add("dve", lambda e: e.tensor_copy(msk[:], mskf[:]), reads=[mskfb], writes=[mskb])
              QTbs = [Buf("QT%d" % i_) for i_ in range(4)]
              KTbs = [Buf("KT%d" % i_) for i_ in range(4)]
              VAbs = [Buf("VA%d" % i_) for i_ in range(4)]
              S.add("pool", lambda e: e.memset(VA[:], 1.0), writes=VAbs)
              for i in range(4):
                  for hh in range(2):
                      for (mine, dst, dstb, mname) in ((q_mine, QT, QTbs[i], "q_mine"), (k_mine, KT, KTbs[i], "k_mine")):
                          S.dma("sp", dst[0:96, hh, i * T:(i + 1) * T], mine[i * 192 + hh * 96:i * 192 + (hh + 1) * 96, :],
                                reads=[scr_tok[mname]], writes=[dstb])
                      for tq in range(4):
                          srcv = v_mine[i * T + tq * 512:i * T + (tq + 1) * 512, hh * 64:(hh + 1) * 64].rearrange("(t p) c -> p t c", p=128)
                          S.dma("sp", VA[:, hh, i * 16 + tq * 4:i * 16 + (tq + 1) * 4, 0:64], srcv, reads=[scr_tok["v_mine"]], writes=[VAbs[i]])
              if "qdbg" in dbg:
                  S.dma("sp", qdbg[0:96, :], QT[0:96, 0, :], reads=[QTb], writes=[scr_tok["qdbg"]])
                  S.dma("sp", qdbg[96:192, :], KT[0:96, 0, :], reads=[KTb], writes=[scr_tok["qdbg"]])
              pSa = [stg.ps([128, 1024]) for _ in range(3)]
              pOa = [stg.ps() for _ in range(2)]
              pTa = [stg.sb([128, 1024], BF16) for _ in range(4)]
              rden = [stg.sb([128, 512], F32) for _ in range(2)]
              ao = [stg.sb([128, 512], BF16) for _ in range(2)]
              tiles = []
              for qb in range(16):
                  for hh in range(2):
                      nkt = 4 * qb + 4
                      for kt in range(0, nkt, 2):
                          tiles.append((qb, hh, kt, nkt))
              LA = 2
              a_tok = [Buf("a_send%d" % j) for j in range(4)]
              msk2 = msk[:].rearrange("p a b -> p (a b)")

              def emit_qk(n):
                  qb, hh, kt, nkt = tiles[n]
                  qsl = slice(qb * 512, (qb + 1) * 512)
                  ps_, psb_ = pSa[n % 3]
                  pt, ptb = pTa[n % 4]
                  for u in range(2):
                      S.add("pe", lambda e, u=u: e.matmul(ps_[:, u * 512:(u + 1) * 512], KT[0:96, hh, (kt + u) * 128:(kt + u + 1) * 128], QT[0:96, hh, qsl], start=True, stop=True),
                            reads=[KTbs[(kt * 128) // T], QTbs[qb // 4]], writes=[psb_])
                  S.add("act", lambda e: e.activation(out=pt[:], in_=ps_[:], func=AF.Exp, scale=96 ** -0.5, bias=-6.0),
                        reads=[psb_], writes=[ptb])
                  if kt >= 4 * qb:
                      j = kt - 4 * qb
                      S.add("dve", lambda e: e.tensor_tensor(pt[:], pt[:], msk2[:, j * 512:(j + 2) * 512], ALU.mult), reads=[ptb, mskb], writes=[ptb])

              def emit_pv(n):
                  qb, hh, kt, nkt = tiles[n]
                  pt, ptb = pTa[n % 4]
                  po, pob = pOa[(qb * 2 + hh) % 2]
                  rdn, rdnb = rden[(qb * 2 + hh) % 2]
                  o, obf = ao[qb % 2]
                  for u in range(2):
                      S.add("pe", lambda e, u=u: e.matmul(po[:], VA[:, hh, kt + u, :], pt[:, u * 512:(u + 1) * 512], start=(kt + u == 0), stop=(kt + u == nkt - 1)),
                            reads=[VAbs[kt // 16], ptb], writes=[pob])
                  if kt + 2 == nkt:
                      S.add("dve", lambda e: e.reciprocal(rdn[64:128, :], po[64:128, :]), reads=[pob], writes=[rdnb])
                      S.add("dve", lambda e: e.tensor_tensor(o[hh * 64:(hh + 1) * 64, :], po[0:64, :], rdn[64:128, :], ALU.mult),
                            reads=[pob, rdnb], writes=[obf])
                      if hh == 1:
                          j = qb // 4
                          S.dma("sp", a_send[j * 128:(j + 1) * 128, (qb % 4) * 512:(qb % 4 + 1) * 512], o[:], reads=[obf], writes=[a_tok[j]])
                          if qb % 4 == 3:
                              S.cc(lambda e: e.collective_compute("AllGather", ALU.bypass, replica_groups=RG, ins=[a_send[j * 128:(j + 1) * 128, :].opt()],
                                                                  outs=[a_all[j * 512:(j + 1) * 512, :].opt()]),
                                   reads=[a_tok[j]], writes=[scr_tok["a_all"]], unit="cca")

              for n in range(len(tiles) + LA):
                  if n < len(tiles):
                      emit_qk(n)
                  if n >= LA:
                      emit_pv(n - LA)
              stg.done()

              chk("B1%d" % l)
              stg = Stage()
              vt, vb = load_vec(stg, l)
              hT, _ = stg.sb([128, 8, T], BF16)
              hTbs = [Buf("hTb%d" % i_) for i_ in range(NB)]
              acts = []
              for nm in ("attnT", "gsT", "xaT"):
                  t_, _ = stg.sb([128, 4, T], BF16)
                  acts.append((t_, [Buf("%s%d" % (nm, i_)) for i_ in range(NB)]))
              for i_ in range(NB):
                  csl = slice(i_ * 512, (i_ + 1) * 512)
                  S.dma("sp", hT[:, :, csl], kview(hT_d)[:, :, csl], reads=[scr_tok["hT"]], writes=[hTbs[i_]])
                  for nm, (t_, tbs_) in zip(("attnT", "gsT", "xaT"), acts):
                      if nm != "attnT":
                          S.dma("sp", t_[:, :, csl], kview(scr[nm])[:, :, csl], reads=[scr_tok[nm]], writes=[tbs_[i_]])
              def fa(e):
                  r = rank_of(e)
                  return e.dma_start(out=attnT_d, in_=a_all[bass.ds(r * 512, 512), :])
              S.dma_fn("sp", fa, reads=[scr_tok["a_all"]], writes=[scr_tok["attnT"]])
              for i_ in range(NB):
                  csl = slice(i_ * 512, (i_ + 1) * 512)
                  S.dma("sp", acts[0][0][:, :, csl], kview(scr["attnT"])[:, :, csl], reads=[scr_tok["attnT"]], writes=[acts[0][1][i_]])
              mg, _ = stg.sb([128, 8, T], BF16)
              mgbs = [[Buf("mg%d_%d" % (o_, i_)) for i_ in range(NB)] for o_ in range(8)]
              wo = [[stg.sb([128, 4, 128], BF16) for _ in range(3)] for _ in range(2)]
              wg = [[stg.sb([128, 8, 128], BF16) for _ in range(3)] for _ in range(2)]
              wov = [kview(W[n][l]) for n in ("womla", "wossm", "wocross")]
              pyy = [stg.ps() for _ in range(3)]
              pgg = [stg.ps() for _ in range(3)]
              sgt = [stg.sb([128, 512], F32) for _ in range(3)]
              tmpm = [stg.sb([128, 512], F32) for _ in range(4)]
              bg = VC["bgate"]
              imc = [0]

              def load_w(oc, brs):
                  for br in brs:
                      S.dma("pool", wo[oc % 2][br][0][:], wov[br][:, :, oc * 128:(oc + 1) * 128], writes=[wo[oc % 2][br][1]])
                      S.dma("pool", wg[oc % 2][br][0][:], win_v[:, :, 1696 + br * 1024 + oc * 128:1696 + br * 1024 + (oc + 1) * 128], writes=[wg[oc % 2][br][1]])

              def grp(oc, tb, br):
                  sl = slice(tb * 512, (tb + 1) * 512)
                  im = imc[0]
                  imc[0] += 1
                  py, pyb = pyy[im % 3]
                  pg, pgb = pgg[im % 3]
                  sgx, sgxb = sgt[im % 3]
                  wot, wotb = wo[oc % 2][br]
                  wgt, wgtb = wg[oc % 2][br]
                  at, atbs = acts[br]
                  for k in range(4):
                      S.add("pe", lambda e, k=k: e.matmul(py[:], wot[:, k, :], at[:, k, sl], start=(k == 0), stop=(k == 3)),
                            reads=[wotb, atbs[tb]], writes=[pyb])
                  for k in range(8):
                      S.add("pe", lambda e, k=k: e.matmul(pg[:], wgt[:, k, :], hT[:, k, sl], start=(k == 0), stop=(k == 7)),
                            reads=[wgtb, hTbs[tb]], writes=[pgb])
                  S.add("act", lambda e: e.activation(out=sgx[:], in_=pg[:], func=AF.Sigmoid, bias=vt[:, bg + br * 8 + oc:bg + br * 8 + oc + 1]),
                        reads=[pgb, vb], writes=[sgxb])
                  return py, pyb, sgx, sgxb

              itm = 0
              for oc in range(8):
                  load_w(oc, (1, 2))
                  for tb in range(NB):
                      sl = slice(tb * 512, (tb + 1) * 512)
                      t1, t1b = tmpm[itm % 4]
                      t2, t2b = tmpm[(itm + 1) % 4]
                      itm += 2
                      py, pyb, sgx, sgxb = grp(oc, tb, 1)
                      S.add("dve", lambda e, t1=t1, py=py, sgx=sgx: e.tensor_tensor(t1[:], py[:], sgx[:], ALU.mult), reads=[pyb, sgxb], writes=[t1b])
                      py, pyb, sgx, sgxb = grp(oc, tb, 2)
                      S.add("dve", lambda e, t2=t2, py=py, sgx=sgx: e.tensor_tensor(t2[:], py[:], sgx[:], ALU.mult), reads=[pyb, sgxb], writes=[t2b])
                      S.add("dve", lambda e, t1=t1, t2=t2, oc=oc, sl=sl: e.tensor_tensor(mg[:, oc, sl], t1[:], t2[:], ALU.add), reads=[t1b, t2b], writes=[mgbs[oc][tb]])
              for oc in range(8):
                  load_w(oc, (0,))
                  for tb in range(NB):
                      sl = slice(tb * 512, (tb + 1) * 512)
                      t1, t1b = tmpm[itm % 4]
                      itm += 1
                      py, pyb, sgx, sgxb = grp(oc, tb, 0)
                      S.add("dve", lambda e, t1=t1, py=py, sgx=sgx: e.tensor_tensor(t1[:], py[:], sgx[:], ALU.mult), reads=[pyb, sgxb], writes=[t1b])
                      S.add("dve", lambda e, t1=t1, oc=oc, sl=sl: e.tensor_tensor(mg[:, oc, sl], mg[:, oc, sl], t1[:], ALU.add), reads=[t1b, mgbs[oc][tb]], writes=[mgbs[oc][tb]])
              mgb = lambda k_, tb_: mgbs[k_][tb_]
              lin2 = [stg.ps() for _ in range(2)]
              xin = [stg.sb([128, 512], F32) for _ in range(6)]
              xo_ = [stg.sb([128, 512], F32) for _ in range(3)]
              ix = [0]
              xcv = kview(x_cur)
              x2v = kview(x2_d)

              NPF = 6

              def load_x(g):
                  oc_, tb_ = g // NB, g % NB
                  xi_, xib_ = xin[g % NPF]
                  S.dma("sp", xi_[:], xcv[:, oc_, tb_ * 512:(tb_ + 1) * 512], reads=[xb_cur], writes=[xib_])
              for g_ in range(NPF):
                  load_x(g_)

              def cons_out(oc, m, tb, ps, psb):
                  g = ix[0]
                  xi, xib = xin[g % NPF]
                  xo2, xo2b = xo_[g % 3]
                  ix[0] += 1
                  sl = slice(tb * 512, (tb + 1) * 512)
                  S.add("dve", lambda e: e.tensor_tensor(xo2[:], ps[:], xi[:], ALU.add), reads=[psb, xib], writes=[xo2b])
                  if g + NPF < 8 * NB:
                      load_x(g + NPF)
                  S.dma("sp", x2v[:, oc, sl], xo2[:], reads=[xo2b], writes=[scr_tok["x2"]])
                  if tb == NB - 1:
                      S.dma("sp", xh_own[oc * 128:(oc + 1) * 128, :], xo2[:, 510:512], reads=[xo2b], writes=[scr_tok["xh_own"]])
              linear_T(stg, mg, mgb, 8, kview(W["wout"][l]), 0, D, cons_out, psums=lin2 + pyy + pgg)
              stg.done()
              chk("B3%d" % l)
              S.cc(lambda e: e.collective_compute("AllGather", ALU.bypass, replica_groups=RG, ins=[xh_own.opt()], outs=[xh_all.opt()]),
                   reads=[scr_tok["xh_own"]], writes=[scr_tok["xh_all"]])

              stg = Stage()
              vt, vb = load_vec(stg, l)
              zt, ztb = stg.sb([128, 8, 2], F32)
              S.add("dve", lambda e: e.memset(zt[:], 0.0), writes=[ztb])
              if l == 0:
                  S.dma("sp", xh_ext[0:D, :].rearrange("(k p) t -> p k t", p=128), zt[:], reads=[ztb], writes=[scr_tok["xh_ext"]])
              S.dma("sp", xh_ext[D:5 * D, :], xh_all, reads=[scr_tok["xh_all"]], writes=[scr_tok["xh_ext"]])
              TH = T + 2
              x2v = kview(x2_d)
              x2, x2b = stg.sb([128, 8, TH], F32)
              h2, h2b = stg.sb([128, 8, TH], BF16)
              x2bs = [Buf("x2h")] + [Buf("x2b%d" % i_) for i_ in range(NB)]
              h2bs = [Buf("h2b%d" % i_) for i_ in range(NB + 1)]
              for i_ in range(NB):
                  S.dma("sp", x2[:, :, 2 + i_ * 512:2 + (i_ + 1) * 512], x2v[:, :, i_ * 512:(i_ + 1) * 512], reads=[scr_tok["x2"]], writes=[x2bs[1 + i_]])

              def fh(e):
                  r = rank_of(e)
                  return e.dma_start(out=x2[:, :, 0:2], in_=xh_ext[bass.ds(r * D, D), :].rearrange("(k p) t -> p k t", p=128))
              S.dma_fn("act", fh, reads=[scr_tok["xh_ext"]], writes=[x2bs[0]])
              sq, sqb = stg.sb([128, 8, 512], BF16)
              pss, pssb = stg.ps()
              rs, rsb = stg.sb([128, 512], F32)
              tm, tmb = stg.sb([128, 512], F32)
              fg = VC["ffng"]
              blocks = [(0, 2)] + [(2 + tb * 512, 2 + (tb + 1) * 512) for tb in range(NB)]
              for bi_, (c0, c1) in (list(enumerate(blocks))[1:] + [(0, blocks[0])]):
                  w_ = c1 - c0
                  x2b = x2bs[bi_]
                  for k in range(8):
                      S.add("act", lambda e, k=k, c0=c0, c1=c1, w_=w_: e.activation(out=sq[:, k, 0:w_], in_=x2[:, k, c0:c1], func=AF.Square), reads=[x2b], writes=[sqb])
                  for k in range(8):
                      S.add("pe", lambda e, k=k, w_=w_: e.matmul(pss[:, 0:w_], ones_b[:], sq[:, k, 0:w_], start=(k == 0), stop=(k == 7)), reads=[sqb, Bc], writes=[pssb])
                  rstd_from(pss[:, 0:w_], rs[:, 0:w_], D, [pssb], [rsb], tm[:, 0:w_], tmb)
                  for k in range(8):
                      S.add("dve", lambda e, k=k, c0=c0, c1=c1, w_=w_: e.scalar_tensor_tensor(h2[:, k, c0:c1], x2[:, k, c0:c1], vt[:, fg + k:fg + k + 1], rs[:, 0:w_], ALU.mult, ALU.mult),
                            reads=[x2b, rsb, vb], writes=[h2bs[bi_]])
                  S.dma("sp", kview(h2T_d)[:, :, c0:c1], h2[:, :, c0:c1], reads=[h2bs[bi_]], writes=[scr_tok["h2T"]])
              stg.done()
              chk("C1%d" % l)
              stg = Stage()
              vt, vb = load_vec(stg, l)
              h2, h2b = stg.sb([128, 8, TH], BF16)
              S.dma("sp", h2[:], kview(h2T_d), reads=[scr_tok["h2T"]], writes=[h2b])
              wupv = kview(W["wup"][l])
              wu = [stg.sb([128, 8, 128], BF16) for _ in range(4)]
              ups = [[stg.sb([128, TH], F32) for _ in range(2)] for _ in range(2)]
              cvs = [[stg.sb([128, T], F32) for _ in range(2)] for _ in range(2)]
              aos = [stg.sb([128, T], BF16) for _ in range(2)]
              pU = [stg.ps() for _ in range(4)]
              iu = 0
              def load_wu(fc_):
                  for half_ in range(2):
                      wt_, wtb_ = wu[(fc_ * 2 + half_) % 4]
                      col_ = half_ * DFF + fc_ * 128
                      S.dma("pool", wt_[:], wupv[:, :, col_:col_ + 128], writes=[wtb_])
              load_wu(0)
              for fc in range(22):
                  if fc + 1 < 22:
                      load_wu(fc + 1)
                  for half in range(2):
                      wt, wtb = wu[(fc * 2 + half) % 4]
                      up, upb = ups[fc % 2][half]
                      for (c0, c1) in blocks:
                          w_ = c1 - c0
                          p, pb = pU[iu % 4]
                          iu += 1
                          for k in range(8):
                              S.add("pe", lambda e, p=p, wt=wt, k=k, c0=c0, c1=c1, w_=w_: e.matmul(p[:, 0:w_], wt[:, k, :], h2[:, k, c0:c1], start=(k == 0), stop=(k == 7)),
                                    reads=[wtb, h2b], writes=[pb])
                          S.add("act", lambda e, p=p, up=up, c0=c0, c1=c1, w_=w_: e.activation(out=up[:, c0:c1], in_=p[:, 0:w_], func=AF.Copy), reads=[pb], writes=[upb])
                      cv, cvb = cvs[fc % 2][half]
                      ch = half * 22 + fc
                      w0, w1, w2, cbb = VC["cw0"] + ch, VC["cw1"] + ch, VC["cw2"] + ch, VC["cb"] + ch
                      S.add("dve", lambda e, cv=cv, up=up, w2=w2, cbb=cbb: e.tensor_scalar(cv[:], up[:, 2:TH], vt[:, w2:w2 + 1], vt[:, cbb:cbb + 1], ALU.mult, ALU.add),
                            reads=[upb, vb], writes=[cvb])
                      S.add("dve", lambda e, cv=cv, up=up, w1=w1: e.scalar_tensor_tensor(cv[:], up[:, 1:TH - 1], vt[:, w1:w1 + 1], cv[:], ALU.mult, ALU.add),
                            reads=[upb, vb, cvb], writes=[cvb])
                      S.add("dve", lambda e, cv=cv, up=up, w0=w0: e.scalar_tensor_tensor(cv[:], up[:, 0:TH - 2], vt[:, w0:w0 + 1], cv[:], ALU.mult, ALU.add),
                            reads=[upb, vb, cvb], writes=[cvb])
                  cg, cgb = cvs[fc % 2][0]
                  cv_, cv_b = cvs[fc % 2][1]
                  ao_, ao_b = aos[fc % 2]
                  S.add("act", lambda e, cg=cg: e.activation(out=cg[:], in_=cg[:], func=AF.Silu), reads=[cgb], writes=[cgb])
                  S.add("dve", lambda e, ao_=ao_, cg=cg, cv_=cv_: e.tensor_tensor(ao_[:], cg[:], cv_[:], ALU.mult), reads=[cgb, cv_b], writes=[ao_b])
                  S.dma("sp", actT_d[fc * 128:(fc + 1) * 128, :], ao_[:], reads=[ao_b], writes=[scr_tok["actT"]])
              stg.done()
              chk("C2%d" % l)
              stg = Stage()
              actT, actTb = stg.sb([128, 22, T], BF16)
              aTbs = [[Buf("aT%d_%d" % (i_, j_)) for j_ in range(NB)] for i_ in range(22)]
              for j_ in range(NB):
                  for i_ in range(22):
                      S.dma("sp", actT[:, i_, j_ * 512:(j_ + 1) * 512], actT_d[i_ * 128:(i_ + 1) * 128, j_ * 512:(j_ + 1) * 512],
                            reads=[scr_tok["actT"]], writes=[aTbs[i_][j_]])
              actTb = lambda k_, tb_: aTbs[k_][tb_]
              xo3 = [stg.sb([128, 512], F32) for _ in range(3)]
              xi3 = [stg.sb([128, 512], F32) for _ in range(3)]
              i3 = [0]
              xnv = kview(x_nxt)

              def cons_down(oc, m, tb, ps, psb):
                  xo2, xo2b = xo3[i3[0] % 3]
                  xi, xib = xi3[i3[0] % 3]
                  i3[0] += 1
                  sl = slice(tb * 512, (tb + 1) * 512)
                  S.dma("sp", xi[:], x2v[:, oc, sl], reads=[scr_tok["x2"]], writes=[xib])
                  S.add("dve", lambda e: e.tensor_tensor(xo2[:], ps[:], xi[:], ALU.add), reads=[psb, xib], writes=[xo2b])
                  S.dma("sp", xnv[:, oc, sl], xo2[:], reads=[xo2b], writes=[xb_nxt])
              linear_T(stg, actT, actTb, 22, W["wdown"][l].rearrange("(k p) n -> p k n", p=128), 0, D, cons_down)
              stg.done()

        except _Stop:
            pass

        S.barrier(final=True)
        S.flush(top, barrier=False)
    return nc


_NC_CACHE = {}


def make_in_maps(inp):
    x = np.asarray(inp["x"], np.float32)
    mem = np.asarray(inp["mem"], np.float32)
    pos = np.asarray(inp["positions"], np.int32)
    ident = np.eye(128, dtype=np.float32)
    masks = np.zeros((128, 4, 512), np.float32)
    kk = np.arange(128)[:, None]
    qq = np.arange(512)[None, :]
    for j in range(4):
        masks[:, j, :] = (qq >= kk + 128 * j).astype(np.float32)
    invf = np.zeros((128, 1), np.float32)
    f = (10000.0 ** (-np.arange(0, 32, 2, dtype=np.float32) / 32)).astype(np.float32)
    invf[0:16, 0] = f
    invf[16:32, 0] = f
    invf[32:48, 0] = -1.0
    invf[48:64, 0] = 1.0
    shared = {"ident": ident, "masks": masks, "invf": invf}
    sw = np.concatenate([np.arange(16, 32), np.arange(0, 16)])
    for l in range(2):
        w_in = np.asarray(inp["w_in"][l], np.float32)
        shared["win%d" % l] = np.ascontiguousarray(np.concatenate([w_in, w_in[:, 640 + sw]], axis=1))
        wq = np.asarray(inp["w_q_b"][l], np.float32)
        qsw_cols = np.concatenate([h * 96 + 64 + sw for h in range(8)])
        shared["wqb%d" % l] = np.ascontiguousarray(np.concatenate([wq, wq[:, qsw_cols]], axis=1))
        wkv = np.asarray(inp["w_kv_b"][l], np.float32).reshape(256, 8, 128)
        shared["wkvk%d" % l] = np.ascontiguousarray(wkv[:, :, :64].reshape(256, 512))
        shared["wkvv%d" % l] = np.ascontiguousarray(wkv[:, :, 64:].reshape(256, 512))
        for nm, key in (("womla", "w_o_mla"), ("wossm", "w_o_ssm"), ("wocross", "w_o_cross"), ("wglu", "w_glu"),
                        ("wmemkv", "w_mem_kv"), ("wout", "w_out"), ("wup", "w_up"), ("wdown", "w_down")):
            shared["%s%d" % (nm, l)] = np.ascontiguousarray(np.asarray(inp[key][l], np.float32))
        shared["vec%d" % l] = pack_vec(inp, l)
        bb, cb = pack_bc(inp, l)
        shared["bblk%d" % l] = bb.reshape(128, -1)
        shared["cblk%d" % l] = cb.reshape(128, -1)
    maps = []
    for c in range(8):
        b, r = c // 4, c % 4
        m = dict(shared)
        m["xT"] = np.ascontiguousarray(x[b, r * T:(r + 1) * T, :].T)
        m["memT"] = np.ascontiguousarray(mem[b].T)
        m["pos"] = np.ascontiguousarray(pos[b, r * T:(r + 1) * T][None, :])
        maps.append(m)
    return maps


def kernel(**inputs):
    if "nc" not in _NC_CACHE:
        _NC_CACHE["nc"] = build()
    nc = _NC_CACHE["nc"]
    maps = make_in_maps(inputs)
    res = run_bass_kernel_spmd(nc, maps, core_ids=list(range(8)))
    out = np.zeros((2, 4 * T, D), np.float32)
    for c in range(8):
        b, r = c // 4, c % 4
        out[b, r * T:(r + 1) * T, :] = np.asarray(res.results[c]["yT"]).T
    return out
```

```python
import math
from contextlib import ExitStack
import numpy as np
import concourse.bass as bass
import concourse.mybir as mybir
from concourse.bass_utils import run_bass_kernel_spmd

F32 = mybir.dt.float32
BF16 = mybir.dt.bfloat16
I32 = mybir.dt.int32
AF = mybir.ActivationFunctionType
ALU = mybir.AluOpType

CE = ["pe", "act", "dve", "pool", "sp"]
NLANES = 24
RG = [[0, 1, 2, 3], [4, 5, 6, 7]]

T = 2048
NB = 4
D = 1024
DFF = 2816
EPS = 1e-6
TWO_PI = 2.0 * math.pi
CW1 = 6.28125
CW2 = TWO_PI - CW1


class Buf:
    __slots__ = ("name", "w", "r")

    def __init__(self, name=""):
        self.name = name
        self.w = None
        self.r = {}


class Sched:
    def __init__(self, nc):
        self.nc = nc
        self.ccu = ["cc", "ccq", "cck", "ccv", "cca"]
        self.units = CE + ["d%d" % i for i in range(NLANES)] + self.ccu
        self.prog = {e: [] for e in CE}
        self.cnt = {u: 0 for u in self.units}
        self.seen = {u: {} for u in self.units}
        self.snaps = {u: [None] for u in self.units}
        self.lane_rr = 0
        self.sems = {}
        self.rank_cache = None

    def _need(self, reads, writes):
        need = {}
        for b in reads:
            if b.w is not None:
                u, c = b.w
                if need.get(u, 0) < c:
                    need[u] = c
        for b in writes:
            if b.w is not None:
                u, c = b.w
                if need.get(u, 0) < c:
                    need[u] = c
            for u, c in b.r.items():
                if need.get(u, 0) < c:
                    need[u] = c
        return need

    def _absorb(self, me, u, c):
        s = self.seen[me]
        if s.get(u, 0) < c:
            s[u] = c
        snap = self.snaps[u][c]
        if snap:
            for k, v in snap.items():
                if s.get(k, 0) < v:
                    s[k] = v

    def _waits(self, eng, need, skip_self=False):
        waits = []
        for u, c in sorted(need.items(), key=lambda kv: -kv[1]):
            if u == eng and skip_self:
                continue
            if self.seen[eng].get(u, 0) >= c:
                continue
            waits.append((u, c))
            self._absorb(eng, u, c)
        return waits

    def add(self, eng, fn, reads=(), writes=()):
        need = self._need(reads, writes)
        waits = self._waits(eng, need, skip_self=(eng == "pe"))
        self.cnt[eng] += 1
        n = self.cnt[eng]
        self.snaps[eng].append(dict(self.seen[eng]))
        self.prog[eng].append((waits, fn, (eng, 1)))
        for b in reads:
            b.r[eng] = n
        for b in writes:
            b.w = (eng, n)
            b.r = {}
        return n

    def dma(self, q, out, in_, reads=(), writes=(), **kw):
        def fn(e, out=out, in_=in_, kw=kw):
            return e.dma_start(out=out, in_=in_, **kw)
        return self.dma_fn(q, fn, reads, writes)

    def dma_fn(self, q, fn, reads=(), writes=()):
        lane = "d%d" % self.lane_rr
        self.lane_rr = (self.lane_rr + 1) % NLANES
        need = self._need(reads, writes)
        prev = self.cnt[lane]
        if prev > 0:
            need[lane] = max(need.get(lane, 0), prev)
        waits = self._waits(q, need)
        self.cnt[lane] += 1
        n = self.cnt[lane]
        self.snaps[lane].append(dict(self.seen[q]))
        self.prog[q].append((waits, fn, (lane, 16)))
        for b in reads:
            b.r[lane] = n
        for b in writes:
            b.w = (lane, n)
            b.r = {}
        return lane, n

    def cc(self, fn, reads=(), writes=(), unit="cc"):
        need = self._need(reads, writes)
        waits = self._waits("pool", need)
        self.cnt[unit] += 1
        n = self.cnt[unit]
        self.snaps[unit].append(dict(self.seen["pool"]))
        self.prog["pool"].append((waits, fn, (unit, 1)))
        for b in reads:
            b.r[unit] = n
        for b in writes:
            b.w = (unit, n)
            b.r = {}
        return n

    def alloc_sems(self, stack):
        for u in self.units:
            self.sems[u] = stack.enter_context(self.nc.semaphore("s_" + u))

    def barrier(self, final=False):
        for eng in CE:
            waits = []
            for u in self.units:
                c = self.cnt[u]
                if u == eng or (u in self.ccu and not final):
                    continue
                if c > 0 and self.seen[eng].get(u, 0) < c:
                    waits.append((u, c))
                    self.seen[eng][u] = c
            if waits:
                self.prog[eng].append((waits, None, None))

    def flush(self, stack, barrier=True):
        if barrier:
            self.barrier()
        nc = self.nc
        block = stack.enter_context(nc.Block())
        mult = {u: (1 if (u in CE or u in self.ccu) else 16) for u in self.units}
        prog = self.prog

        def run(engobj, items):
            self.rank_cache = None
            for waits, fn, inc in items:
                for u, c in waits:
                    engobj.wait_ge(self.sems[u], c * mult[u])
                if fn is not None:
                    ins = fn(engobj)
                    ins.then_inc(self.sems[inc[0]], inc[1])

        @block.tensor
        def _(e):
            run(e, prog["pe"])

        @block.scalar
        def _(e):
            run(e, prog["act"])

        @block.vector
        def _(e):
            run(e, prog["dve"])

        @block.gpsimd
        def _(e):
            run(e, prog["pool"])

        @block.sync
        def _(e):
            run(e, prog["sp"])

        self.prog = {e: [] for e in CE}


VC = {}
_o = 0
for _n, _w in [("mixg", 8), ("qag", 3), ("kvag", 2), ("qg", 1), ("qgsw", 1), ("kg", 1), ("kgsw", 1),
               ("xqg", 1), ("xkg", 1), ("memg", 8), ("bgate", 24), ("ffng", 8), ("cw0", 44), ("cw1", 44),
               ("cw2", 44), ("cb", 44), ("bglu", 4), ("ssmd", 4), ("lre", 16), ("lim", 16), ("ldt", 16)]:
    VC[_n] = _o
    _o += _w
NV = _o


def _chunks(v, n):
    return np.ascontiguousarray(np.asarray(v, np.float32).reshape(n, 128).T)


def pack_vec(inp, l):
    v = np.zeros((128, NV), np.float32)

    def put(name, arr):
        arr = np.asarray(arr, np.float32)
        v[: arr.shape[0], VC[name]: VC[name] + arr.shape[1]] = arr

    put("mixg", _chunks(inp["norm_mix_g"][l], 8))
    put("qag", _chunks(inp["q_a_norm_g"][l], 3))
    put("kvag", _chunks(inp["kv_a_norm_g"][l], 2))
    sw = np.concatenate([np.arange(80, 96), np.arange(64, 80)])
    for nm, g in (("q", inp["q_norm_g"][l]), ("k", inp["k_norm_g"][l])):
        g = np.asarray(g, np.float32)
        put(nm + "g", g[:, None])
        gs = np.zeros((96, 1), np.float32)
        gs[64:96, 0] = g[sw]
        put(nm + "gsw", gs)
    put("xqg", np.asarray(inp["xq_norm_g"][l])[:, None])
    put("xkg", np.asarray(inp["xk_norm_g"][l])[:, None])
    put("memg", _chunks(inp["mem_norm_g"][l], 8))
    put("bgate", _chunks(inp["b_gate"][l], 24))
    put("ffng", _chunks(inp["norm_ffn_g"][l], 8))
    for j in range(3):
        put("cw%d" % j, _chunks(inp["conv_w"][l][j], 44))
    put("cb", _chunks(inp["conv_b"][l], 44))
    put("bglu", _chunks(inp["b_glu"][l], 4))
    put("ssmd", _chunks(np.asarray(inp["ssm_d"][l]).reshape(-1), 4))
    lre = np.asarray(inp["ssm_lambda_re"][l], np.float32).reshape(16, 128).T
    lim = np.asarray(inp["ssm_lambda_im"][l], np.float32).reshape(16, 128).T
    ldt = np.repeat(np.asarray(inp["ssm_log_dt"][l], np.float32)[:, None], 64, 1).reshape(16, 128).T
    put("lre", lre)
    put("lim", lim)
    put("ldt", ldt)
    return v


def pack_bc(inp, l):
    bb = np.zeros((128, 2, 16, 32), np.float32)
    cb = np.zeros((128, 2, 16, 32), np.float32)
    for ri, (bn, cn) in enumerate((("ssm_b_re", "ssm_c_re"), ("ssm_b_im", "ssm_c_im"))):
        B = np.asarray(inp[bn][l], np.float32)
        C = np.asarray(inp[cn][l], np.float32)
        for i in range(16):
            for g2 in range(2):
                g = 2 * i + g2
                bb[g2 * 64:(g2 + 1) * 64, ri, i, g2 * 16:(g2 + 1) * 16] = B[g]
                cb[g2 * 64:(g2 + 1) * 64, ri, i, g2 * 16:(g2 + 1) * 16] = C[g].T
    return bb, cb


class _Stop(Exception):
    pass


def build(dbg=(), stop=None):
    nc = bass.Bass("TRN2", target_bir_lowering=False)
    dbg = set(dbg)

    def chk(name):
        if stop == name:
            raise _Stop()

    def ext_in(name, shape, dt=F32):
        return nc.dram_tensor(name, list(shape), dt, kind="ExternalInput").ap()

    xT_in = ext_in("xT", [D, T])
    memT_in = ext_in("memT", [D, 256])
    pos_in = ext_in("pos", [1, T], I32)
    ident_in = ext_in("ident", [128, 128])
    masks_in = ext_in("masks", [128, 4, 512])
    invf_in = ext_in("invf", [128, 1])
    W = {}
    for nm, shp in [("win", [D, 4800]), ("wqb", [384, 1024]), ("wkvk", [256, 512]), ("wkvv", [256, 512]),
                    ("womla", [512, D]), ("wossm", [512, D]), ("wocross", [512, D]), ("wglu", [512, 512]),
                    ("wmemkv", [D, D]), ("wout", [D, D]), ("wup", [D, 2 * DFF]), ("wdown", [DFF, D]),
                    ("vec", [128, NV]), ("bblk", [128, 2 * 16 * 32]), ("cblk", [128, 2 * 16 * 32])]:
        W[nm] = [ext_in("%s%d" % (nm, l), shp) for l in range(2)]
    yT_out = nc.dram_tensor("yT", [D, T], F32, kind="ExternalOutput").ap()

    scr = {}
    scr_tok = {}

    def scratch(name, shape, dt, internal=False):
        kind = "Internal"
        if name in dbg and not internal:
            kind = "ExternalOutput"
        t = nc.dram_tensor(name, list(shape), dt, kind=kind)
        scr[name] = t.ap()
        scr_tok[name] = Buf(name)
        return t.ap()

    xs = [scratch("xs0", [D, T], F32), scratch("xs1", [D, T], F32)]
    hT_d = scratch("hT", [D, T], BF16)
    cqT_d = scratch("cqT", [384, T], BF16)
    ckvT_d = scratch("ckvT", [256, T], BF16)
    krT_d = scratch("krT", [64, T], F32)
    uT_d = scratch("uT", [512, T], BF16)
    xqT_d = scratch("xqT", [512, T], BF16)
    ropeC_d = scratch("ropeC", [32, T], F32)
    ropeS_d = scratch("ropeS", [32, T], F32)
    q_own = scratch("q_own", [768, T], BF16, True)
    k_own = scratch("k_own", [768, T], BF16, True)
    v_own = scratch("v_own", [4 * T, 128], BF16, True)
    qk_all = scratch("qk_all", [2 * 4 * 768, T], BF16, True)
    q_all = qk_all[0:4 * 768, :]
    k_all = qk_all[4 * 768:8 * 768, :]
    for _n, _ap in (("q_all", q_all), ("k_all", k_all)):
        scr[_n] = _ap
        scr_tok[_n] = Buf(_n)
    v_all = scratch("v_all", [16 * T, 128], BF16, True)
    qk_mine = scratch("qk_mine", [2 * 4 * 192, T], BF16)
    q_mine = qk_mine[0:768, :]
    k_mine = qk_mine[768:1536, :]
    for _n in ("q_mine", "k_mine"):
        scr_tok[_n] = Buf(_n)
    v_mine = scratch("v_mine", [4 * T, 128], BF16)
    a_send = scratch("a_send", [4 * 128, T], BF16, True)
    a_all = scratch("a_all", [16 * 128, T], BF16, True)
    attnT_d = scratch("attnT", [512, T], BF16)
    xaT_d = scratch("xaT", [512, T], BF16)
    gsT_d = scratch("gsT", [512, T], BF16)
    e_own = scratch("e_own", [128, 32], F32, True)
    e_all = scratch("e_all", [4 * 128, 32], F32, True)
    f_scr = scratch("f_scr", [4 * 128, 32], F32)
    xh_own = scratch("xh_own", [D, 2], F32, True)
    xh_all = scratch("xh_all", [4 * D, 2], F32, True)
    xh_ext = scratch("xh_ext", [5 * D, 2], F32)
    x2_d = scratch("x2", [D, T], F32)
    h2T_d = scratch("h2T", [D, T + 2], BF16)
    actT_d = scratch("actT", [DFF, T], BF16)
    qdbg = scratch("qdbg", [192, 4 * T], BF16)
    scratch("dq", [768, T], BF16)
    scratch("dk", [768, T], BF16)
    scratch("dv", [4 * T, 128], BF16)

    def kview(ap, p=128):
        return ap.rearrange("(k p) t -> p k t", p=p)

    with ExitStack() as top:
        S = Sched(nc)
        S.alloc_sems(top)

        def rank_of(e):
            if S.rank_cache is None:
                S.rank_cache = e.partition_id() % 4
            return S.rank_cache

        ident_b = top.enter_context(nc.sbuf_tensor("ident_b", [128, 128], BF16))
        ones_b = top.enter_context(nc.sbuf_tensor("ones_b", [128, 128], BF16))
        Bc = Buf("consts")
        S.dma("pool", ident_b[:], ident_in, writes=[Bc])
        S.add("dve", lambda e: e.memset(ones_b[:], 1.0), writes=[Bc])
        eps_t = top.enter_context(nc.sbuf_tensor("eps_t", [128, 1], F32))
        S.add("dve", lambda e: e.memset(eps_t[:], EPS), writes=[Bc])

        class Stage:
            _ctr = [0]

            def __init__(self):
                self.st = ExitStack()

            def sb(self, shape, dt, name=None):
                Stage._ctr[0] += 1
                t = self.st.enter_context(nc.sbuf_tensor("t%d" % Stage._ctr[0], list(shape), dt))
                return t, Buf()

            def ps(self, shape=(128, 512), dt=F32):
                Stage._ctr[0] += 1
                t = self.st.enter_context(nc.psum_tensor("p%d" % Stage._ctr[0], list(shape), dt))
                return t, Buf()

            def done(self, barrier=True):
                S.flush(self.st, barrier=barrier)
                self.st.close()

        def load_vec(stg, l):
            vt, vb = stg.sb([128, NV], F32)
            S.dma("sp", vt[:], W["vec"][l], writes=[vb])
            return vt, vb

        def rstd_from(ps_ap, out_ap, n, reads, writes, tmp_ap, tmpb):
            np_ = out_ap.shape[0]
            S.add("act", lambda e: e.activation(out=tmp_ap, in_=ps_ap, func=AF.Ln, scale=1.0 / n, bias=eps_t[0:np_, 0:1]), reads=list(reads) + [Bc], writes=[tmpb])
            S.add("act", lambda e: e.activation(out=out_ap, in_=tmp_ap, func=AF.Exp, scale=-0.5), reads=[tmpb], writes=writes)

        def norm_T(stg, src, srcb, kch, ncols, gcol, vt, vb, dst, dstb, nblk=None, bw=512, psum=None):
            nblk = nblk if nblk is not None else ncols // bw
            sq, sqb = stg.sb([128, kch, bw], BF16)
            pss, pssb = psum if psum is not None else stg.ps()
            rs, rsb = stg.sb([128, bw], F32)
            tm, tmb = stg.sb([128, bw], F32)
            for tb in range(nblk):
                sl = slice(tb * bw, (tb + 1) * bw)
                for k in range(kch):
                    S.add("act", lambda e, k=k, sl=sl: e.activation(out=sq[:, k, :], in_=src[:, k, sl], func=AF.Square),
                          reads=[srcb(tb) if callable(srcb) else srcb], writes=[sqb])
                for k in range(kch):
                    S.add("pe", lambda e, k=k: e.matmul(pss[:, 0:bw], ones_b[:], sq[:, k, :], start=(k == 0), stop=(k == kch - 1)),
                          reads=[sqb, Bc], writes=[pssb])
                rstd_from(pss[:, 0:bw], rs[:], 128 * kch, [pssb], [rsb], tm[:], tmb)
                for k in range(kch):
                    S.add("dve", lambda e, k=k, sl=sl: e.scalar_tensor_tensor(
                        dst[:, k, sl], src[:, k, sl], vt[:, gcol + k:gcol + k + 1], rs[:], ALU.mult, ALU.mult),
                        reads=[srcb(tb) if callable(srcb) else srcb, rsb, vb], writes=[dstb(tb) if callable(dstb) else dstb])

        def linear_T(stg, inT, inb, kch, wview, col0, ncols, consumer, nblk=NB, bw=512, wq="pool", psums=None):
            wsp = 2 if kch >= 16 else 1
            wb = [(stg.sb([128, kch, 128], BF16)[0], [Buf() for _ in range(wsp)]) for _ in range(2)]
            kcut = [(i_ * kch) // wsp for i_ in range(wsp + 1)]
            pss = psums if psums is not None else [stg.ps() for _ in range(3)]
            npz = len(pss)
            noc = (ncols + 127) // 128
            it = 0
            for oc in range(noc):
                m = min(128, ncols - oc * 128)
                wt, wtbs = wb[oc % 2]
                for i_ in range(wsp):
                    S.dma(wq, wt[:, kcut[i_]:kcut[i_ + 1], 0:m], wview[:, kcut[i_]:kcut[i_ + 1], col0 + oc * 128: col0 + oc * 128 + m], writes=[wtbs[i_]])
                for tb in range(nblk):
                    ps, psb = pss[it % npz]
                    it += 1
                    for k in range(kch):
                        S.add("pe", lambda e, k=k, ps=ps, wt=wt, m=m, tb=tb: e.matmul(
                            ps[0:m, 0:bw], wt[:, k, 0:m], inT[:, k, tb * bw:(tb + 1) * bw], start=(k == 0), stop=(k == kch - 1)),
                            reads=[wtbs[min(wsp - 1, (k * wsp) // kch)], (inb(k, tb) if callable(inb) else inb)], writes=[psb])
                    consumer(oc, m, tb, ps, psb)

        stg = Stage()
        posi, posib = stg.sb([32, T], I32)
        posf, posfb = stg.sb([32, T], F32)
        invf, invfb = stg.sb([32, 1], F32)
        S.dma("sp", posi[:], pos_in.partition_broadcast(32), writes=[posib])
        S.dma("sp", invf[:], invf_in[0:32, :], writes=[invfb])
        S.add("dve", lambda e: e.tensor_copy(posf[:], posi[:]), reads=[posib], writes=[posfb])
        ang, angb = stg.sb([32, T], F32)
        qn_, qnb = stg.sb([32, T], F32)
        ni, nib = stg.sb([32, T], I32)
        nf, nfb = stg.sb([32, T], F32)
        rr, rrb = stg.sb([32, T], F32)
        mk, mkb = stg.sb([32, T], F32)
        sn, snb = stg.sb([32, T], F32)
        cs, csb = stg.sb([32, T], F32)
        S.add("dve", lambda e: e.tensor_scalar(ang[:], posf[:], invf[:, 0:1], None, ALU.mult), reads=[posfb, invfb], writes=[angb])
        S.add("dve", lambda e: e.tensor_scalar(qn_[:], ang[:], 1.0 / TWO_PI, None, ALU.mult), reads=[angb], writes=[qnb])
        S.add("dve", lambda e: e.tensor_copy(ni[:], qn_[:]), reads=[qnb], writes=[nib])
        S.add("dve", lambda e: e.tensor_copy(nf[:], ni[:]), reads=[nib], writes=[nfb])
        S.add("dve", lambda e: e.scalar_tensor_tensor(rr[:], nf[:], -CW1, ang[:], ALU.mult, ALU.add), reads=[nfb, angb], writes=[rrb])
        S.add("dve", lambda e: e.scalar_tensor_tensor(rr[:], nf[:], -CW2, rr[:], ALU.mult, ALU.add), reads=[nfb, rrb], writes=[rrb])
        S.add("dve", lambda e: e.tensor_scalar(rr[:], rr[:], math.pi, -math.pi, ALU.min, ALU.max), reads=[rrb], writes=[rrb])
        S.add("act", lambda e: e.activation(out=sn[:], in_=rr[:], func=AF.Sin), reads=[rrb], writes=[snb])
        S.add("dve", lambda e: e.tensor_scalar(qn_[:], rr[:], math.pi / 2, None, ALU.add), reads=[rrb], writes=[qnb])
        S.add("dve", lambda e: e.tensor_scalar(mk[:], qn_[:], math.pi, -TWO_PI, ALU.is_gt, ALU.mult), reads=[qnb], writes=[mkb])
        S.add("dve", lambda e: e.tensor_tensor(qn_[:], qn_[:], mk[:], ALU.add), reads=[qnb, mkb], writes=[qnb])
        S.add("dve", lambda e: e.tensor_scalar(qn_[:], qn_[:], math.pi, -math.pi, ALU.min, ALU.max), reads=[qnb], writes=[qnb])
        S.add("act", lambda e: e.activation(out=cs[:], in_=qn_[:], func=AF.Sin), reads=[qnb], writes=[csb])
        sgn, sgnb = stg.sb([32, 1], F32)
        S.dma("sp", sgn[:], invf_in[32:64, :], writes=[sgnb])
        S.add("dve", lambda e: e.tensor_scalar(sn[:], sn[:], sgn[:, 0:1], None, ALU.mult), reads=[snb, sgnb], writes=[snb])
        S.dma("sp", ropeC_d, cs[:], reads=[csb], writes=[scr_tok["ropeC"]])
        S.dma("sp", ropeS_d, sn[:], reads=[snb], writes=[scr_tok["ropeS"]])
        stg.done()

        try:
          for l in (range(2) if stop != "s0" else []):
              x_cur = xs[l % 2] if l > 0 else xT_in
              xb_cur = scr_tok["xs%d" % (l % 2)] if l > 0 else Buf("x_in")
              x_nxt = xs[(l + 1) % 2] if l == 0 else yT_out
              xb_nxt = scr_tok["xs%d" % ((l + 1) % 2)] if l == 0 else Buf("yout")
              win_v = kview(W["win"][l])
              stg = Stage()
              vt, vb = load_vec(stg, l)
              xt, xtb = stg.sb([128, 8, T], F32)
              hT, hTb = stg.sb([128, 8, T], BF16)
              xtbs = [Buf("xt%d" % i_) for i_ in range(NB)]
              hTbs = [Buf("hT%d" % i_) for i_ in range(NB)]
              for i_ in range(NB):
                  S.dma("sp", xt[:, :, i_ * 512:(i_ + 1) * 512], kview(x_cur)[:, :, i_ * 512:(i_ + 1) * 512], reads=[xb_cur], writes=[xtbs[i_]])
              norm_T(stg, xt, lambda tb_: xtbs[tb_], 8, T, VC["mixg"], vt, vb, hT, lambda tb_: hTbs[tb_])
              S.dma("sp", kview(hT_d), hT[:], reads=hTbs, writes=[scr_tok["hT"]])
              hTb = lambda k_, tb_: hTbs[tb_]
              stage_o = [stg.sb([128, 512], BF16) for _ in range(3)]
              stage_f = [stg.sb([128, 512], F32) for _ in range(2)]
              lps = [stg.ps() for _ in range(4)]
              cnt = [0]

              def cons_bf(dst_ap_fn, tokname):
                  def c(oc, m, tb, ps, psb):
                      so, sob = stage_o[cnt[0] % 3]
                      cnt[0] += 1
                      eng = "act" if cnt[0] % 2 else "dve"
                      if eng == "act":
                          S.add("act", lambda e: e.activation(out=so[0:m, :], in_=ps[0:m, :], func=AF.Copy), reads=[psb], writes=[sob])
                      else:
                          S.add("dve", lambda e: e.tensor_copy(so[0:m, :], ps[0:m, :]), reads=[psb], writes=[sob])
                      S.dma("sp", dst_ap_fn(oc, m, tb), so[0:m, :], reads=[sob], writes=[scr_tok[tokname]])
                  return c

              linear_T(stg, hT, hTb, 8, win_v, 0, 384, cons_bf(lambda oc, m, tb: cqT_d[oc * 128:oc * 128 + m, tb * 512:(tb + 1) * 512], "cqT"), psums=lps)
              linear_T(stg, hT, hTb, 8, win_v, 384, 256, cons_bf(lambda oc, m, tb: ckvT_d[oc * 128:oc * 128 + m, tb * 512:(tb + 1) * 512], "ckvT"), psums=lps)
              linear_T(stg, hT, hTb, 8, win_v, 672, 512, cons_bf(lambda oc, m, tb: uT_d[oc * 128:oc * 128 + m, tb * 512:(tb + 1) * 512], "uT"), psums=lps)
              linear_T(stg, hT, hTb, 8, win_v, 1184, 512, cons_bf(lambda oc, m, tb: xqT_d[oc * 128:oc * 128 + m, tb * 512:(tb + 1) * 512], "xqT"), psums=lps)

              def cons_kr(half):
                  def c(oc, m, tb, ps, psb):
                      so, sob = stage_f[cnt[0] % 2]
                      cnt[0] += 1
                      S.add("act", lambda e: e.activation(out=so[0:32, :], in_=ps[0:32, :], func=AF.Copy), reads=[psb], writes=[sob])
                      S.dma("sp", krT_d[half * 32:(half + 1) * 32, tb * 512:(tb + 1) * 512], so[0:32, :], reads=[sob], writes=[scr_tok["krT"]])
                  return c
              linear_T(stg, hT, hTb, 8, win_v, 640, 32, cons_kr(0), psums=lps)
              linear_T(stg, hT, hTb, 8, win_v, 4768, 32, cons_kr(1), psums=lps)
              stg.done()

              chk("A1%d" % l)
              stg = Stage()
              vt, vb = load_vec(stg, l)
              cq, cqb = stg.sb([128, 3, T], BF16)
              ckv, ckvb = stg.sb([128, 2, T], BF16)
              cqn, cqnb = stg.sb([128, 3, T], BF16)
              ckvn, ckvnb = stg.sb([128, 2, T], BF16)
              S.dma("sp", cq[:], kview(cqT_d), reads=[scr_tok["cqT"]], writes=[cqb])
              S.dma("sp", ckv[:], kview(ckvT_d), reads=[scr_tok["ckvT"]], writes=[ckvb])
              npsA2 = stg.ps()
              norm_T(stg, cq, cqb, 3, T, VC["qag"], vt, vb, cqn, cqnb, psum=npsA2)
              norm_T(stg, ckv, ckvb, 2, T, VC["kvag"], vt, vb, ckvn, ckvnb, psum=npsA2)
              rc, rcb = stg.sb([96, T], F32)
              rsn, rsnb = stg.sb([96, T], F32)
              kr, krb = stg.sb([96, T], F32)
              krs, krsb = stg.sb([96, T], F32)
              S.dma("sp", rc[64:96, :], ropeC_d, reads=[scr_tok["ropeC"]], writes=[rcb])
              S.dma("sp", rsn[64:96, :], ropeS_d, reads=[scr_tok["ropeS"]], writes=[rsnb])
              S.dma("sp", kr[64:96, :], krT_d[0:32, :], reads=[scr_tok["krT"]], writes=[krb])
              S.dma("sp", krs[64:96, :], krT_d[32:64, :], reads=[scr_tok["krT"]], writes=[krsb])
              wq, wqb_ = stg.sb([128, 3, 1024], BF16)
              wkk, wkkb = stg.sb([128, 2, 512], BF16)
              wkv, wkvb = stg.sb([128, 2, 512], BF16)
              S.dma("pool", wq[:], kview(W["wqb"][l]), writes=[wqb_])
              S.dma("pool", wkk[:], kview(W["wkvk"][l]), writes=[wkkb])
              S.dma("pool", wkv[:], kview(W["wkvv"][l]), writes=[wkvb])
              krr, krrb = stg.sb([96, T], F32)
              t1, t1b = stg.sb([96, T], F32)
              R = slice(64, 96)
              gk = VC["kg"]
              S.add("dve", lambda e: e.scalar_tensor_tensor(krr[R, :], kr[R, :], vt[R, gk:gk + 1], rc[R, :], ALU.mult, ALU.mult),
                    reads=[krb, rcb, vb], writes=[krrb])
              S.add("dve", lambda e: e.scalar_tensor_tensor(t1[R, :], krs[R, :], vt[R, gk + 1:gk + 2], rsn[R, :], ALU.mult, ALU.mult),
                    reads=[krsb, rsnb, vb], writes=[t1b])
              S.add("dve", lambda e: e.tensor_tensor(krr[R, :], krr[R, :], t1[R, :], ALU.add), reads=[krrb, t1b], writes=[krrb])
              krsq, krsqb = stg.sb([96, T], BF16)
              S.add("act", lambda e: e.activation(out=krsq[R, :], in_=kr[R, :], func=AF.Square), reads=[krb], writes=[krsqb])

              psq = [stg.ps() for _ in range(3)]
              psw = [stg.ps() for _ in range(2)]
              pss2 = [stg.ps() for _ in range(2)]
              sqs = [stg.sb([96, 512], BF16) for _ in range(3)]
              rss = [stg.sb([96, 512], F32) for _ in range(3)]
              tms = [stg.sb([96, 512], F32) for _ in range(2)]
              qf = [stg.sb([96, 512], F32) for _ in range(2)]
              qsf = [stg.sb([96, 512], F32) for _ in range(2)]
              ob = [stg.sb([96, 512], BF16) for _ in range(3)]
              gq = VC["qg"]
              GCq, GCqb = stg.sb([96, T], F32)
              GSq, GSqb = stg.sb([96, T], F32)
              S.add("dve", lambda e: e.tensor_scalar(GCq[R, :], rc[R, :], vt[R, gq:gq + 1], None, ALU.mult), reads=[rcb, vb], writes=[GCqb])
              S.add("dve", lambda e: e.tensor_scalar(GSq[R, :], rsn[R, :], vt[R, gq + 1:gq + 2], None, ALU.mult), reads=[rsnb, vb], writes=[GSqb])
              items = []
              for h in range(8):
                  for tb in range(NB):
                      items.append(("q", h, tb))
                      items.append(("k", h, tb))

              def a2_stage1(n):
                  kind, h, tb = items[n]
                  sl = slice(tb * 512, (tb + 1) * 512)
                  pq, pqb = psq[n % 3]
                  sq, sqb = sqs[n % 3]
                  if kind == "q":
                      pw, pwb = psw[(n // 2) % 2]
                      for k in range(3):
                          S.add("pe", lambda e, k=k: e.matmul(pq[0:96, :], wq[:, k, h * 96:(h + 1) * 96], cqn[:, k, sl],
                                                              start=(k == 0), stop=(k == 2)), reads=[wqb_, cqnb], writes=[pqb])
                      for k in range(3):
                          S.add("pe", lambda e, k=k: e.matmul(pw[64:96, :], wq[:, k, 768 + h * 32:768 + (h + 1) * 32], cqn[:, k, sl],
                                                              start=(k == 0), stop=(k == 2), tile_position=(0, 64)), reads=[wqb_, cqnb], writes=[pwb])
                      S.add("act", lambda e: e.activation(out=sq[0:96, :], in_=pq[0:96, :], func=AF.Square), reads=[pqb], writes=[sqb])
                  else:
                      for k in range(2):
                          S.add("pe", lambda e, k=k: e.matmul(pq[0:64, :], wkk[:, k, h * 64:(h + 1) * 64], ckvn[:, k, sl],
                                                              start=(k == 0), stop=(k == 1)), reads=[wkkb, ckvnb], writes=[pqb])
                      S.add("act", lambda e: e.activation(out=sq[0:64, :], in_=pq[0:64, :], func=AF.Square), reads=[pqb], writes=[sqb])

              def a2_stage2(n):
                  kind, h, tb = items[n]
                  sl = slice(tb * 512, (tb + 1) * 512)
                  sq, sqb = sqs[n % 3]
                  p2, p2b = pss2[n % 2]
                  rs, rsb = rss[n % 3]
                  tm, tmb = tms[n % 2]
                  if kind == "q":
                      S.add("pe", lambda e: e.matmul(p2[0:96, :], ones_b[0:96, 0:96], sq[0:96, :], start=True, stop=True),
                            reads=[sqb, Bc], writes=[p2b])
                  else:
                      S.add("pe", lambda e: e.matmul(p2[0:96, :], ones_b[0:64, 0:96], sq[0:64, :], start=True, stop=False, tile_position=(0, 0)),
                            reads=[sqb, Bc], writes=[p2b])
                      S.add("pe", lambda e: e.matmul(p2[0:96, :], ones_b[64:96, 0:96], krsq[64:96, sl], start=False, stop=True, tile_position=(64, 0)),
                            reads=[krsqb, Bc], writes=[p2b])
                  rstd_from(p2[0:96, :], rs[0:96, :], 96, [p2b], [rsb], tm[0:96, :], tmb)

              def a2_stage3(n):
                  kind, h, tb = items[n]
                  sl = slice(tb * 512, (tb + 1) * 512)
                  pq, pqb = psq[n % 3]
                  rs, rsb = rss[n % 3]
                  o, obf = ob[n % 3]
                  if kind == "q":
                      pw, pwb = psw[(n // 2) % 2]
                      q_f, q_fb = qf[(n // 2) % 2]
                      qs_f, qs_fb = qsf[(n // 2) % 2]
                      S.add("dve", lambda e: e.tensor_tensor(q_f[R, :], pq[R, :], GCq[R, sl], ALU.mult), reads=[pqb, GCqb], writes=[q_fb])
                      S.add("dve", lambda e: e.tensor_tensor(qs_f[R, :], pw[R, :], GSq[R, sl], ALU.mult), reads=[pwb, GSqb], writes=[qs_fb])
                      S.add("dve", lambda e: e.scalar_tensor_tensor(o[0:64, :], pq[0:64, :], vt[0:64, gq:gq + 1], rs[0:64, :], ALU.mult, ALU.mult),
                            reads=[pqb, rsb, vb], writes=[obf])
                      S.add("dve", lambda e: e.tensor_tensor(q_f[R, :], q_f[R, :], qs_f[R, :], ALU.add), reads=[q_fb, qs_fb], writes=[q_fb])
                      S.add("dve", lambda e: e.tensor_tensor(o[R, :], q_f[R, :], rs[R, :], ALU.mult), reads=[q_fb, rsb], writes=[obf])
                      S.dma("sp", q_own[h * 96:(h + 1) * 96, sl], o[0:96, :], reads=[obf], writes=[scr_tok["q_own"]])
                  else:
                      S.add("dve", lambda e: e.scalar_tensor_tensor(o[0:64, :], pq[0:64, :], vt[0:64, gk:gk + 1], rs[0:64, :], ALU.mult, ALU.mult),
                            reads=[pqb, rsb, vb], writes=[obf])
                      S.add("dve", lambda e: e.tensor_tensor(o[R, :], krr[R, sl], rs[R, :], ALU.mult), reads=[krrb, rsb], writes=[obf])
                      S.dma("sp", k_own[h * 96:(h + 1) * 96, sl], o[0:96, :], reads=[obf], writes=[scr_tok["k_own"]])

              NI = len(items)
              for n in range(NI + 2):
                  if n < NI:
                      a2_stage1(n)
                  if 0 <= n - 1 < NI:
                      a2_stage2(n - 1)
                  if n - 2 >= 0:
                      a2_stage3(n - 2)
              pv = psq[0:2]
              vo = [stg.sb([128, 512], BF16) for _ in range(2)]
              for tt in range(16):
                  p, pb = pv[tt % 2]
                  o, obf = vo[tt % 2]
                  for k in range(2):
                      S.add("pe", lambda e, k=k, p=p, tt=tt: e.matmul(p[:, :], ckvn[:, k, tt * 128:(tt + 1) * 128], wkv[:, k, :], start=(k == 0), stop=(k == 1)),
                            reads=[ckvnb, wkvb], writes=[pb])
                  S.add("act", lambda e, o=o, p=p: e.activation(out=o[:], in_=p[:], func=AF.Copy), reads=[pb], writes=[obf])
                  S.dma("sp", v_own.rearrange("(j t) c -> t j c", j=4)[tt * 128:(tt + 1) * 128, :, :], o[:].rearrange("p (j c) -> p j c", j=4),
                        reads=[obf], writes=[scr_tok["v_own"]])
              stg.done()
              if "dq" in dbg and l == 0:
                  for (sname, dname) in (("q_own", "dq"), ("k_own", "dk"), ("v_own", "dv")):
                      S.dma("sp", scr[dname], scr[sname], reads=[scr_tok[sname]], writes=[scr_tok[dname]])
                  with ExitStack() as tmpst:
                      S.flush(tmpst, barrier=True)
              chk("A2%d" % l)
              stg = Stage()
              vt, vb = load_vec(stg, l)
              mt_, mtb = stg.sb([128, 8, 256], F32)
              mn, mnb = stg.sb([128, 8, 256], BF16)
              S.dma("sp", mt_[:], kview(memT_in), writes=[mtb])
              pssx, pssxb = stg.ps()
              norm_T(stg, mt_, mtb, 8, 256, VC["memg"], vt, vb, mn, mnb, nblk=1, bw=256, psum=(pssx, pssxb))
              kxT, kxTb = stg.sb([128, 4, 256], BF16)
              vx, vxb = stg.sb([128, 2, 512], BF16)
              wmv = kview(W["wmemkv"][l])
              sqx, sqxb = stg.sb([128, 512], BF16)
              rsx, rsxb = stg.sb([128, 512], F32)
              tmx, tmxb = stg.sb([128, 512], F32)
              xkg = VC["xkg"]

              def cons_kx(oc, m, tb, ps, psb):
                  S.add("act", lambda e: e.activation(out=sqx[:, 0:256], in_=ps[:, 0:256], func=AF.Square), reads=[psb], writes=[sqxb])
                  S.add("pe", lambda e: e.matmul(pssx[:, 0:256], ones_b[:], sqx[:, 0:256], start=True, stop=True), reads=[sqxb, Bc], writes=[pssxb])
                  rstd_from(pssx[:, 0:256], rsx[:, 0:256], 128, [pssxb], [rsxb], tmx[:, 0:256], tmxb)
                  S.add("dve", lambda e: e.scalar_tensor_tensor(kxT[:, oc, :], ps[:, 0:256], vt[:, xkg:xkg + 1], rsx[:, 0:256], ALU.mult, ALU.mult),
                        reads=[psb, rsxb, vb], writes=[kxTb])
              pvx = [stg.ps() for _ in range(2)]
              linear_T(stg, mn, mnb, 8, wmv, 0, 512, cons_kx, nblk=1, bw=256, psums=pvx)
              wv_, wv_b = stg.sb([128, 8, 512], BF16)
              S.dma("pool", wv_[:], wmv[:, :, 512:1024], writes=[wv_b])
              for mt in range(2):
                  p, pb = pvx[mt]
                  for k in range(8):
                      S.add("pe", lambda e, k=k, p=p, mt=mt: e.matmul(p[:, :], mn[:, k, mt * 128:(mt + 1) * 128], wv_[:, k, :], start=(k == 0), stop=(k == 7)),
                            reads=[mnb, wv_b], writes=[pb])
                  S.add("act", lambda e, p=p, mt=mt: e.activation(out=vx[:, mt, :], in_=p[:], func=AF.Copy), reads=[pb], writes=[vxb])
              xq, xqb = stg.sb([128, 4, T], BF16)
              S.dma("sp", xq[:], kview(xqT_d), reads=[scr_tok["xqT"]], writes=[xqb])
              sqx2 = [(sqx, sqxb), stg.sb([128, 512], BF16)]
              rsx2 = [(rsx, rsxb), stg.sb([128, 512], F32)]
              tmx2 = [(tmx, tmxb), stg.sb([128, 512], F32)]
              pssx2 = [(pssx, pssxb), stg.ps()]
              rd2 = [stg.sb([128, 512], F32) for _ in range(2)]
              qxn = [stg.sb([128, 512], BF16) for _ in range(2)]
              pT = [stg.sb([128, 512], BF16) for _ in range(3)]
              pS = [stg.ps() for _ in range(2)]
              pO = [stg.ps() for _ in range(1)]
              pD = [stg.ps() for _ in range(1)]
              xo = [stg.sb([128, 512], BF16) for _ in range(2)]
              xqg = VC["xqg"]
              it = 0
              ip = 0
              qxn = qxn + [stg.sb([128, 512], BF16)]
              pT = pT + [stg.sb([128, 512], BF16)]
              xitems = [(hx, tb) for hx in range(4) for tb in range(NB)]
              pO2 = [pO[0], pvx[0]]
              pD2 = [pD[0], pvx[1]]
              rt2 = [stg.sb([128, 512], F32) for _ in range(2)]

              def x_s1(n):
                  hx, tb = xitems[n]
                  sl = slice(tb * 512, (tb + 1) * 512)
                  qx, qxb = qxn[n % 3]
                  sqx_, sqx_b = sqx2[n % 2]
                  rsx_, rsx_b = rsx2[n % 2]
                  tmx_, tmx_b = tmx2[n % 2]
                  pssx_, pssx_b = pssx2[n % 2]
                  S.add("act", lambda e: e.activation(out=sqx_[:], in_=xq[:, hx, sl], func=AF.Square), reads=[xqb], writes=[sqx_b])
                  S.add("pe", lambda e: e.matmul(pssx_[:], ones_b[:], sqx_[:], start=True, stop=True), reads=[sqx_b, Bc], writes=[pssx_b])
                  rstd_from(pssx_[:], rsx_[:], 128, [pssx_b], [rsx_b], tmx_[:], tmx_b)
                  S.add("dve", lambda e: e.scalar_tensor_tensor(qx[:], xq[:, hx, sl], vt[:, xqg:xqg + 1], rsx_[:], ALU.mult, ALU.mult),
                        reads=[xqb, rsx_b, vb], writes=[qxb])

              def x_s2(n):
                  hx, tb = xitems[n]
                  qx, qxb = qxn[n % 3]
                  for mt in range(2):
                      ps_, psb_ = pS[mt]
                      pt, ptb = pT[(2 * n + mt) % 4]
                      S.add("pe", lambda e, ps_=ps_, mt=mt: e.matmul(ps_[:], kxT[:, hx, mt * 128:(mt + 1) * 128], qx[:], start=True, stop=True),
                            reads=[kxTb, qxb], writes=[psb_])
                      S.add("act", lambda e, pt=pt, ps_=ps_: e.activation(out=pt[:], in_=ps_[:], func=AF.Exp, scale=128 ** -0.5, bias=-6.0),
                            reads=[psb_], writes=[ptb])

              def x_s3(n):
                  hx, tb = xitems[n]
                  sl = slice(tb * 512, (tb + 1) * 512)
                  po, pob = pO2[n % 2]
                  pd, pdb = pD2[n % 2]
                  o, obf = xo[n % 2]
                  rd, rdb = rd2[n % 2]
                  rt, rtb = rt2[n % 2]
                  for mt in range(2):
                      pt, ptb = pT[(2 * n + mt) % 4]
                      S.add("pe", lambda e, mt=mt, pt=pt: e.matmul(po[:], vx[:, mt, hx * 128:(hx + 1) * 128], pt[:], start=(mt == 0), stop=(mt == 1)),
                            reads=[vxb, ptb], writes=[pob])
                  for mt in range(2):
                      pt, ptb = pT[(2 * n + mt) % 4]
                      S.add("pe", lambda e, mt=mt, pt=pt: e.matmul(pd[:], ones_b[:], pt[:], start=(mt == 0), stop=(mt == 1)),
                            reads=[Bc, ptb], writes=[pdb])
                  S.add("act", lambda e: e.activation(out=rt[:], in_=pd[:], func=AF.Ln), reads=[pdb], writes=[rtb])
                  S.add("act", lambda e: e.activation(out=rd[:], in_=rt[:], func=AF.Exp, scale=-1.0), reads=[rtb], writes=[rdb])
                  S.add("dve", lambda e: e.tensor_tensor(o[:], po[:], rd[:], ALU.mult), reads=[pob, rdb], writes=[obf])
                  S.dma("sp", xaT_d[hx * 128:(hx + 1) * 128, sl], o[:], reads=[obf], writes=[scr_tok["xaT"]])

              NXI = len(xitems)
              for n in range(NXI + 2):
                  if n < NXI:
                      x_s1(n)
                  if 0 <= n - 1 < NXI:
                      x_s2(n - 1)
                  if n - 2 >= 0:
                      x_s3(n - 2)
              for a, b, rows in ((("q_own", "q_all", 192), ("k_own", "k_all", 192), ("v_own", "v_all", T)) if "nocc" not in dbg else ()):
                  for j in range(4):
                      S.cc(lambda e, a=a, b=b, rows=rows, j=j: e.collective_compute(
                          "AllGather", ALU.bypass, replica_groups=RG, ins=[scr[a][j * rows:(j + 1) * rows, :].opt()],
                          outs=[scr[b][j * 4 * rows:(j + 1) * 4 * rows, :].opt()]), reads=[scr_tok[a]], writes=[scr_tok[b]], unit="cc" + a[0])

              stg.done()

              chk("A3%d" % l)
              stg = Stage()
              vt, vb = load_vec(stg, l)
              P16 = [128, 16]

              def v16(name):
                  return vt[:, VC[name]:VC[name] + 16]
              sm = {}
              smb = Buf("ssm_small")

              def small(name, shape=P16, dt=F32):
                  t, _ = stg.sb(shape, dt)
                  sm[name] = t
                  return t

              def sop(fn, eng="dve"):
                  S.add(eng, fn, reads=[smb, vb], writes=[smb])

              for nmz in ["dt", "lrdt", "th", "mag", "qn", "nf", "r", "rc_", "mk", "sin", "cos", "are", "aim", "ere", "den", "t1", "t2", "fre", "fim"]:
                  small(nmz)
              small("ni", P16, I32)
              sop(lambda e: e.activation(out=sm["dt"][:], in_=v16("ldt"), func=AF.Exp), "act")
              sop(lambda e: e.tensor_tensor(sm["lrdt"][:], v16("lre"), sm["dt"][:], ALU.mult))
              sop(lambda e: e.tensor_tensor(sm["th"][:], v16("lim"), sm["dt"][:], ALU.mult))
              sop(lambda e: e.activation(out=sm["mag"][:], in_=sm["lrdt"][:], func=AF.Exp), "act")
              sop(lambda e: e.tensor_scalar(sm["qn"][:], sm["th"][:], 1.0 / TWO_PI, None, ALU.mult))
              sop(lambda e: e.tensor_copy(sm["ni"][:], sm["qn"][:]))
              sop(lambda e: e.tensor_copy(sm["nf"][:], sm["ni"][:]))
              sop(lambda e: e.scalar_tensor_tensor(sm["r"][:], sm["nf"][:], -CW1, sm["th"][:], ALU.mult, ALU.add))
              sop(lambda e: e.scalar_tensor_tensor(sm["r"][:], sm["nf"][:], -CW2, sm["r"][:], ALU.mult, ALU.add))
              sop(lambda e: e.tensor_scalar(sm["r"][:], sm["r"][:], math.pi, -math.pi, ALU.min, ALU.max))
              sop(lambda e: e.activation(out=sm["sin"][:], in_=sm["r"][:], func=AF.Sin), "act")
              sop(lambda e: e.tensor_scalar(sm["rc_"][:], sm["r"][:], math.pi / 2, None, ALU.add))
              sop(lambda e: e.tensor_scalar(sm["mk"][:], sm["rc_"][:], math.pi, -TWO_PI, ALU.is_gt, ALU.mult))
              sop(lambda e: e.tensor_tensor(sm["rc_"][:], sm["rc_"][:], sm["mk"][:], ALU.add))
              sop(lambda e: e.tensor_scalar(sm["rc_"][:], sm["rc_"][:], math.pi, -math.pi, ALU.min, ALU.max))
              sop(lambda e: e.activation(out=sm["cos"][:], in_=sm["rc_"][:], func=AF.Sin), "act")
              sop(lambda e: e.tensor_tensor(sm["are"][:], sm["mag"][:], sm["cos"][:], ALU.mult))
              sop(lambda e: e.tensor_tensor(sm["aim"][:], sm["mag"][:], sm["sin"][:], ALU.mult))
              sop(lambda e: e.tensor_scalar(sm["ere"][:], sm["are"][:], -1.0, None, ALU.add))
              sop(lambda e: e.tensor_tensor(sm["den"][:], v16("lre"), v16("lre"), ALU.mult))
              sop(lambda e: e.tensor_tensor(sm["t1"][:], v16("lim"), v16("lim"), ALU.mult))
              sop(lambda e: e.tensor_tensor(sm["den"][:], sm["den"][:], sm["t1"][:], ALU.add))
              sop(lambda e: e.reciprocal(sm["den"][:], sm["den"][:]))
              sop(lambda e: e.tensor_tensor(sm["t1"][:], sm["ere"][:], v16("lre"), ALU.mult))
              sop(lambda e: e.tensor_tensor(sm["t2"][:], sm["aim"][:], v16("lim"), ALU.mult))
              sop(lambda e: e.tensor_tensor(sm["t1"][:], sm["t1"][:], sm["t2"][:], ALU.add))
              sop(lambda e: e.tensor_tensor(sm["fre"][:], sm["t1"][:], sm["den"][:], ALU.mult))
              sop(lambda e: e.tensor_tensor(sm["t1"][:], sm["aim"][:], v16("lre"), ALU.mult))
              sop(lambda e: e.tensor_tensor(sm["t2"][:], sm["ere"][:], v16("lim"), ALU.mult))
              sop(lambda e: e.tensor_tensor(sm["t1"][:], sm["t1"][:], sm["t2"][:], ALU.subtract))
              sop(lambda e: e.tensor_tensor(sm["fim"][:], sm["t1"][:], sm["den"][:], ALU.mult))
              RS = 8
              NK = T // RS
              TL = NK.bit_length() - 1
              apw = small("apw", [128, RS, 2, 16])
              tA = small("tA")
              tB = small("tB")

              def cmul(ore, oim, xre, xim, yre, yim):
                  sop(lambda e: e.tensor_tensor(tA[:], xre, yre, ALU.mult))
                  sop(lambda e: e.tensor_tensor(tB[:], xim, yim, ALU.mult))
                  sop(lambda e: e.tensor_tensor(ore, tA[:], tB[:], ALU.subtract))
                  sop(lambda e: e.tensor_tensor(tA[:], xre, yim, ALU.mult))
                  sop(lambda e: e.tensor_tensor(tB[:], xim, yre, ALU.mult))
                  sop(lambda e: e.tensor_tensor(oim, tA[:], tB[:], ALU.add))
              sop(lambda e: e.tensor_copy(apw[:, 0, 0, :], sm["are"][:]))
              sop(lambda e: e.tensor_copy(apw[:, 0, 1, :], sm["aim"][:]))
              for j in range(1, RS):
                  cmul(apw[:, j, 0, :], apw[:, j, 1, :], apw[:, j - 1, 0, :], apw[:, j - 1, 1, :], sm["are"][:], sm["aim"][:])
              NLV = TL + 1
              ald = small("ald", [128, NLV, 3, 16])
              sop(lambda e: e.tensor_copy(ald[:, 0, 0, :], apw[:, RS - 1, 0, :]))
              sop(lambda e: e.tensor_copy(ald[:, 0, 1, :], apw[:, RS - 1, 1, :]))
              for d in range(1, NLV):
                  cmul(ald[:, d, 0, :], ald[:, d, 1, :], ald[:, d - 1, 0, :], ald[:, d - 1, 1, :], ald[:, d - 1, 0, :], ald[:, d - 1, 1, :])
              for d in range(NLV):
                  sop(lambda e, d=d: e.tensor_scalar(ald[:, d, 2, :], ald[:, d, 1, :], -1.0, None, ALU.mult))
              bblk, bblkb = stg.sb([128, 2, 16, 32], F32)
              cblk, cblkb = stg.sb([128, 2, 16, 32], F32)
              S.dma("sp", bblk[:], W["bblk"][l].rearrange("p (a b c) -> p a b c", a=2, b=16), writes=[smb])
              S.dma("sp", cblk[:], W["cblk"][l].rearrange("p (a b c) -> p a b c", a=2, b=16), writes=[smb])
              Lf = small("Lf", [128, 2, 2, 16, 32])
              Lb = small("Lb", [128, RS, 2, 16, 32], BF16)
              tL1 = small("tL1", [128, 16, 32])
              tL2 = small("tL2", [128, 16, 32])

              def bc(ap16):
                  return ap16.unsqueeze(2).broadcast_to([128, 16, 32])

              def cmul_b(ore, oim, xre, xim, sre, sim):
                  sop(lambda e: e.tensor_tensor(tL1[:], xre, bc(sre), ALU.mult))
                  sop(lambda e: e.tensor_tensor(tL2[:], xim, bc(sim), ALU.mult))
                  sop(lambda e: e.tensor_tensor(ore, tL1[:], tL2[:], ALU.subtract))
                  sop(lambda e: e.tensor_tensor(tL1[:], xre, bc(sim), ALU.mult))
                  sop(lambda e: e.tensor_tensor(tL2[:], xim, bc(sre), ALU.mult))
                  sop(lambda e: e.tensor_tensor(oim, tL1[:], tL2[:], ALU.add))
              cmul_b(Lf[:, 0, 0], Lf[:, 0, 1], bblk[:, 0], bblk[:, 1], sm["fre"][:], sm["fim"][:])
              sop(lambda e: e.tensor_copy(Lb[:, 0], Lf[:, 0]))
              for tau in range(1, RS):
                  cmul_b(Lf[:, tau % 2, 0], Lf[:, tau % 2, 1], Lf[:, (tau - 1) % 2, 0], Lf[:, (tau - 1) % 2, 1], sm["are"][:], sm["aim"][:])
                  sop(lambda e, tau=tau: e.tensor_copy(Lb[:, tau], Lf[:, tau % 2]))
              Cb = small("Cb", [128, 2, 16, 32], BF16)
              sop(lambda e: e.tensor_copy(Cb[:, 0], cblk[:, 0]))
              sop(lambda e: e.tensor_scalar(Cb[:, 1], cblk[:, 1], -1.0, None, ALU.mult))
              Kt = small("Kt", [128, 4, RS, 128], BF16)
              LT = small("LT", [128, 4, RS, 2, 128], BF16)
              tokK = Buf("tokK")
              tokLT = Buf("tokLT")
              S.add("dve", lambda e: e.memset(Kt[:], 0.0), writes=[tokK])
              pK, pKb = stg.ps([128, 4, 128])
              pKbm = [Buf("pK%d" % m_) for m_ in range(4)]
              pL = [stg.ps([128, 4, 128]) for _ in range(2)]
              pX = stg.ps([128, 4, 128])
              for q in range(4):
                for th in range(RS // 4):
                  for m in range(4):
                      i = 4 * q + m
                      ms = slice(m * 32, (m + 1) * 32)
                      for t4 in range(4):
                          tau = th * 4 + t4
                          S.add("pe", lambda e, tau=tau, t4=t4, i=i, ms=ms: e.matmul(pK[ms, t4, ms], Lb[:, tau, 0, i, :], Cb[:, 0, i, :], start=True, stop=False, tile_position=(0, ms.start)),
                                reads=[smb], writes=[pKbm[m]])
                          S.add("pe", lambda e, tau=tau, t4=t4, i=i, ms=ms: e.matmul(pK[ms, t4, ms], Lb[:, tau, 1, i, :], Cb[:, 1, i, :], start=False, stop=True, tile_position=(0, ms.start)),
                                reads=[smb], writes=[pKbm[m]])
                          for ri in range(2):
                              S.add("pe", lambda e, tau=tau, t4=t4, i=i, ms=ms, ri=ri: e.matmul(pL[ri][0][ms, t4, :], Lb[:, tau, ri, i, :], ident_b[:], start=True, stop=True, tile_position=(0, ms.start)),
                                    reads=[smb, Bc], writes=[pL[ri][1]])
                      S.add("act", lambda e, q=q, ms=ms, th=th: e.activation(out=Kt[ms, q, th * 4:(th + 1) * 4, ms], in_=pK[ms, :, ms], func=AF.Copy), reads=[pKbm[m]], writes=[tokK])
                  for ri in range(2):
                      S.add("act", lambda e, q=q, ri=ri, th=th: e.activation(out=LT[:, q, th * 4:(th + 1) * 4, ri, :], in_=pL[ri][0][:], func=AF.Copy), reads=[pL[ri][1]], writes=[tokLT])
              Gf = small("Gf", [128, 2, 16, 32])
              Gb = small("Gb", [128, RS, 2, 16, 32], BF16)
              tG1 = small("tG1", [128, 16, 32])
              tG2 = small("tG2", [128, 16, 32])
              tokG = Buf("tokG")

              def gop(fn):
                  S.add("dve", fn, reads=[smb, tokG], writes=[tokG])
              for j in range(RS):
                  sre, sim = apw[:, j, 0, :], apw[:, j, 1, :]
                  gop(lambda e, sre=sre: e.tensor_tensor(tG1[:], cblk[:, 0], bc(sre), ALU.mult))
                  gop(lambda e, sim=sim: e.tensor_tensor(tG2[:], cblk[:, 1], bc(sim), ALU.mult))
                  gop(lambda e: e.tensor_tensor(Gf[:, 0], tG1[:], tG2[:], ALU.subtract))
                  gop(lambda e, sim=sim: e.tensor_tensor(tG1[:], cblk[:, 0], bc(sim), ALU.mult))
                  gop(lambda e, sre=sre: e.tensor_tensor(tG2[:], cblk[:, 1], bc(sre), ALU.mult))
                  gop(lambda e: e.tensor_tensor(Gf[:, 1], tG1[:], tG2[:], ALU.add))
                  gop(lambda e, j=j: e.tensor_copy(Gb[:, j, 0], Gf[:, 0]))
                  gop(lambda e, j=j: e.tensor_scalar(Gb[:, j, 1], Gf[:, 1], -1.0, None, ALU.mult))

              dcol = VC["ssmd"]
              for q in range(4):
                  S.add("dve", lambda e, q=q: e.scalar_tensor_tensor(Kt[:, q, 0, :], ident_b[:], vt[:, dcol + q:dcol + q + 1], Kt[:, q, 0, :], ALU.mult, ALU.add),
                        reads=[tokK, vb, Bc], writes=[tokK])

              uT, uTb = stg.sb([128, 4, T], BF16)
              S.dma("sp", uT[:], kview(uT_d), reads=[scr_tok["uT"]], writes=[uTb])
              def fqk(e):
                  r = rank_of(e)
                  return e.dma_start(out=qk_mine.rearrange("(s j) t -> s j t", s=2),
                                     in_=qk_all.rearrange("(s j) t -> s j t", s=2)[:, bass.ds(r * 768, 768), :])
              S.dma_fn("sp", fqk, reads=[scr_tok["q_all"], scr_tok["k_all"]], writes=[scr_tok["q_mine"], scr_tok["k_mine"]])

              def fv(e):
                  r = rank_of(e)
                  return e.dma_start(out=v_mine, in_=v_all[bass.ds(r * 4 * T, 4 * T), :])
              S.dma_fn("pool", fv, reads=[scr_tok["v_all"]], writes=[scr_tok["v_mine"]])
              Vst = [stg.sb([128, 2, NK + 8], F32) for _ in range(2)]
              Eo, Eob = stg.sb([128, 2, 16], F32)
              pV = pL + [pX]
              Ssb, Ssbb = stg.sb([128, 16, 2, NK + 8], BF16)
              ra = [stg.sb([128, 2, 256], F32) for _ in range(2)]

              def compute_V(i, dst, dstb):
                  q, m = i // 4, i % 4
                  ms = slice(m * 32, (m + 1) * 32)
                  for ri in range(2):
                      p, pb = pV[(i * 2 + ri) % 3]
                      for j in range(RS):
                          S.add("pe", lambda e, p=p, j=j, ri=ri, q=q, ms=ms, m=m: e.matmul(
                              p[:].rearrange("p a b -> p (a b)")[:, 0:NK], LT[ms, q, RS - 1 - j, ri, :], uT[ms, q, j:T:RS],
                              start=(j == 0), stop=(j == RS - 1), tile_position=(m * 32, 0)), reads=[tokLT, uTb], writes=[pb])
                      S.add("act", lambda e, p=p, ri=ri, dst=dst: e.activation(out=dst[:, ri, 1:1 + NK], in_=p[:].rearrange("p a b -> p (a b)")[:, 0:NK], func=AF.Copy), reads=[pb], writes=[dstb])

              def interleave(lists):
                  n = max(len(x) for x in lists)
                  for k in range(n):
                      for x in lists:
                          if k < len(x):
                              eng, fn, rd_, wr_ = x[k]
                              S.add(eng, fn, reads=rd_, writes=wr_)

              def tree_ops(i, vs, vsb, ra_):
                  ops = []
                  src, srcb = vs, vsb
                  n = NK
                  off = 1
                  for d in range(TL):
                      n2 = n // 2
                      dstt, dsttb = ra_[d % 2]
                      ar = ald[:, d, 0, i:i + 1]
                      ai = ald[:, d, 1, i:i + 1]
                      nai = ald[:, d, 2, i:i + 1]
                      ev = lambda ri, src=src, off=off, n=n: src[:, ri, off:off + n:2]
                      od = lambda ri, src=src, off=off, n=n: src[:, ri, off + 1:off + n:2]
                      out_re = dstt[:, 0, 0:n2] if d < TL - 1 else Eo[:, 0, i:i + 1]
                      out_im = dstt[:, 1, 0:n2] if d < TL - 1 else Eo[:, 1, i:i + 1]
                      wtok = dsttb if d < TL - 1 else Eob
                      ops.append(("dve", lambda e, o=out_re, a=ev(0), b=od(0), ar=ar: e.scalar_tensor_tensor(o, a, ar, b, ALU.mult, ALU.add), [srcb, smb], [wtok]))
                      ops.append(("dve", lambda e, o=out_im, a=ev(0), b=od(1), ai=ai: e.scalar_tensor_tensor(o, a, ai, b, ALU.mult, ALU.add), [srcb, smb, wtok], [wtok]))
                      ops.append(("dve", lambda e, o=out_re, a=ev(1), nai=nai: e.scalar_tensor_tensor(o, a, nai, o, ALU.mult, ALU.add), [srcb, smb, wtok], [wtok]))
                      ops.append(("dve", lambda e, o=out_im, a=ev(1), ar=ar: e.scalar_tensor_tensor(o, a, ar, o, ALU.mult, ALU.add), [srcb, smb, wtok], [wtok]))
                      src, srcb = dstt, dsttb
                      n = n2
                      off = 0
                  return ops

              ra4 = [ra, [stg.sb([128, 2, 256], F32) for _ in range(2)]]
              for ip_ in range(8):
                  lists = []
                  for t_ in range(2):
                      i = 2 * ip_ + t_
                      vs, vsb = Vst[t_]
                      compute_V(i, vs, vsb)
                      lists.append(tree_ops(i, vs, vsb, ra4[t_]))
                  interleave(lists)
              S.dma("sp", e_own, Eo[:].rearrange("p a b -> p (a b)"), reads=[Eob], writes=[scr_tok["e_own"]])
              S.cc(lambda e: e.collective_compute("AllGather", ALU.bypass, replica_groups=RG, ins=[e_own.opt()], outs=[e_all.opt()]),
                   reads=[scr_tok["e_own"]], writes=[scr_tok["e_all"]])
              Ea, Eab = stg.sb([128, 4, 2, 16], F32)
              Fa, Fab = stg.sb([128, 4, 2, 16], F32)
              S.dma("sp", Ea[:], e_all.rearrange("(r p) (a b) -> p r a b", p=128, a=2), reads=[scr_tok["e_all"]], writes=[Eab])
              S.add("dve", lambda e: e.memset(Fa[:], 0.0), writes=[Fab])
              A9r, A9i = ald[:, NLV - 1, 0, :], ald[:, NLV - 1, 1, :]
              for rnk in range(1, 4):
                  S.add("dve", lambda e, rnk=rnk: e.tensor_tensor(tA[:], Fa[:, rnk - 1, 0, :], A9r, ALU.mult), reads=[Fab, smb], writes=[smb])
                  S.add("dve", lambda e, rnk=rnk: e.tensor_tensor(tB[:], Fa[:, rnk - 1, 1, :], A9i, ALU.mult), reads=[Fab, smb], writes=[smb])
                  S.add("dve", lambda e: e.tensor_tensor(tA[:], tA[:], tB[:], ALU.subtract), reads=[smb], writes=[smb])
                  S.add("dve", lambda e, rnk=rnk: e.tensor_tensor(Fa[:, rnk, 0, :], tA[:], Ea[:, rnk - 1, 0, :], ALU.add), reads=[smb, Eab], writes=[Fab])
                  S.add("dve", lambda e, rnk=rnk: e.tensor_tensor(tA[:], Fa[:, rnk - 1, 0, :], A9i, ALU.mult), reads=[Fab, smb], writes=[smb])
                  S.add("dve", lambda e, rnk=rnk: e.tensor_tensor(tB[:], Fa[:, rnk - 1, 1, :], A9r, ALU.mult), reads=[Fab, smb], writes=[smb])
                  S.add("dve", lambda e: e.tensor_tensor(tA[:], tA[:], tB[:], ALU.add), reads=[smb], writes=[smb])
                  S.add("dve", lambda e, rnk=rnk: e.tensor_tensor(Fa[:, rnk, 1, :], tA[:], Ea[:, rnk - 1, 1, :], ALU.add), reads=[smb, Eab], writes=[Fab])
              S.dma("sp", f_scr.rearrange("(r p) (a b) -> p r a b", p=128, a=2), Fa[:], reads=[Fab], writes=[scr_tok["f_scr"]])
              Fo, Fob = stg.sb([128, 2, 16], F32)

              def ff(e):
                  r = rank_of(e)
                  return e.dma_start(out=Fo[:].rearrange("p a b -> p (a b)"), in_=f_scr[bass.ds(r * 128, 128), :])
              S.dma_fn("pool", ff, reads=[scr_tok["f_scr"]], writes=[Fob])

              NX = NK + 1
              pp4 = [[stg.sb([128, 2, NK + 8], F32) for _ in range(2)] for _ in range(2)]

              def scan_ops(i, vs, vsb, pp):
                  ops = []
                  src, srcbs = vs, [vsb]
                  for d in range(NLV):
                      s_ = 1 << d
                      dstt, dsttb = pp[d % 2]
                      dpre = pp_pre[id(dsttb)]
                      ar = ald[:, d, 0, i:i + 1]
                      ai = ald[:, d, 1, i:i + 1]
                      nai = ald[:, d, 2, i:i + 1]
                      lo = lambda ri, src=src, s_=s_: src[:, ri, 0:NX - s_]
                      hi = lambda ri, src=src, s_=s_: src[:, ri, s_:NX]
                      o_re = dstt[:, 0, s_:NX]
                      o_im = dstt[:, 1, s_:NX]
                      ops.append(("pool", lambda e, dstt=dstt, src=src, s_=s_: e.tensor_copy(dstt[:, :, 0:s_], src[:, :, 0:s_]), list(srcbs), [dpre]))
                      ops.append(("dve", lambda e, o=o_re, a=lo(0), b=hi(0), ar=ar: e.scalar_tensor_tensor(o, a, ar, b, ALU.mult, ALU.add), srcbs + [smb], [dsttb]))
                      ops.append(("dve", lambda e, o=o_im, a=lo(0), b=hi(1), ai=ai: e.scalar_tensor_tensor(o, a, ai, b, ALU.mult, ALU.add), srcbs + [smb, dsttb], [dsttb]))
                      ops.append(("dve", lambda e, o=o_re, a=lo(1), nai=nai: e.scalar_tensor_tensor(o, a, nai, o, ALU.mult, ALU.add), srcbs + [smb, dsttb], [dsttb]))
                      ops.append(("dve", lambda e, o=o_im, a=lo(1), ar=ar: e.scalar_tensor_tensor(o, a, ar, o, ALU.mult, ALU.add), srcbs + [smb, dsttb], [dsttb]))
                      src, srcbs = dstt, [dsttb, dpre]
                  ops.append(("act", lambda e, i=i, src=src: e.activation(out=Ssb[:, i, :, 0:NX], in_=src[:, :, 0:NX], func=AF.Copy), list(srcbs), [Ssbb]))
                  return ops

              pp_pre = {}
              for pl in pp4:
                  for (_t, _b) in pl:
                      pp_pre[id(_b)] = Buf("pre")

              yg, ygb = stg.sb([128, 4, T], BF16)
              pY = [stg.ps() for _ in range(2)]
              glups = [stg.ps() for _ in range(2)]
              iy = [0]
              def emit_out(q):
                  for j in range(RS):
                      p, pb = pY[iy[0] % 2]
                      iy[0] += 1
                      first = True
                      for tau in range(j + 1):
                          S.add("pe", lambda e, p=p, q=q, tau=tau, j=j, first=first: e.matmul(p[:, 0:NK], Kt[:, q, tau, :], uT[:, q, (j - tau):T:RS], start=first, stop=False),
                                reads=[tokK, uTb], writes=[pb])
                          first = False
                      for m in range(4):
                          i = 4 * q + m
                          ms = slice(m * 32, (m + 1) * 32)
                          for ri in range(2):
                              lastmm = (m == 3 and ri == 1)
                              S.add("pe", lambda e, p=p, ms=ms, j=j, ri=ri, i=i, lastmm=lastmm: e.matmul(p[ms, 0:NK], Gb[:, j, ri, i, :], Ssb[:, i, ri, 0:NK], start=False, stop=lastmm, tile_position=(0, ms.start)),
                                    reads=[tokG, Ssbb], writes=[pb])
                      S.add("act", lambda e, p=p, q=q, j=j: e.activation(out=yg[:, q, j:T:RS], in_=p[:, 0:NK], func=AF.Gelu_apprx_tanh), reads=[pb], writes=[ygb])
              for ip_ in range(8):
                  lists = []
                  for t_ in range(2):
                      i = 2 * ip_ + t_
                      vs, vsb = Vst[t_]
                      compute_V(i, vs, vsb)
                      for ri in range(2):
                          S.add("pool", lambda e, vs=vs, ri=ri, i=i: e.tensor_copy(vs[:, ri, 0:1], Fo[:, ri, i:i + 1]), reads=[Fob], writes=[vsb])
                  if ip_ >= 2 and ip_ % 2 == 0:
                      emit_out((ip_ - 2) // 2)
                  for t_ in range(2):
                      i = 2 * ip_ + t_
                      vs, vsb = Vst[t_]
                      lists.append(scan_ops(i, vs, vsb, pp4[t_]))
                  interleave(lists)
              emit_out(3)

              gs, gsb = stg.sb([128, 4, T], BF16)
              sg = [stg.sb([128, 512], BF16) for _ in range(2)]
              gi = [0]
              bgl = VC["bglu"]

              def cons_glu(oc, m, tb, ps, psb):
                  s_, s_b = sg[gi[0] % 2]
                  gi[0] += 1
                  S.add("act", lambda e: e.activation(out=s_[:], in_=ps[:], func=AF.Sigmoid, bias=vt[:, bgl + oc:bgl + oc + 1]), reads=[psb, vb], writes=[s_b])
                  gtok = Buf("gs_%d_%d" % (oc, tb))
                  S.add("dve", lambda e: e.tensor_tensor(gs[:, oc, tb * 512:(tb + 1) * 512], yg[:, oc, tb * 512:(tb + 1) * 512], s_[:], ALU.mult),
                        reads=[s_b, ygb], writes=[gtok])
                  S.dma("sp", gsT_d[oc * 128:(oc + 1) * 128, tb * 512:(tb + 1) * 512], gs[:, oc, tb * 512:(tb + 1) * 512], reads=[gtok], writes=[scr_tok["gsT"]])
              linear_T(stg, yg, ygb, 4, kview(W["wglu"][l]), 0, 512, cons_glu, psums=glups)
              stg.done()

              chk("SSM%d" % l)
              stg = Stage()
              QT, QTb = stg.sb([96, 2, 4 * T], BF16)
              KT, KTb = stg.sb([96, 2, 4 * T], BF16)
              VA, VAb = stg.sb([128, 2, 64, 128], BF16)
              mskf, mskfb = stg.sb([128, 4, 512], F32)
              msk, mskb = stg.sb([128, 4, 512], BF16)
              S.dma("sp", mskf[:], masks_in, writes=[mskfb])
              S.add("dve", lambda e: e.tensor_copy(msk[:], mskf[:]), reads=[mskfb], writes=[mskb])
              QTbs = [Buf("QT%d" % i_) for i_ in range(4)]
              KTbs = [Buf("KT%d" % i_) for i_ in range(4)]
              VAbs = [Buf("VA%d" % i_) for i_ in range(4)]
              S.add("pool", lambda e: e.memset(VA[:], 1.0), writes=VAbs)
              for i in range(4):
                  for hh in range(2):
                      for (mine, dst, dstb, mname) in ((q_mine, QT, QTbs[i], "q_mine"), (k_mine, KT, KTbs[i], "k_mine")):
                          S.dma("sp", dst[0:96, hh, i * T:(i + 1) * T], mine[i * 192 + hh * 96:i * 192 + (hh + 1) * 96, :],
                                reads=[scr_tok[mname]], writes=[dstb])
                      for tq in range(4):
                          srcv = v_mine[i * T + tq * 512:i * T + (tq + 1) * 512, hh * 64:(hh + 1) * 64].rearrange("(t p) c -> p t c", p=128)
                          S.dma("sp", VA[:, hh, i * 16 + tq * 4:i * 16 + (tq + 1) * 4, 0:64], srcv, reads=[scr_tok["v_mine"]], writes=[VAbs[i]])
              if "qdbg" in dbg:
                  S.dma("sp", qdbg[0:96, :], QT[0:96, 0, :], reads=[QTb], writes=[scr_tok["qdbg"]])
                  S.dma("sp", qdbg[96:192, :], KT[0:96, 0, :], reads=[KTb], writes=[scr_tok["qdbg"]])
              pSa = [stg.ps([128, 1024]) for _ in range(3)]
              pOa = [stg.ps() for _ in range(2)]
              pTa = [stg.sb([128, 1024], BF16) for _ in range(4)]
              rden = [stg.sb([128, 512], F32) for _ in range(2)]
              ao = [stg.sb([128, 512], BF16) for _ in range(2)]
              tiles = []
              for qb in range(16):
                  for hh in range(2):
                      nkt = 4 * qb + 4
                      for kt in range(0, nkt, 2):
                          tiles.append((qb, hh, kt, nkt))
              LA = 2
              a_tok = [Buf("a_send%d" % j) for j in range(4)]
              msk2 = msk[:].rearrange("p a b -> p (a b)")

              def emit_qk(n):
                  qb, hh, kt, nkt = tiles[n]
                  qsl = slice(qb * 512, (qb + 1) * 512)
                  ps_, psb_ = pSa[n % 3]
                  pt, ptb = pTa[n % 4]
                  for u in range(2):
                      S.add("pe", lambda e, u=u: e.matmul(ps_[:, u * 512:(u + 1) * 512], KT[0:96, hh, (kt + u) * 128:(kt + u + 1) * 128], QT[0:96, hh, qsl], start=True, stop=True),
                            reads=[KTbs[(kt * 128) // T], QTbs[qb // 4]], writes=[psb_])
                  S.add("act", lambda e: e.activation(out=pt[:], in_=ps_[:], func=AF.Exp, scale=96 ** -0.5, bias=-6.0),
                        reads=[psb_], writes=[ptb])
                  if kt >= 4 * qb:
                      j = kt - 4 * qb
                      S.add("dve", lambda e: e.tensor_tensor(pt[:], pt[:], msk2[:, j * 512:(j + 2) * 512], ALU.mult), reads=[ptb, mskb], writes=[ptb])

              def emit_pv(n):
                  qb, hh, kt, nkt = tiles[n]
                  pt, ptb = pTa[n % 4]
                  po, pob = pOa[(qb * 2 + hh) % 2]
                  rdn, rdnb = rden[(qb * 2 + hh) % 2]
                  o, obf = ao[qb % 2]
                  for u in range(2):
                      S.add("pe", lambda e, u=u: e.matmul(po[:], VA[:, hh, kt + u, :], pt[:, u * 512:(u + 1) * 512], start=(kt + u == 0), stop=(kt + u == nkt - 1)),
                            reads=[VAbs[kt // 16], ptb], writes=[pob])
                  if kt + 2 == nkt:
                      S.add("dve", lambda e: e.reciprocal(rdn[64:128, :], po[64:128, :]), reads=[pob], writes=[rdnb])
                      S.add("dve", lambda e: e.tensor_tensor(o[hh * 64:(hh + 1) * 64, :], po[0:64, :], rdn[64:128, :], ALU.mult),
                            reads=[pob, rdnb], writes=[obf])
                      if hh == 1:
                          j = qb // 4
                          S.dma("sp", a_send[j * 128:(j + 1) * 128, (qb % 4) * 512:(qb % 4 + 1) * 512], o[:], reads=[obf], writes=[a_tok[j]])
                          if qb % 4 == 3:
                              S.cc(lambda e: e.collective_compute("AllGather", ALU.bypass, replica_groups=RG, ins=[a_send[j * 128:(j + 1) * 128, :].opt()],
                                                                  outs=[a_all[j * 512:(j + 1) * 512, :].opt()]),
                                   reads=[a_tok[j]], writes=[scr_tok["a_all"]], unit="cca")

              for n in range(len(tiles) + LA):
                  if n < len(tiles):
                      emit_qk(n)
                  if n >= LA:
                      emit_pv(n - LA)
              stg.done()

              chk("B1%d" % l)
              stg = Stage()
              vt, vb = load_vec(stg, l)
              hT, _ = stg.sb([128, 8, T], BF16)
              hTbs = [Buf("hTb%d" % i_) for i_ in range(NB)]
              acts = []
              for nm in ("attnT", "gsT", "xaT"):
                  t_, _ = stg.sb([128, 4, T], BF16)
                  acts.append((t_, [Buf("%s%d" % (nm, i_)) for i_ in range(NB)]))
              for i_ in range(NB):
                  csl = slice(i_ * 512, (i_ + 1) * 512)
                  S.dma("sp", hT[:, :, csl], kview(hT_d)[:, :, csl], reads=[scr_tok["hT"]], writes=[hTbs[i_]])
                  for nm, (t_, tbs_) in zip(("attnT", "gsT", "xaT"), acts):
                      if nm != "attnT":
                          S.dma("sp", t_[:, :, csl], kview(scr[nm])[:, :, csl], reads=[scr_tok[nm]], writes=[tbs_[i_]])
              def fa(e):
                  r = rank_of(e)
                  return e.dma_start(out=attnT_d, in_=a_all[bass.ds(r * 512, 512), :])
              S.dma_fn("sp", fa, reads=[scr_tok["a_all"]], writes=[scr_tok["attnT"]])
              for i_ in range(NB):
                  csl = slice(i_ * 512, (i_ + 1) * 512)
                  S.dma("sp", acts[0][0][:, :, csl], kview(scr["attnT"])[:, :, csl], reads=[scr_tok["attnT"]], writes=[acts[0][1][i_]])
              mg, _ = stg.sb([128, 8, T], BF16)
              mgbs = [[Buf("mg%d_%d" % (o_, i_)) for i_ in range(NB)] for o_ in range(8)]
              wo = [[stg.sb([128, 4, 128], BF16) for _ in range(3)] for _ in range(2)]
              wg = [[stg.sb([128, 8, 128], BF16) for _ in range(3)] for _ in range(2)]
              wov = [kview(W[n][l]) for n in ("womla", "wossm", "wocross")]
              pyy = [stg.ps() for _ in range(3)]
              pgg = [stg.ps() for _ in range(3)]
              sgt = [stg.sb([128, 512], F32) for _ in range(3)]
              tmpm = [stg.sb([128, 512], F32) for _ in range(4)]
              bg = VC["bgate"]
              imc = [0]

              def load_w(oc, brs):
                  for br in brs:
                      S.dma("pool", wo[oc % 2][br][0][:], wov[br][:, :, oc * 128:(oc + 1) * 128], writes=[wo[oc % 2][br][1]])
                      S.dma("pool", wg[oc % 2][br][0][:], win_v[:, :, 1696 + br * 1024 + oc * 128:1696 + br * 1024 + (oc + 1) * 128], writes=[wg[oc % 2][br][1]])

              def grp(oc, tb, br):
                  sl = slice(tb * 512, (tb + 1) * 512)
                  im = imc[0]
                  imc[0] += 1
                  py, pyb = pyy[im % 3]
                  pg, pgb = pgg[im % 3]
                  sgx, sgxb = sgt[im % 3]
                  wot, wotb = wo[oc % 2][br]
                  wgt, wgtb = wg[oc % 2][br]
                  at, atbs = acts[br]
                  for k in range(4):
                      S.add("pe", lambda e, k=k: e.matmul(py[:], wot[:, k, :], at[:, k, sl], start=(k == 0), stop=(k == 3)),
                            reads=[wotb, atbs[tb]], writes=[pyb])
                  for k in range(8):
                      S.add("pe", lambda e, k=k: e.matmul(pg[:], wgt[:, k, :], hT[:, k, sl], start=(k == 0), stop=(k == 7)),
                            reads=[wgtb, hTbs[tb]], writes=[pgb])
                  S.add("act", lambda e: e.activation(out=sgx[:], in_=pg[:], func=AF.Sigmoid, bias=vt[:, bg + br * 8 + oc:bg + br * 8 + oc + 1]),
                        reads=[pgb, vb], writes=[sgxb])
                  return py, pyb, sgx, sgxb

              itm = 0
              for oc in range(8):
                  load_w(oc, (1, 2))
                  for tb in range(NB):
                      sl = slice(tb * 512, (tb + 1) * 512)
                      t1, t1b = tmpm[itm % 4]
                      t2, t2b = tmpm[(itm + 1) % 4]
                      itm += 2
                      py, pyb, sgx, sgxb = grp(oc, tb, 1)
                      S.add("dve", lambda e, t1=t1, py=py, sgx=sgx: e.tensor_tensor(t1[:], py[:], sgx[:], ALU.mult), reads=[pyb, sgxb], writes=[t1b])
                      py, pyb, sgx, sgxb = grp(oc, tb, 2)
                      S.add("dve", lambda e, t2=t2, py=py, sgx=sgx: e.tensor_tensor(t2[:], py[:], sgx[:], ALU.mult), reads=[pyb, sgxb], writes=[t2b])
                      S.add("dve", lambda e, t1=t1, t2=t2, oc=oc, sl=sl: e.tensor_tensor(mg[:, oc, sl], t1[:], t2[:], ALU.add), reads=[t1b, t2b], writes=[mgbs[oc][tb]])
              for oc in range(8):
                  load_w(oc, (0,))
                  for tb in range(NB):
                      sl = slice(tb * 512, (tb + 1) * 512)
                      t1, t1b = tmpm[itm % 4]
                      itm += 1
                      py, pyb, sgx, sgxb = grp(oc, tb, 0)
                      S.add("dve", lambda e, t1=t1, py=py, sgx=sgx: e.tensor_tensor(t1[:], py[:], sgx[:], ALU.mult), reads=[pyb, sgxb], writes=[t1b])
                      S.add("dve", lambda e, t1=t1, oc=oc, sl=sl: e.tensor_tensor(mg[:, oc, sl], mg[:, oc, sl], t1[:], ALU.add), reads=[t1b, mgbs[oc][tb]], writes=[mgbs[oc][tb]])
              mgb = lambda k_, tb_: mgbs[k_][tb_]
              lin2 = [stg.ps() for _ in range(2)]
              xin = [stg.sb([128, 512], F32) for _ in range(6)]
              xo_ = [stg.sb([128, 512], F32) for _ in range(3)]
              ix = [0]
              xcv = kview(x_cur)
              x2v = kview(x2_d)

              NPF = 6

              def load_x(g):
                  oc_, tb_ = g // NB, g % NB
                  xi_, xib_ = xin[g % NPF]
                  S.dma("sp", xi_[:], xcv[:, oc_, tb_ * 512:(tb_ + 1) * 512], reads=[xb_cur], writes=[xib_])
              for g_ in range(NPF):
                  load_x(g_)

              def cons_out(oc, m, tb, ps, psb):
                  g = ix[0]
                  xi, xib = xin[g % NPF]
                  xo2, xo2b = xo_[g % 3]
                  ix[0] += 1
                  sl = slice(tb * 512, (tb + 1) * 512)
                  S.add("dve", lambda e: e.tensor_tensor(xo2[:], ps[:], xi[:], ALU.add), reads=[psb, xib], writes=[xo2b])
                  if g + NPF < 8 * NB:
                      load_x(g + NPF)
                  S.dma("sp", x2v[:, oc, sl], xo2[:], reads=[xo2b], writes=[scr_tok["x2"]])
                  if tb == NB - 1:
                      S.dma("sp", xh_own[oc * 128:(oc + 1) * 128, :], xo2[:, 510:512], reads=[xo2b], writes=[scr_tok["xh_own"]])
              linear_T(stg, mg, mgb, 8, kview(W["wout"][l]), 0, D, cons_out, psums=lin2 + pyy + pgg)
              stg.done()
              chk("B3%d" % l)
              S.cc(lambda e: e.collective_compute("AllGather", ALU.bypass, replica_groups=RG, ins=[xh_own.opt()], outs=[xh_all.opt()]),
                   reads=[scr_tok["xh_own"]], writes=[scr_tok["xh_all"]])

              stg = Stage()
              vt, vb = load_vec(stg, l)
              zt, ztb = stg.sb([128, 8, 2], F32)
              S.add("dve", lambda e: e.memset(zt[:], 0.0), writes=[ztb])
              if l == 0:
                  S.dma("sp", xh_ext[0:D, :].rearrange("(k p) t -> p k t", p=128), zt[:], reads=[ztb], writes=[scr_tok["xh_ext"]])
              S.dma("sp", xh_ext[D:5 * D, :], xh_all, reads=[scr_tok["xh_all"]], writes=[scr_tok["xh_ext"]])
              TH = T + 2
              x2v = kview(x2_d)
              x2, x2b = stg.sb([128, 8, TH], F32)
              h2, h2b = stg.sb([128, 8, TH], BF16)
              x2bs = [Buf("x2h")] + [Buf("x2b%d" % i_) for i_ in range(NB)]
              h2bs = [Buf("h2b%d" % i_) for i_ in range(NB + 1)]
              for i_ in range(NB):
                  S.dma("sp", x2[:, :, 2 + i_ * 512:2 + (i_ + 1) * 512], x2v[:, :, i_ * 512:(i_ + 1) * 512], reads=[scr_tok["x2"]], writes=[x2bs[1 + i_]])

              def fh(e):
                  r = rank_of(e)
                  return e.dma_start(out=x2[:, :, 0:2], in_=xh_ext[bass.ds(r * D, D), :].rearrange("(k p) t -> p k t", p=128))
              S.dma_fn("act", fh, reads=[scr_tok["xh_ext"]], writes=[x2bs[0]])
              sq, sqb = stg.sb([128, 8, 512], BF16)
              pss, pssb = stg.ps()
              rs, rsb = stg.sb([128, 512], F32)
              tm, tmb = stg.sb([128, 512], F32)
              fg = VC["ffng"]
              blocks = [(0, 2)] + [(2 + tb * 512, 2 + (tb + 1) * 512) for tb in range(NB)]
              for bi_, (c0, c1) in (list(enumerate(blocks))[1:] + [(0, blocks[0])]):
                  w_ = c1 - c0
                  x2b = x2bs[bi_]
                  for k in range(8):
                      S.add("act", lambda e, k=k, c0=c0, c1=c1, w_=w_: e.activation(out=sq[:, k, 0:w_], in_=x2[:, k, c0:c1], func=AF.Square), reads=[x2b], writes=[sqb])
                  for k in range(8):
                      S.add("pe", lambda e, k=k, w_=w_: e.matmul(pss[:, 0:w_], ones_b[:], sq[:, k, 0:w_], start=(k == 0), stop=(k == 7)), reads=[sqb, Bc], writes=[pssb])
                  rstd_from(pss[:, 0:w_], rs[:, 0:w_], D, [pssb], [rsb], tm[:, 0:w_], tmb)
                  for k in range(8):
                      S.add("dve", lambda e, k=k, c0=c0, c1=c1, w_=w_: e.scalar_tensor_tensor(h2[:, k, c0:c1], x2[:, k, c0:c1], vt[:, fg + k:fg + k + 1], rs[:, 0:w_], ALU.mult, ALU.mult),
                            reads=[x2b, rsb, vb], writes=[h2bs[bi_]])
                  S.dma("sp", kview(h2T_d)[:, :, c0:c1], h2[:, :, c0:c1], reads=[h2bs[bi_]], writes=[scr_tok["h2T"]])
              stg.done()
              chk("C1%d" % l)
              stg = Stage()
              vt, vb = load_vec(stg, l)
              h2, h2b = stg.sb([128, 8, TH], BF16)
              h2ld = [(0, 514)] + [(2 + tb_ * 512, 2 + (tb_ + 1) * 512) for tb_ in range(1, NB)]
              h2lb = [Buf("h2l%d" % i_) for i_ in range(NB)]
              for i_, (a0, a1) in enumerate(h2ld):
                  S.dma("sp", h2[:, :, a0:a1], kview(h2T_d)[:, :, a0:a1], reads=[scr_tok["h2T"]], writes=[h2lb[i_]])

              def h2tok(c0_):
                  return h2lb[0] if c0_ < 514 else h2lb[(c0_ - 2) // 512]
              wupv = kview(W["wup"][l])
              wu = [stg.sb([128, 8, 128], BF16) for _ in range(4)]
              ups = [[stg.sb([128, TH], F32) for _ in range(2)] for _ in range(2)]
              cvs = [[stg.sb([128, T], F32) for _ in range(2)] for _ in range(2)]
              aos = [stg.sb([128, T], BF16) for _ in range(2)]
              pU = [stg.ps() for _ in range(4)]
              iu = 0
              def load_wu(fc_):
                  for half_ in range(2):
                      wt_, wtb_ = wu[(fc_ * 2 + half_) % 4]
                      col_ = half_ * DFF + fc_ * 128
                      S.dma("pool", wt_[:], wupv[:, :, col_:col_ + 128], writes=[wtb_])
              load_wu(0)
              for fc in range(22):
                  if fc + 1 < 22:
                      load_wu(fc + 1)
                  for half in range(2):
                      wt, wtb = wu[(fc * 2 + half) % 4]
                      up, upb = ups[fc % 2][half]
                      for (c0, c1) in blocks:
                          w_ = c1 - c0
                          p, pb = pU[iu % 4]
                          iu += 1
                          for k in range(8):
                              S.add("pe", lambda e, p=p, wt=wt, k=k, c0=c0, c1=c1, w_=w_: e.matmul(p[:, 0:w_], wt[:, k, :], h2[:, k, c0:c1], start=(k == 0), stop=(k == 7)),
                                    reads=[wtb, h2tok(c0)], writes=[pb])
                          S.add("act", lambda e, p=p, up=up, c0=c0, c1=c1, w_=w_: e.activation(out=up[:, c0:c1], in_=p[:, 0:w_], func=AF.Copy), reads=[pb], writes=[upb])
                      cv, cvb = cvs[fc % 2][half]
                      ch = half * 22 + fc
                      w0, w1, w2, cbb = VC["cw0"] + ch, VC["cw1"] + ch, VC["cw2"] + ch, VC["cb"] + ch
                      S.add("dve", lambda e, cv=cv, up=up, w2=w2, cbb=cbb: e.tensor_scalar(cv[:], up[:, 2:TH], vt[:, w2:w2 + 1], vt[:, cbb:cbb + 1], ALU.mult, ALU.add),
                            reads=[upb, vb], writes=[cvb])
                      S.add("dve", lambda e, cv=cv, up=up, w1=w1: e.scalar_tensor_tensor(cv[:], up[:, 1:TH - 1], vt[:, w1:w1 + 1], cv[:], ALU.mult, ALU.add),
                            reads=[upb, vb, cvb], writes=[cvb])
                      S.add("dve", lambda e, cv=cv, up=up, w0=w0: e.scalar_tensor_tensor(cv[:], up[:, 0:TH - 2], vt[:, w0:w0 + 1], cv[:], ALU.mult, ALU.add),
                            reads=[upb, vb, cvb], writes=[cvb])
                  cg, cgb = cvs[fc % 2][0]
                  cv_, cv_b = cvs[fc % 2][1]
                  ao_, ao_b = aos[fc % 2]
                  S.add("act", lambda e, cg=cg: e.activation(out=cg[:], in_=cg[:], func=AF.Silu), reads=[cgb], writes=[cgb])
                  S.add("dve", lambda e, ao_=ao_, cg=cg, cv_=cv_: e.tensor_tensor(ao_[:], cg[:], cv_[:], ALU.mult), reads=[cgb, cv_b], writes=[ao_b])
                  S.dma("sp", actT_d[fc * 128:(fc + 1) * 128, :], ao_[:], reads=[ao_b], writes=[scr_tok["actT"]])
              stg.done()
              chk("C2%d" % l)
              stg = Stage()
              actT, actTb = stg.sb([128, 22, T], BF16)
              aTbs = [[Buf("aT%d_%d" % (i_, j_)) for j_ in range(NB)] for i_ in range(22)]
              for j_ in range(NB):
                  for i_ in range(22):
                      S.dma("sp", actT[:, i_, j_ * 512:(j_ + 1) * 512], actT_d[i_ * 128:(i_ + 1) * 128, j_ * 512:(j_ + 1) * 512],
                            reads=[scr_tok["actT"]], writes=[aTbs[i_][j_]])
              actTb = lambda k_, tb_: aTbs[k_][tb_]
              xo3 = [stg.sb([128, 512], F32) for _ in range(3)]
              xi3 = [stg.sb([128, 512], F32) for _ in range(3)]
              i3 = [0]
              xnv = kview(x_nxt)

              def cons_down(oc, m, tb, ps, psb):
                  xo2, xo2b = xo3[i3[0] % 3]
                  xi, xib = xi3[i3[0] % 3]
                  i3[0] += 1
                  sl = slice(tb * 512, (tb + 1) * 512)
                  S.dma("sp", xi[:], x2v[:, oc, sl], reads=[scr_tok["x2"]], writes=[xib])
                  S.add("dve", lambda e: e.tensor_tensor(xo2[:], ps[:], xi[:], ALU.add), reads=[psb, xib], writes=[xo2b])
                  S.dma("sp", xnv[:, oc, sl], xo2[:], reads=[xo2b], writes=[xb_nxt])
              linear_T(stg, actT, actTb, 22, W["wdown"][l].rearrange("(k p) n -> p k n", p=128), 0, D, cons_down)
              stg.done()

        except _Stop:
            pass

        S.barrier(final=True)
        S.flush(top, barrier=False)
    return nc


_NC_CACHE = {}


def make_in_maps(inp):
    x = np.asarray(inp["x"], np.float32)
    mem = np.asarray(inp["mem"], np.float32)
    pos = np.asarray(inp["positions"], np.int32)
    ident = np.eye(128, dtype=np.float32)
    masks = np.zeros((128, 4, 512), np.float32)
    kk = np.arange(128)[:, None]
    qq = np.arange(512)[None, :]
    for j in range(4):
        masks[:, j, :] = (qq >= kk + 128 * j).astype(np.float32)
    invf = np.zeros((128, 1), np.float32)
    f = (10000.0 ** (-np.arange(0, 32, 2, dtype=np.float32) / 32)).astype(np.float32)
    invf[0:16, 0] = f
    invf[16:32, 0] = f
    invf[32:48, 0] = -1.0
    invf[48:64, 0] = 1.0
    shared = {"ident": ident, "masks": masks, "invf": invf}
    sw = np.concatenate([np.arange(16, 32), np.arange(0, 16)])
    for l in range(2):
        w_in = np.asarray(inp["w_in"][l], np.float32)
        shared["win%d" % l] = np.ascontiguousarray(np.concatenate([w_in, w_in[:, 640 + sw]], axis=1))
        wq = np.asarray(inp["w_q_b"][l], np.float32)
        qsw_cols = np.concatenate([h * 96 + 64 + sw for h in range(8)])
        shared["wqb%d" % l] = np.ascontiguousarray(np.concatenate([wq, wq[:, qsw_cols]], axis=1))
        wkv = np.asarray(inp["w_kv_b"][l], np.float32).reshape(256, 8, 128)
        shared["wkvk%d" % l] = np.ascontiguousarray(wkv[:, :, :64].reshape(256, 512))
        shared["wkvv%d" % l] = np.ascontiguousarray(wkv[:, :, 64:].reshape(256, 512))
        for nm, key in (("womla", "w_o_mla"), ("wossm", "w_o_ssm"), ("wocross", "w_o_cross"), ("wglu", "w_glu"),
                        ("wmemkv", "w_mem_kv"), ("wout", "w_out"), ("wup", "w_up"), ("wdown", "w_down")):
            shared["%s%d" % (nm, l)] = np.ascontiguousarray(np.asarray(inp[key][l], np.float32))
        shared["vec%d" % l] = pack_vec(inp, l)
        bb, cb = pack_bc(inp, l)
        shared["bblk%d" % l] = bb.reshape(128, -1)
        shared["cblk%d" % l] = cb.reshape(128, -1)
    maps = []
    for c in range(8):
        b, r = c // 4, c % 4
        m = dict(shared)
        m["xT"] = np.ascontiguousarray(x[b, r * T:(r + 1) * T, :].T)
        m["memT"] = np.ascontiguousarray(mem[b].T)
        m["pos"] = np.ascontiguousarray(pos[b, r * T:(r + 1) * T][None, :])
        maps.append(m)
    return maps


def kernel(**inputs):
    if "nc" not in _NC_CACHE:
        _NC_CACHE["nc"] = build()
    nc = _NC_CACHE["nc"]
    maps = make_in_maps(inputs)
    res = run_bass_kernel_spmd(nc, maps, core_ids=list(range(8)))
    out = np.zeros((2, 4 * T, D), np.float32)
    for c in range(8):
        b, r = c // 4, c % 4
        out[b, r * T:(r + 1) * T, :] = np.asarray(res.results[c]["yT"]).T
    return out
```

```python
import math
from contextlib import ExitStack
import numpy as np
import concourse.bass as bass
import concourse.mybir as mybir
from concourse.bass_utils import run_bass_kernel_spmd

F32 = mybir.dt.float32
BF16 = mybir.dt.bfloat16
I32 = mybir.dt.int32
AF = mybir.ActivationFunctionType
ALU = mybir.AluOpType

CE = ["pe", "act", "dve", "pool", "sp"]
NLANES = 64
RG = [[0, 1, 2, 3], [4, 5, 6, 7]]

T = 2048
NB = 4
D = 1024
DFF = 2816
EPS = 1e-6
TWO_PI = 2.0 * math.pi
CW1 = 6.28125
CW2 = TWO_PI - CW1


class Buf:
    __slots__ = ("name", "w", "r")

    def __init__(self, name=""):
        self.name = name
        self.w = None
        self.r = {}


class Sched:
    def __init__(self, nc):
        self.nc = nc
        self.ccu = ["cc", "ccq", "cck", "ccv", "cca"]
        self.units = CE + ["d%d" % i for i in range(NLANES)] + self.ccu
        self.prog = {e: [] for e in CE}
        self.cnt = {u: 0 for u in self.units}
        self.seen = {u: {} for u in self.units}
        self.snaps = {u: [None] for u in self.units}
        self.lane_rr = 0
        self.sems = {}
        self.rank_cache = None

    def _need(self, reads, writes):
        need = {}
        for b in reads:
            if b.w is not None:
                u, c = b.w
                if need.get(u, 0) < c:
                    need[u] = c
        for b in writes:
            if b.w is not None:
                u, c = b.w
                if need.get(u, 0) < c:
                    need[u] = c
            for u, c in b.r.items():
                if need.get(u, 0) < c:
                    need[u] = c
        return need

    def _absorb(self, me, u, c):
        s = self.seen[me]
        if s.get(u, 0) < c:
            s[u] = c
        snap = self.snaps[u][c]
        if snap:
            for k, v in snap.items():
                if s.get(k, 0) < v:
                    s[k] = v

    def _waits(self, eng, need, skip_self=False):
        waits = []
        for u, c in sorted(need.items(), key=lambda kv: -kv[1]):
            if u == eng and skip_self:
                continue
            if self.seen[eng].get(u, 0) >= c:
                continue
            waits.append((u, c))
            self._absorb(eng, u, c)
        return waits

    def add(self, eng, fn, reads=(), writes=()):
        need = self._need(reads, writes)
        waits = self._waits(eng, need, skip_self=(eng == "pe"))
        self.cnt[eng] += 1
        n = self.cnt[eng]
        self.snaps[eng].append(dict(self.seen[eng]))
        self.prog[eng].append((waits, fn, (eng, 1)))
        for b in reads:
            b.r[eng] = n
        for b in writes:
            b.w = (eng, n)
            b.r = {}
        return n

    def dma(self, q, out, in_, reads=(), writes=(), **kw):
        def fn(e, out=out, in_=in_, kw=kw):
            return e.dma_start(out=out, in_=in_, **kw)
        return self.dma_fn(q, fn, reads, writes)

    def dma_fn(self, q, fn, reads=(), writes=()):
        lane = "d%d" % self.lane_rr
        self.lane_rr = (self.lane_rr + 1) % NLANES
        need = self._need(reads, writes)
        prev = self.cnt[lane]
        if prev > 0:
            need[lane] = max(need.get(lane, 0), prev)
        waits = self._waits(q, need)
        self.cnt[lane] += 1
        n = self.cnt[lane]
        self.snaps[lane].append(dict(self.seen[q]))
        self.prog[q].append((waits, fn, (lane, 16)))
        for b in reads:
            b.r[lane] = n
        for b in writes:
            b.w = (lane, n)
            b.r = {}
        return lane, n

    def cc(self, fn, reads=(), writes=(), unit="cc"):
        need = self._need(reads, writes)
        waits = self._waits("pool", need)
        self.cnt[unit] += 1
        n = self.cnt[unit]
        self.snaps[unit].append(dict(self.seen["pool"]))
        self.prog["pool"].append((waits, fn, (unit, 1)))
        for b in reads:
            b.r[unit] = n
        for b in writes:
            b.w = (unit, n)
            b.r = {}
        return n

    def alloc_sems(self, stack):
        for u in self.units:
            self.sems[u] = stack.enter_context(self.nc.semaphore("s_" + u))

    def barrier(self, final=False):
        for eng in CE:
            waits = []
            for u in self.units:
                c = self.cnt[u]
                if u == eng or (u in self.ccu and not final):
                    continue
                if c > 0 and self.seen[eng].get(u, 0) < c:
                    waits.append((u, c))
                    self.seen[eng][u] = c
            if waits:
                self.prog[eng].append((waits, None, None))

    def flush(self, stack, barrier=True):
        if barrier:
            self.barrier()
        nc = self.nc
        block = stack.enter_context(nc.Block())
        mult = {u: (1 if (u in CE or u in self.ccu) else 16) for u in self.units}
        prog = self.prog

        def run(engobj, items):
            self.rank_cache = None
            for waits, fn, inc in items:
                for u, c in waits:
                    engobj.wait_ge(self.sems[u], c * mult[u])
                if fn is not None:
                    ins = fn(engobj)
                    ins.then_inc(self.sems[inc[0]], inc[1])

        @block.tensor
        def _(e):
            run(e, prog["pe"])

        @block.scalar
        def _(e):
            run(e, prog["act"])

        @block.vector
        def _(e):
            run(e, prog["dve"])

        @block.gpsimd
        def _(e):
            run(e, prog["pool"])

        @block.sync
        def _(e):
            run(e, prog["sp"])

        self.prog = {e: [] for e in CE}


VC = {}
_o = 0
for _n, _w in [("mixg", 8), ("qag", 3), ("kvag", 2), ("qg", 1), ("qgsw", 1), ("kg", 1), ("kgsw", 1),
               ("xqg", 1), ("xkg", 1), ("memg", 8), ("bgate", 24), ("ffng", 8), ("cw0", 44), ("cw1", 44),
               ("cw2", 44), ("cb", 44), ("bglu", 4), ("ssmd", 4), ("lre", 16), ("lim", 16), ("ldt", 16)]:
    VC[_n] = _o
    _o += _w
NV = _o


def _chunks(v, n):
    return np.ascontiguousarray(np.asarray(v, np.float32).reshape(n, 128).T)


def pack_vec(inp, l):
    v = np.zeros((128, NV), np.float32)

    def put(name, arr):
        arr = np.asarray(arr, np.float32)
        v[: arr.shape[0], VC[name]: VC[name] + arr.shape[1]] = arr

    put("mixg", _chunks(inp["norm_mix_g"][l], 8))
    put("qag", _chunks(inp["q_a_norm_g"][l], 3))
    put("kvag", _chunks(inp["kv_a_norm_g"][l], 2))
    sw = np.concatenate([np.arange(80, 96), np.arange(64, 80)])
    for nm, g in (("q", inp["q_norm_g"][l]), ("k", inp["k_norm_g"][l])):
        g = np.asarray(g, np.float32)
        put(nm + "g", g[:, None])
        gs = np.zeros((96, 1), np.float32)
        gs[64:96, 0] = g[sw]
        put(nm + "gsw", gs)
    put("xqg", np.asarray(inp["xq_norm_g"][l])[:, None])
    put("xkg", np.asarray(inp["xk_norm_g"][l])[:, None])
    put("memg", _chunks(inp["mem_norm_g"][l], 8))
    put("bgate", _chunks(inp["b_gate"][l], 24))
    put("ffng", _chunks(inp["norm_ffn_g"][l], 8))
    for j in range(3):
        put("cw%d" % j, _chunks(inp["conv_w"][l][j], 44))
    put("cb", _chunks(inp["conv_b"][l], 44))
    put("bglu", _chunks(inp["b_glu"][l], 4))
    put("ssmd", _chunks(np.asarray(inp["ssm_d"][l]).reshape(-1), 4))
    lre = np.asarray(inp["ssm_lambda_re"][l], np.float32).reshape(16, 128).T
    lim = np.asarray(inp["ssm_lambda_im"][l], np.float32).reshape(16, 128).T
    ldt = np.repeat(np.asarray(inp["ssm_log_dt"][l], np.float32)[:, None], 64, 1).reshape(16, 128).T
    put("lre", lre)
    put("lim", lim)
    put("ldt", ldt)
    return v


def pack_bc(inp, l):
    bb = np.zeros((128, 2, 16, 32), np.float32)
    cb = np.zeros((128, 2, 16, 32), np.float32)
    for ri, (bn, cn) in enumerate((("ssm_b_re", "ssm_c_re"), ("ssm_b_im", "ssm_c_im"))):
        B = np.asarray(inp[bn][l], np.float32)
        C = np.asarray(inp[cn][l], np.float32)
        for i in range(16):
            for g2 in range(2):
                g = 2 * i + g2
                bb[g2 * 64:(g2 + 1) * 64, ri, i, g2 * 16:(g2 + 1) * 16] = B[g]
                cb[g2 * 64:(g2 + 1) * 64, ri, i, g2 * 16:(g2 + 1) * 16] = C[g].T
    return bb, cb


class _Stop(Exception):
    pass


def build(dbg=(), stop=None):
    nc = bass.Bass("TRN2", target_bir_lowering=False)
    dbg = set(dbg)

    def chk(name):
        if stop == name:
            raise _Stop()

    def ext_in(name, shape, dt=F32):
        return nc.dram_tensor(name, list(shape), dt, kind="ExternalInput").ap()

    xT_in = ext_in("xT", [D, T])
    memT_in = ext_in("memT", [D, 256])
    pos_in = ext_in("pos", [1, T], I32)
    ident_in = ext_in("ident", [128, 128])
    masks_in = ext_in("masks", [128, 4, 512])
    invf_in = ext_in("invf", [128, 1])
    W = {}
    for nm, shp in [("win", [D, 4800]), ("wqb", [384, 1024]), ("wkvk", [256, 512]), ("wkvv", [256, 512]),
                    ("womla", [512, D]), ("wossm", [512, D]), ("wocross", [512, D]), ("wglu", [512, 512]),
                    ("wmemkv", [D, D]), ("wout", [D, D]), ("wup", [D, 2 * DFF]), ("wdown", [DFF, D]),
                    ("vec", [128, NV]), ("bblk", [128, 2 * 16 * 32]), ("cblk", [128, 2 * 16 * 32])]:
        W[nm] = [ext_in("%s%d" % (nm, l), shp) for l in range(2)]
    yT_out = nc.dram_tensor("yT", [D, T], F32, kind="ExternalOutput").ap()

    scr = {}
    scr_tok = {}

    def scratch(name, shape, dt, internal=False):
        kind = "Internal"
        if name in dbg and not internal:
            kind = "ExternalOutput"
        t = nc.dram_tensor(name, list(shape), dt, kind=kind)
        scr[name] = t.ap()
        scr_tok[name] = Buf(name)
        return t.ap()

    xs = [scratch("xs0", [D, T], F32), scratch("xs1", [D, T], F32)]
    hT_d = scratch("hT", [D, T], BF16)
    cqT_d = scratch("cqT", [384, T], BF16)
    ckvT_d = scratch("ckvT", [256, T], BF16)
    krT_d = scratch("krT", [64, T], F32)
    uT_d = scratch("uT", [512, T], BF16)
    xqT_d = scratch("xqT", [512, T], BF16)
    ropeC_d = scratch("ropeC", [32, T], F32)
    ropeS_d = scratch("ropeS", [32, T], F32)
    q_own = scratch("q_own", [768, T], BF16, True)
    k_own = scratch("k_own", [768, T], BF16, True)
    v_own = scratch("v_own", [4 * T, 128], BF16, True)
    qk_all = scratch("qk_all", [2 * 4 * 768, T], BF16, True)
    q_all = qk_all[0:4 * 768, :]
    k_all = qk_all[4 * 768:8 * 768, :]
    for _n, _ap in (("q_all", q_all), ("k_all", k_all)):
        scr[_n] = _ap
        scr_tok[_n] = Buf(_n)
    v_all = scratch("v_all", [16 * T, 128], BF16, True)
    qk_mine = scratch("qk_mine", [2 * 4 * 192, T], BF16)
    q_mine = qk_mine[0:768, :]
    k_mine = qk_mine[768:1536, :]
    for _n in ("q_mine", "k_mine"):
        scr_tok[_n] = Buf(_n)
    v_mine = scratch("v_mine", [4 * T, 128], BF16)
    a_send = scratch("a_send", [4 * 128, T], BF16, True)
    a_all = scratch("a_all", [16 * 128, T], BF16, True)
    attnT_d = scratch("attnT", [512, T], BF16)
    xaT_d = scratch("xaT", [512, T], BF16)
    gsT_d = scratch("gsT", [512, T], BF16)
    e_own = scratch("e_own", [128, 32], F32, True)
    e_all = scratch("e_all", [4 * 128, 32], F32, True)
    f_scr = scratch("f_scr", [4 * 128, 32], F32)
    xh_own = scratch("xh_own", [D, 2], F32, True)
    xh_all = scratch("xh_all", [4 * D, 2], F32, True)
    xh_ext = scratch("xh_ext", [5 * D, 2], F32)
    x2_d = scratch("x2", [D, T], F32)
    h2T_d = scratch("h2T", [D, T + 2], BF16)
    actT_d = scratch("actT", [DFF, T], BF16)
    qdbg = scratch("qdbg", [192, 4 * T], BF16)
    scratch("dq", [768, T], BF16)
    scratch("dk", [768, T], BF16)
    scratch("dv", [4 * T, 128], BF16)

    def kview(ap, p=128):
        return ap.rearrange("(k p) t -> p k t", p=p)

    with ExitStack() as top:
        S = Sched(nc)
        S.alloc_sems(top)

        def rank_of(e):
            if S.rank_cache is None:
                S.rank_cache = e.partition_id() % 4
            return S.rank_cache

        ident_b = top.enter_context(nc.sbuf_tensor("ident_b", [128, 128], BF16))
        ones_b = top.enter_context(nc.sbuf_tensor("ones_b", [128, 128], BF16))
        Bc = Buf("consts")
        S.dma("pool", ident_b[:], ident_in, writes=[Bc])
        S.add("dve", lambda e: e.memset(ones_b[:], 1.0), writes=[Bc])
        eps_t = top.enter_context(nc.sbuf_tensor("eps_t", [128, 1], F32))
        S.add("dve", lambda e: e.memset(eps_t[:], EPS), writes=[Bc])

        class Stage:
            _ctr = [0]

            def __init__(self):
                self.st = ExitStack()

            def sb(self, shape, dt, name=None):
                Stage._ctr[0] += 1
                t = self.st.enter_context(nc.sbuf_tensor("t%d" % Stage._ctr[0], list(shape), dt))
                return t, Buf()

            def ps(self, shape=(128, 512), dt=F32):
                Stage._ctr[0] += 1
                t = self.st.enter_context(nc.psum_tensor("p%d" % Stage._ctr[0], list(shape), dt))
                return t, Buf()

            def done(self, barrier=True):
                S.flush(self.st, barrier=barrier)
                self.st.close()

        def load_vec(stg, l):
            vt, vb = stg.sb([128, NV], F32)
            S.dma("sp", vt[:], W["vec"][l], writes=[vb])
            return vt, vb

        def rstd_from(ps_ap, out_ap, n, reads, writes, tmp_ap, tmpb):
            np_ = out_ap.shape[0]
            S.add("act", lambda e: e.activation(out=tmp_ap, in_=ps_ap, func=AF.Ln, scale=1.0 / n, bias=eps_t[0:np_, 0:1]), reads=list(reads) + [Bc], writes=[tmpb])
            S.add("act", lambda e: e.activation(out=out_ap, in_=tmp_ap, func=AF.Exp, scale=-0.5), reads=[tmpb], writes=writes)

        def norm_T(stg, src, srcb, kch, ncols, gcol, vt, vb, dst, dstb, nblk=None, bw=512, psum=None):
            nblk = nblk if nblk is not None else ncols // bw
            sq, sqb = stg.sb([128, kch, bw], BF16)
            pss, pssb = psum if psum is not None else stg.ps()
            rs, rsb = stg.sb([128, bw], F32)
            tm, tmb = stg.sb([128, bw], F32)
            for tb in range(nblk):
                sl = slice(tb * bw, (tb + 1) * bw)
                for k in range(kch):
                    S.add("act", lambda e, k=k, sl=sl: e.activation(out=sq[:, k, :], in_=src[:, k, sl], func=AF.Square),
                          reads=[srcb(tb) if callable(srcb) else srcb], writes=[sqb])
                for k in range(kch):
                    S.add("pe", lambda e, k=k: e.matmul(pss[:, 0:bw], ones_b[:], sq[:, k, :], start=(k == 0), stop=(k == kch - 1)),
                          reads=[sqb, Bc], writes=[pssb])
                rstd_from(pss[:, 0:bw], rs[:], 128 * kch, [pssb], [rsb], tm[:], tmb)
                for k in range(kch):
                    S.add("dve", lambda e, k=k, sl=sl: e.scalar_tensor_tensor(
                        dst[:, k, sl], src[:, k, sl], vt[:, gcol + k:gcol + k + 1], rs[:], ALU.mult, ALU.mult),
                        reads=[srcb(tb) if callable(srcb) else srcb, rsb, vb], writes=[dstb(tb) if callable(dstb) else dstb])

        def linear_T(stg, inT, inb, kch, wview, col0, ncols, consumer, nblk=NB, bw=512, wq="pool", psums=None):
            wsp = 2 if kch >= 16 else 1
            wb = [(stg.sb([128, kch, 128], BF16)[0], [Buf() for _ in range(wsp)]) for _ in range(2)]
            kcut = [(i_ * kch) // wsp for i_ in range(wsp + 1)]
            pss = psums if psums is not None else [stg.ps() for _ in range(3)]
            npz = len(pss)
            noc = (ncols + 127) // 128
            it = 0
            for oc in range(noc):
                m = min(128, ncols - oc * 128)
                wt, wtbs = wb[oc % 2]
                for i_ in range(wsp):
                    S.dma(wq, wt[:, kcut[i_]:kcut[i_ + 1], 0:m], wview[:, kcut[i_]:kcut[i_ + 1], col0 + oc * 128: col0 + oc * 128 + m], writes=[wtbs[i_]])
                for tb in range(nblk):
                    ps, psb = pss[it % npz]
                    it += 1
                    for k in range(kch):
                        S.add("pe", lambda e, k=k, ps=ps, wt=wt, m=m, tb=tb: e.matmul(
                            ps[0:m, 0:bw], wt[:, k, 0:m], inT[:, k, tb * bw:(tb + 1) * bw], start=(k == 0), stop=(k == kch - 1)),
                            reads=[wtbs[min(wsp - 1, (k * wsp) // kch)], (inb(k, tb) if callable(inb) else inb)], writes=[psb])
                    consumer(oc, m, tb, ps, psb)

        stg = Stage()
        posi, posib = stg.sb([32, T], I32)
        posf, posfb = stg.sb([32, T], F32)
        invf, invfb = stg.sb([32, 1], F32)
        S.dma("sp", posi[:], pos_in.partition_broadcast(32), writes=[posib])
        S.dma("sp", invf[:], invf_in[0:32, :], writes=[invfb])
        S.add("dve", lambda e: e.tensor_copy(posf[:], posi[:]), reads=[posib], writes=[posfb])
        ang, angb = stg.sb([32, T], F32)
        qn_, qnb = stg.sb([32, T], F32)
        ni, nib = stg.sb([32, T], I32)
        nf, nfb = stg.sb([32, T], F32)
        rr, rrb = stg.sb([32, T], F32)
        mk, mkb = stg.sb([32, T], F32)
        sn, snb = stg.sb([32, T], F32)
        cs, csb = stg.sb([32, T], F32)
        S.add("dve", lambda e: e.tensor_scalar(ang[:], posf[:], invf[:, 0:1], None, ALU.mult), reads=[posfb, invfb], writes=[angb])
        S.add("dve", lambda e: e.tensor_scalar(qn_[:], ang[:], 1.0 / TWO_PI, None, ALU.mult), reads=[angb], writes=[qnb])
        S.add("dve", lambda e: e.tensor_copy(ni[:], qn_[:]), reads=[qnb], writes=[nib])
        S.add("dve", lambda e: e.tensor_copy(nf[:], ni[:]), reads=[nib], writes=[nfb])
        S.add("dve", lambda e: e.scalar_tensor_tensor(rr[:], nf[:], -CW1, ang[:], ALU.mult, ALU.add), reads=[nfb, angb], writes=[rrb])
        S.add("dve", lambda e: e.scalar_tensor_tensor(rr[:], nf[:], -CW2, rr[:], ALU.mult, ALU.add), reads=[nfb, rrb], writes=[rrb])
        S.add("dve", lambda e: e.tensor_scalar(rr[:], rr[:], math.pi, -math.pi, ALU.min, ALU.max), reads=[rrb], writes=[rrb])
        S.add("act", lambda e: e.activation(out=sn[:], in_=rr[:], func=AF.Sin), reads=[rrb], writes=[snb])
        S.add("dve", lambda e: e.tensor_scalar(qn_[:], rr[:], math.pi / 2, None, ALU.add), reads=[rrb], writes=[qnb])
        S.add("dve", lambda e: e.tensor_scalar(mk[:], qn_[:], math.pi, -TWO_PI, ALU.is_gt, ALU.mult), reads=[qnb], writes=[mkb])
        S.add("dve", lambda e: e.tensor_tensor(qn_[:], qn_[:], mk[:], ALU.add), reads=[qnb, mkb], writes=[qnb])
        S.add("dve", lambda e: e.tensor_scalar(qn_[:], qn_[:], math.pi, -math.pi, ALU.min, ALU.max), reads=[qnb], writes=[qnb])
        S.add("act", lambda e: e.activation(out=cs[:], in_=qn_[:], func=AF.Sin), reads=[qnb], writes=[csb])
        sgn, sgnb = stg.sb([32, 1], F32)
        S.dma("sp", sgn[:], invf_in[32:64, :], writes=[sgnb])
        S.add("dve", lambda e: e.tensor_scalar(sn[:], sn[:], sgn[:, 0:1], None, ALU.mult), reads=[snb, sgnb], writes=[snb])
        S.dma("sp", ropeC_d, cs[:], reads=[csb], writes=[scr_tok["ropeC"]])
        S.dma("sp", ropeS_d, sn[:], reads=[snb], writes=[scr_tok["ropeS"]])
        stg.done()

        try:
          for l in (range(2) if stop != "s0" else []):
              x_cur = xs[l % 2] if l > 0 else xT_in
              xb_cur = scr_tok["xs%d" % (l % 2)] if l > 0 else Buf("x_in")
              x_nxt = xs[(l + 1) % 2] if l == 0 else yT_out
              xb_nxt = scr_tok["xs%d" % ((l + 1) % 2)] if l == 0 else Buf("yout")
              win_v = kview(W["win"][l])
              stg = Stage()
              vt, vb = load_vec(stg, l)
              xt, xtb = stg.sb([128, 8, T], F32)
              hT, hTb = stg.sb([128, 8, T], BF16)
              xtbs = [Buf("xt%d" % i_) for i_ in range(NB)]
              hTbs = [Buf("hT%d" % i_) for i_ in range(NB)]
              for i_ in range(NB):
                  S.dma("sp", xt[:, :, i_ * 512:(i_ + 1) * 512], kview(x_cur)[:, :, i_ * 512:(i_ + 1) * 512], reads=[xb_cur], writes=[xtbs[i_]])
              norm_T(stg, xt, lambda tb_: xtbs[tb_], 8, T, VC["mixg"], vt, vb, hT, lambda tb_: hTbs[tb_])
              S.dma("sp", kview(hT_d), hT[:], reads=hTbs, writes=[scr_tok["hT"]])
              hTb = lambda k_, tb_: hTbs[tb_]
              stage_o = [stg.sb([128, 512], BF16) for _ in range(3)]
              stage_f = [stg.sb([128, 512], F32) for _ in range(2)]
              lps = [stg.ps() for _ in range(4)]
              cnt = [0]

              def cons_bf(dst_ap_fn, tokname):
                  def c(oc, m, tb, ps, psb):
                      so, sob = stage_o[cnt[0] % 3]
                      cnt[0] += 1
                      eng = "act" if cnt[0] % 2 else "dve"
                      if eng == "act":
                          S.add("act", lambda e: e.activation(out=so[0:m, :], in_=ps[0:m, :], func=AF.Copy), reads=[psb], writes=[sob])
                      else:
                          S.add("dve", lambda e: e.tensor_copy(so[0:m, :], ps[0:m, :]), reads=[psb], writes=[sob])
                      S.dma("sp", dst_ap_fn(oc, m, tb), so[0:m, :], reads=[sob], writes=[scr_tok[tokname]])
                  return c

              linear_T(stg, hT, hTb, 8, win_v, 0, 384, cons_bf(lambda oc, m, tb: cqT_d[oc * 128:oc * 128 + m, tb * 512:(tb + 1) * 512], "cqT"), psums=lps)
              linear_T(stg, hT, hTb, 8, win_v, 384, 256, cons_bf(lambda oc, m, tb: ckvT_d[oc * 128:oc * 128 + m, tb * 512:(tb + 1) * 512], "ckvT"), psums=lps)
              linear_T(stg, hT, hTb, 8, win_v, 672, 512, cons_bf(lambda oc, m, tb: uT_d[oc * 128:oc * 128 + m, tb * 512:(tb + 1) * 512], "uT"), psums=lps)
              linear_T(stg, hT, hTb, 8, win_v, 1184, 512, cons_bf(lambda oc, m, tb: xqT_d[oc * 128:oc * 128 + m, tb * 512:(tb + 1) * 512], "xqT"), psums=lps)

              def cons_kr(half):
                  def c(oc, m, tb, ps, psb):
                      so, sob = stage_f[cnt[0] % 2]
                      cnt[0] += 1
                      S.add("act", lambda e: e.activation(out=so[0:32, :], in_=ps[0:32, :], func=AF.Copy), reads=[psb], writes=[sob])
                      S.dma("sp", krT_d[half * 32:(half + 1) * 32, tb * 512:(tb + 1) * 512], so[0:32, :], reads=[sob], writes=[scr_tok["krT"]])
                  return c
              linear_T(stg, hT, hTb, 8, win_v, 640, 32, cons_kr(0), psums=lps)
              linear_T(stg, hT, hTb, 8, win_v, 4768, 32, cons_kr(1), psums=lps)
              stg.done()

              chk("A1%d" % l)
              stg = Stage()
              vt, vb = load_vec(stg, l)
              cq, cqb = stg.sb([128, 3, T], BF16)
              ckv, ckvb = stg.sb([128, 2, T], BF16)
              cqn, cqnb = stg.sb([128, 3, T], BF16)
              ckvn, ckvnb = stg.sb([128, 2, T], BF16)
              S.dma("sp", cq[:], kview(cqT_d), reads=[scr_tok["cqT"]], writes=[cqb])
              S.dma("sp", ckv[:], kview(ckvT_d), reads=[scr_tok["ckvT"]], writes=[ckvb])
              npsA2 = stg.ps()
              norm_T(stg, cq, cqb, 3, T, VC["qag"], vt, vb, cqn, cqnb, psum=npsA2)
              norm_T(stg, ckv, ckvb, 2, T, VC["kvag"], vt, vb, ckvn, ckvnb, psum=npsA2)
              rc, rcb = stg.sb([96, T], F32)
              rsn, rsnb = stg.sb([96, T], F32)
              kr, krb = stg.sb([96, T], F32)
              krs, krsb = stg.sb([96, T], F32)
              S.dma("sp", rc[64:96, :], ropeC_d, reads=[scr_tok["ropeC"]], writes=[rcb])
              S.dma("sp", rsn[64:96, :], ropeS_d, reads=[scr_tok["ropeS"]], writes=[rsnb])
              S.dma("sp", kr[64:96, :], krT_d[0:32, :], reads=[scr_tok["krT"]], writes=[krb])
              S.dma("sp", krs[64:96, :], krT_d[32:64, :], reads=[scr_tok["krT"]], writes=[krsb])
              wq, wqb_ = stg.sb([128, 3, 1024], BF16)
              wkk, wkkb = stg.sb([128, 2, 512], BF16)
              wkv, wkvb = stg.sb([128, 2, 512], BF16)
              S.dma("pool", wq[:], kview(W["wqb"][l]), writes=[wqb_])
              S.dma("pool", wkk[:], kview(W["wkvk"][l]), writes=[wkkb])
              S.dma("pool", wkv[:], kview(W["wkvv"][l]), writes=[wkvb])
              krr, krrb = stg.sb([96, T], F32)
              t1, t1b = stg.sb([96, T], F32)
              R = slice(64, 96)
              gk = VC["kg"]
              S.add("dve", lambda e: e.scalar_tensor_tensor(krr[R, :], kr[R, :], vt[R, gk:gk + 1], rc[R, :], ALU.mult, ALU.mult),
                    reads=[krb, rcb, vb], writes=[krrb])
              S.add("dve", lambda e: e.scalar_tensor_tensor(t1[R, :], krs[R, :], vt[R, gk + 1:gk + 2], rsn[R, :], ALU.mult, ALU.mult),
                    reads=[krsb, rsnb, vb], writes=[t1b])
              S.add("dve", lambda e: e.tensor_tensor(krr[R, :], krr[R, :], t1[R, :], ALU.add), reads=[krrb, t1b], writes=[krrb])
              krsq, krsqb = stg.sb([96, T], BF16)
              S.add("act", lambda e: e.activation(out=krsq[R, :], in_=kr[R, :], func=AF.Square), reads=[krb], writes=[krsqb])

              psq = [stg.ps() for _ in range(3)]
              psw = [stg.ps() for _ in range(2)]
              pss2 = [stg.ps() for _ in range(2)]
              sqs = [stg.sb([96, 512], BF16) for _ in range(3)]
              rss = [stg.sb([96, 512], F32) for _ in range(3)]
              tms = [stg.sb([96, 512], F32) for _ in range(2)]
              qf = [stg.sb([96, 512], F32) for _ in range(2)]
              qsf = [stg.sb([96, 512], F32) for _ in range(2)]
              ob = [stg.sb([96, 512], BF16) for _ in range(3)]
              gq = VC["qg"]
              GCq, GCqb = stg.sb([96, T], F32)
              GSq, GSqb = stg.sb([96, T], F32)
              S.add("dve", lambda e: e.tensor_scalar(GCq[R, :], rc[R, :], vt[R, gq:gq + 1], None, ALU.mult), reads=[rcb, vb], writes=[GCqb])
              S.add("dve", lambda e: e.tensor_scalar(GSq[R, :], rsn[R, :], vt[R, gq + 1:gq + 2], None, ALU.mult), reads=[rsnb, vb], writes=[GSqb])
              items = []
              for h in range(8):
                  for tb in range(NB):
                      items.append(("q", h, tb))
                      items.append(("k", h, tb))

              def a2_stage1(n):
                  kind, h, tb = items[n]
                  sl = slice(tb * 512, (tb + 1) * 512)
                  pq, pqb = psq[n % 3]
                  sq, sqb = sqs[n % 3]
                  if kind == "q":
                      pw, pwb = psw[(n // 2) % 2]
                      for k in range(3):
                          S.add("pe", lambda e, k=k: e.matmul(pq[0:96, :], wq[:, k, h * 96:(h + 1) * 96], cqn[:, k, sl],
                                                              start=(k == 0), stop=(k == 2)), reads=[wqb_, cqnb], writes=[pqb])
                      for k in range(3):
                          S.add("pe", lambda e, k=k: e.matmul(pw[64:96, :], wq[:, k, 768 + h * 32:768 + (h + 1) * 32], cqn[:, k, sl],
                                                              start=(k == 0), stop=(k == 2), tile_position=(0, 64)), reads=[wqb_, cqnb], writes=[pwb])
                      S.add("act", lambda e: e.activation(out=sq[0:96, :], in_=pq[0:96, :], func=AF.Square), reads=[pqb], writes=[sqb])
                  else:
                      for k in range(2):
                          S.add("pe", lambda e, k=k: e.matmul(pq[0:64, :], wkk[:, k, h * 64:(h + 1) * 64], ckvn[:, k, sl],
                                                              start=(k == 0), stop=(k == 1)), reads=[wkkb, ckvnb], writes=[pqb])
                      S.add("act", lambda e: e.activation(out=sq[0:64, :], in_=pq[0:64, :], func=AF.Square), reads=[pqb], writes=[sqb])

              def a2_stage2(n):
                  kind, h, tb = items[n]
                  sl = slice(tb * 512, (tb + 1) * 512)
                  sq, sqb = sqs[n % 3]
                  p2, p2b = pss2[n % 2]
                  rs, rsb = rss[n % 3]
                  tm, tmb = tms[n % 2]
                  if kind == "q":
                      S.add("pe", lambda e: e.matmul(p2[0:96, :], ones_b[0:96, 0:96], sq[0:96, :], start=True, stop=True),
                            reads=[sqb, Bc], writes=[p2b])
                  else:
                      S.add("pe", lambda e: e.matmul(p2[0:96, :], ones_b[0:64, 0:96], sq[0:64, :], start=True, stop=False, tile_position=(0, 0)),
                            reads=[sqb, Bc], writes=[p2b])
                      S.add("pe", lambda e: e.matmul(p2[0:96, :], ones_b[64:96, 0:96], krsq[64:96, sl], start=False, stop=True, tile_position=(64, 0)),
                            reads=[krsqb, Bc], writes=[p2b])
                  rstd_from(p2[0:96, :], rs[0:96, :], 96, [p2b], [rsb], tm[0:96, :], tmb)

              def a2_stage3(n):
                  kind, h, tb = items[n]
                  sl = slice(tb * 512, (tb + 1) * 512)
                  pq, pqb = psq[n % 3]
                  rs, rsb = rss[n % 3]
                  o, obf = ob[n % 3]
                  if kind == "q":
                      pw, pwb = psw[(n // 2) % 2]
                      q_f, q_fb = qf[(n // 2) % 2]
                      qs_f, qs_fb = qsf[(n // 2) % 2]
                      S.add("dve", lambda e: e.tensor_tensor(q_f[R, :], pq[R, :], GCq[R, sl], ALU.mult), reads=[pqb, GCqb], writes=[q_fb])
                      S.add("dve", lambda e: e.tensor_tensor(qs_f[R, :], pw[R, :], GSq[R, sl], ALU.mult), reads=[pwb, GSqb], writes=[qs_fb])
                      S.add("dve", lambda e: e.scalar_tensor_tensor(o[0:64, :], pq[0:64, :], vt[0:64, gq:gq + 1], rs[0:64, :], ALU.mult, ALU.mult),
                            reads=[pqb, rsb, vb], writes=[obf])
                      S.add("dve", lambda e: e.tensor_tensor(q_f[R, :], q_f[R, :], qs_f[R, :], ALU.add), reads=[q_fb, qs_fb], writes=[q_fb])
                      S.add("dve", lambda e: e.tensor_tensor(o[R, :], q_f[R, :], rs[R, :], ALU.mult), reads=[q_fb, rsb], writes=[obf])
                      S.dma("sp", q_own[h * 96:(h + 1) * 96, sl], o[0:96, :], reads=[obf], writes=[scr_tok["q_own"]])
                  else:
                      S.add("dve", lambda e: e.scalar_tensor_tensor(o[0:64, :], pq[0:64, :], vt[0:64, gk:gk + 1], rs[0:64, :], ALU.mult, ALU.mult),
                            reads=[pqb, rsb, vb], writes=[obf])
                      S.add("dve", lambda e: e.tensor_tensor(o[R, :], krr[R, sl], rs[R, :], ALU.mult), reads=[krrb, rsb], writes=[obf])
                      S.dma("sp", k_own[h * 96:(h + 1) * 96, sl], o[0:96, :], reads=[obf], writes=[scr_tok["k_own"]])

              NI = len(items)
              for n in range(NI + 2):
                  if n < NI:
                      a2_stage1(n)
                  if 0 <= n - 1 < NI:
                      a2_stage2(n - 1)
                  if n - 2 >= 0:
                      a2_stage3(n - 2)
              pv = psq[0:2]
              vo = [stg.sb([128, 512], BF16) for _ in range(2)]
              for tt in range(16):
                  p, pb = pv[tt % 2]
                  o, obf = vo[tt % 2]
                  for k in range(2):
                      S.add("pe", lambda e, k=k, p=p, tt=tt: e.matmul(p[:, :], ckvn[:, k, tt * 128:(tt + 1) * 128], wkv[:, k, :], start=(k == 0), stop=(k == 1)),
                            reads=[ckvnb, wkvb], writes=[pb])
                  S.add("act", lambda e, o=o, p=p: e.activation(out=o[:], in_=p[:], func=AF.Copy), reads=[pb], writes=[obf])
                  S.dma("sp", v_own.rearrange("(j t) c -> t j c", j=4)[tt * 128:(tt + 1) * 128, :, :], o[:].rearrange("p (j c) -> p j c", j=4),
                        reads=[obf], writes=[scr_tok["v_own"]])
              stg.done()
              if "dq" in dbg and l == 0:
                  for (sname, dname) in (("q_own", "dq"), ("k_own", "dk"), ("v_own", "dv")):
                      S.dma("sp", scr[dname], scr[sname], reads=[scr_tok[sname]], writes=[scr_tok[dname]])
                  with ExitStack() as tmpst:
                      S.flush(tmpst, barrier=True)
              chk("A2%d" % l)
              stg = Stage()
              vt, vb = load_vec(stg, l)
              mt_, mtb = stg.sb([128, 8, 256], F32)
              mn, mnb = stg.sb([128, 8, 256], BF16)
              S.dma("sp", mt_[:], kview(memT_in), writes=[mtb])
              pssx, pssxb = stg.ps()
              norm_T(stg, mt_, mtb, 8, 256, VC["memg"], vt, vb, mn, mnb, nblk=1, bw=256, psum=(pssx, pssxb))
              kxT, kxTb = stg.sb([128, 4, 256], BF16)
              vx, vxb = stg.sb([128, 2, 512], BF16)
              wmv = kview(W["wmemkv"][l])
              sqx, sqxb = stg.sb([128, 512], BF16)
              rsx, rsxb = stg.sb([128, 512], F32)
              tmx, tmxb = stg.sb([128, 512], F32)
              xkg = VC["xkg"]

              def cons_kx(oc, m, tb, ps, psb):
                  S.add("act", lambda e: e.activation(out=sqx[:, 0:256], in_=ps[:, 0:256], func=AF.Square), reads=[psb], writes=[sqxb])
                  S.add("pe", lambda e: e.matmul(pssx[:, 0:256], ones_b[:], sqx[:, 0:256], start=True, stop=True), reads=[sqxb, Bc], writes=[pssxb])
                  rstd_from(pssx[:, 0:256], rsx[:, 0:256], 128, [pssxb], [rsxb], tmx[:, 0:256], tmxb)
                  S.add("dve", lambda e: e.scalar_tensor_tensor(kxT[:, oc, :], ps[:, 0:256], vt[:, xkg:xkg + 1], rsx[:, 0:256], ALU.mult, ALU.mult),
                        reads=[psb, rsxb, vb], writes=[kxTb])
              pvx = [stg.ps() for _ in range(2)]
              linear_T(stg, mn, mnb, 8, wmv, 0, 512, cons_kx, nblk=1, bw=256, psums=pvx)
              wv_, wv_b = stg.sb([128, 8, 512], BF16)
              S.dma("pool", wv_[:], wmv[:, :, 512:1024], writes=[wv_b])
              for mt in range(2):
                  p, pb = pvx[mt]
                  for k in range(8):
                      S.add("pe", lambda e, k=k, p=p, mt=mt: e.matmul(p[:, :], mn[:, k, mt * 128:(mt + 1) * 128], wv_[:, k, :], start=(k == 0), stop=(k == 7)),
                            reads=[mnb, wv_b], writes=[pb])
                  S.add("act", lambda e, p=p, mt=mt: e.activation(out=vx[:, mt, :], in_=p[:], func=AF.Copy), reads=[pb], writes=[vxb])
              xq, xqb = stg.sb([128, 4, T], BF16)
              S.dma("sp", xq[:], kview(xqT_d), reads=[scr_tok["xqT"]], writes=[xqb])
              sqx2 = [(sqx, sqxb), stg.sb([128, 512], BF16)]
              rsx2 = [(rsx, rsxb), stg.sb([128, 512], F32)]
              tmx2 = [(tmx, tmxb), stg.sb([128, 512], F32)]
              pssx2 = [(pssx, pssxb), stg.ps()]
              rd2 = [stg.sb([128, 512], F32) for _ in range(2)]
              qxn = [stg.sb([128, 512], BF16) for _ in range(2)]
              pT = [stg.sb([128, 512], BF16) for _ in range(3)]
              pS = [stg.ps() for _ in range(2)]
              pO = [stg.ps() for _ in range(1)]
              pD = [stg.ps() for _ in range(1)]
              xo = [stg.sb([128, 512], BF16) for _ in range(2)]
              xqg = VC["xqg"]
              it = 0
              ip = 0
              qxn = qxn + [stg.sb([128, 512], BF16)]
              pT = pT + [stg.sb([128, 512], BF16)]
              xitems = [(hx, tb) for hx in range(4) for tb in range(NB)]
              pO2 = [pO[0], pvx[0]]
              pD2 = [pD[0], pvx[1]]
              rt2 = [stg.sb([128, 512], F32) for _ in range(2)]

              def x_s1(n):
                  hx, tb = xitems[n]
                  sl = slice(tb * 512, (tb + 1) * 512)
                  qx, qxb = qxn[n % 3]
                  sqx_, sqx_b = sqx2[n % 2]
                  rsx_, rsx_b = rsx2[n % 2]
                  tmx_, tmx_b = tmx2[n % 2]
                  pssx_, pssx_b = pssx2[n % 2]
                  S.add("act", lambda e: e.activation(out=sqx_[:], in_=xq[:, hx, sl], func=AF.Square), reads=[xqb], writes=[sqx_b])
                  S.add("pe", lambda e: e.matmul(pssx_[:], ones_b[:], sqx_[:], start=True, stop=True), reads=[sqx_b, Bc], writes=[pssx_b])
                  rstd_from(pssx_[:], rsx_[:], 128, [pssx_b], [rsx_b], tmx_[:], tmx_b)
                  S.add("dve", lambda e: e.scalar_tensor_tensor(qx[:], xq[:, hx, sl], vt[:, xqg:xqg + 1], rsx_[:], ALU.mult, ALU.mult),
                        reads=[xqb, rsx_b, vb], writes=[qxb])

              def x_s2(n):
                  hx, tb = xitems[n]
                  qx, qxb = qxn[n % 3]
                  for mt in range(2):
                      ps_, psb_ = pS[mt]
                      pt, ptb = pT[(2 * n + mt) % 4]
                      S.add("pe", lambda e, ps_=ps_, mt=mt: e.matmul(ps_[:], kxT[:, hx, mt * 128:(mt + 1) * 128], qx[:], start=True, stop=True),
                            reads=[kxTb, qxb], writes=[psb_])
                      S.add("act", lambda e, pt=pt, ps_=ps_: e.activation(out=pt[:], in_=ps_[:], func=AF.Exp, scale=128 ** -0.5, bias=-6.0),
                            reads=[psb_], writes=[ptb])

              def x_s3(n):
                  hx, tb = xitems[n]
                  sl = slice(tb * 512, (tb + 1) * 512)
                  po, pob = pO2[n % 2]
                  pd, pdb = pD2[n % 2]
                  o, obf = xo[n % 2]
                  rd, rdb = rd2[n % 2]
                  rt, rtb = rt2[n % 2]
                  for mt in range(2):
                      pt, ptb = pT[(2 * n + mt) % 4]
                      S.add("pe", lambda e, mt=mt, pt=pt: e.matmul(po[:], vx[:, mt, hx * 128:(hx + 1) * 128], pt[:], start=(mt == 0), stop=(mt == 1)),
                            reads=[vxb, ptb], writes=[pob])
                  for mt in range(2):
                      pt, ptb = pT[(2 * n + mt) % 4]
                      S.add("pe", lambda e, mt=mt, pt=pt: e.matmul(pd[:], ones_b[:], pt[:], start=(mt == 0), stop=(mt == 1)),
                            reads=[Bc, ptb], writes=[pdb])
                  S.add("act", lambda e: e.activation(out=rt[:], in_=pd[:], func=AF.Ln), reads=[pdb], writes=[rtb])
                  S.add("act", lambda e: e.activation(out=rd[:], in_=rt[:], func=AF.Exp, scale=-1.0), reads=[rtb], writes=[rdb])
                  S.add("dve", lambda e: e.tensor_tensor(o[:], po[:], rd[:], ALU.mult), reads=[pob, rdb], writes=[obf])
                  S.dma("sp", xaT_d[hx * 128:(hx + 1) * 128, sl], o[:], reads=[obf], writes=[scr_tok["xaT"]])

              NXI = len(xitems)
              for n in range(NXI + 2):
                  if n < NXI:
                      x_s1(n)
                  if 0 <= n - 1 < NXI:
                      x_s2(n - 1)
                  if n - 2 >= 0:
                      x_s3(n - 2)
              for a, b, rows in ((("q_own", "q_all", 192), ("k_own", "k_all", 192), ("v_own", "v_all", T)) if "nocc" not in dbg else ()):
                  for j in range(4):
                      S.cc(lambda e, a=a, b=b, rows=rows, j=j: e.collective_compute(
                          "AllGather", ALU.bypass, replica_groups=RG, ins=[scr[a][j * rows:(j + 1) * rows, :].opt()],
                          outs=[scr[b][j * 4 * rows:(j + 1) * 4 * rows, :].opt()]), reads=[scr_tok[a]], writes=[scr_tok[b]], unit="cc" + a[0])

              stg.done()

              chk("A3%d" % l)
              stg = Stage()
              vt, vb = load_vec(stg, l)
              P16 = [128, 16]

              def v16(name):
                  return vt[:, VC[name]:VC[name] + 16]
              sm = {}
              smb = Buf("ssm_small")

              def small(name, shape=P16, dt=F32):
                  t, _ = stg.sb(shape, dt)
                  sm[name] = t
                  return t

              def sop(fn, eng="dve"):
                  S.add(eng, fn, reads=[smb, vb], writes=[smb])

              for nmz in ["dt", "lrdt", "th", "mag", "qn", "nf", "r", "rc_", "mk", "sin", "cos", "are", "aim", "ere", "den", "t1", "t2", "fre", "fim"]:
                  small(nmz)
              small("ni", P16, I32)
              sop(lambda e: e.activation(out=sm["dt"][:], in_=v16("ldt"), func=AF.Exp), "act")
              sop(lambda e: e.tensor_tensor(sm["lrdt"][:], v16("lre"), sm["dt"][:], ALU.mult))
              sop(lambda e: e.tensor_tensor(sm["th"][:], v16("lim"), sm["dt"][:], ALU.mult))
              sop(lambda e: e.activation(out=sm["mag"][:], in_=sm["lrdt"][:], func=AF.Exp), "act")
              sop(lambda e: e.tensor_scalar(sm["qn"][:], sm["th"][:], 1.0 / TWO_PI, None, ALU.mult))
              sop(lambda e: e.tensor_copy(sm["ni"][:], sm["qn"][:]))
              sop(lambda e: e.tensor_copy(sm["nf"][:], sm["ni"][:]))
              sop(lambda e: e.scalar_tensor_tensor(sm["r"][:], sm["nf"][:], -CW1, sm["th"][:], ALU.mult, ALU.add))
              sop(lambda e: e.scalar_tensor_tensor(sm["r"][:], sm["nf"][:], -CW2, sm["r"][:], ALU.mult, ALU.add))
              sop(lambda e: e.tensor_scalar(sm["r"][:], sm["r"][:], math.pi, -math.pi, ALU.min, ALU.max))
              sop(lambda e: e.activation(out=sm["sin"][:], in_=sm["r"][:], func=AF.Sin), "act")
              sop(lambda e: e.tensor_scalar(sm["rc_"][:], sm["r"][:], math.pi / 2, None, ALU.add))
              sop(lambda e: e.tensor_scalar(sm["mk"][:], sm["rc_"][:], math.pi, -TWO_PI, ALU.is_gt, ALU.mult))
              sop(lambda e: e.tensor_tensor(sm["rc_"][:], sm["rc_"][:], sm["mk"][:], ALU.add))
              sop(lambda e: e.tensor_scalar(sm["rc_"][:], sm["rc_"][:], math.pi, -math.pi, ALU.min, ALU.max))
              sop(lambda e: e.activation(out=sm["cos"][:], in_=sm["rc_"][:], func=AF.Sin), "act")
              sop(lambda e: e.tensor_tensor(sm["are"][:], sm["mag"][:], sm["cos"][:], ALU.mult))
              sop(lambda e: e.tensor_tensor(sm["aim"][:], sm["mag"][:], sm["sin"][:], ALU.mult))
              sop(lambda e: e.tensor_scalar(sm["ere"][:], sm["are"][:], -1.0, None, ALU.add))
              sop(lambda e: e.tensor_tensor(sm["den"][:], v16("lre"), v16("lre"), ALU.mult))
              sop(lambda e: e.tensor_tensor(sm["t1"][:], v16("lim"), v16("lim"), ALU.mult))
              sop(lambda e: e.tensor_tensor(sm["den"][:], sm["den"][:], sm["t1"][:], ALU.add))
              sop(lambda e: e.reciprocal(sm["den"][:], sm["den"][:]))
              sop(lambda e: e.tensor_tensor(sm["t1"][:], sm["ere"][:], v16("lre"), ALU.mult))
              sop(lambda e: e.tensor_tensor(sm["t2"][:], sm["aim"][:], v16("lim"), ALU.mult))
              sop(lambda e: e.tensor_tensor(sm["t1"][:], sm["t1"][:], sm["t2"][:], ALU.add))
              sop(lambda e: e.tensor_tensor(sm["fre"][:], sm["t1"][:], sm["den"][:], ALU.mult))
              sop(lambda e: e.tensor_tensor(sm["t1"][:], sm["aim"][:], v16("lre"), ALU.mult))
              sop(lambda e: e.tensor_tensor(sm["t2"][:], sm["ere"][:], v16("lim"), ALU.mult))
              sop(lambda e: e.tensor_tensor(sm["t1"][:], sm["t1"][:], sm["t2"][:], ALU.subtract))
              sop(lambda e: e.tensor_tensor(sm["fim"][:], sm["t1"][:], sm["den"][:], ALU.mult))
              RS = 8
              NK = T // RS
              TL = NK.bit_length() - 1
              apw = small("apw", [128, RS, 2, 16])
              tA = small("tA")
              tB = small("tB")

              def cmul(ore, oim, xre, xim, yre, yim):
                  sop(lambda e: e.tensor_tensor(tA[:], xre, yre, ALU.mult))
                  sop(lambda e: e.tensor_tensor(tB[:], xim, yim, ALU.mult))
                  sop(lambda e: e.tensor_tensor(ore, tA[:], tB[:], ALU.subtract))
                  sop(lambda e: e.tensor_tensor(tA[:], xre, yim, ALU.mult))
                  sop(lambda e: e.tensor_tensor(tB[:], xim, yre, ALU.mult))
                  sop(lambda e: e.tensor_tensor(oim, tA[:], tB[:], ALU.add))
              sop(lambda e: e.tensor_copy(apw[:, 0, 0, :], sm["are"][:]))
              sop(lambda e: e.tensor_copy(apw[:, 0, 1, :], sm["aim"][:]))
              for j in range(1, RS):
                  cmul(apw[:, j, 0, :], apw[:, j, 1, :], apw[:, j - 1, 0, :], apw[:, j - 1, 1, :], sm["are"][:], sm["aim"][:])
              NLV = TL + 1
              ald = small("ald", [128, NLV, 3, 16])
              sop(lambda e: e.tensor_copy(ald[:, 0, 0, :], apw[:, RS - 1, 0, :]))
              sop(lambda e: e.tensor_copy(ald[:, 0, 1, :], apw[:, RS - 1, 1, :]))
              for d in range(1, NLV):
                  cmul(ald[:, d, 0, :], ald[:, d, 1, :], ald[:, d - 1, 0, :], ald[:, d - 1, 1, :], ald[:, d - 1, 0, :], ald[:, d - 1, 1, :])
              for d in range(NLV):
                  sop(lambda e, d=d: e.tensor_scalar(ald[:, d, 2, :], ald[:, d, 1, :], -1.0, None, ALU.mult))
              bblk, bblkb = stg.sb([128, 2, 16, 32], F32)
              cblk, cblkb = stg.sb([128, 2, 16, 32], F32)
              S.dma("sp", bblk[:], W["bblk"][l].rearrange("p (a b c) -> p a b c", a=2, b=16), writes=[smb])
              S.dma("sp", cblk[:], W["cblk"][l].rearrange("p (a b c) -> p a b c", a=2, b=16), writes=[smb])
              Lf = small("Lf", [128, 2, 2, 16, 32])
              Lb = small("Lb", [128, RS, 2, 16, 32], BF16)
              tL1 = small("tL1", [128, 16, 32])
              tL2 = small("tL2", [128, 16, 32])

              def bc(ap16):
                  return ap16.unsqueeze(2).broadcast_to([128, 16, 32])

              def cmul_b(ore, oim, xre, xim, sre, sim):
                  sop(lambda e: e.tensor_tensor(tL1[:], xre, bc(sre), ALU.mult))
                  sop(lambda e: e.tensor_tensor(tL2[:], xim, bc(sim), ALU.mult))
                  sop(lambda e: e.tensor_tensor(ore, tL1[:], tL2[:], ALU.subtract))
                  sop(lambda e: e.tensor_tensor(tL1[:], xre, bc(sim), ALU.mult))
                  sop(lambda e: e.tensor_tensor(tL2[:], xim, bc(sre), ALU.mult))
                  sop(lambda e: e.tensor_tensor(oim, tL1[:], tL2[:], ALU.add))
              cmul_b(Lf[:, 0, 0], Lf[:, 0, 1], bblk[:, 0], bblk[:, 1], sm["fre"][:], sm["fim"][:])
              sop(lambda e: e.tensor_copy(Lb[:, 0], Lf[:, 0]))
              for tau in range(1, RS):
                  cmul_b(Lf[:, tau % 2, 0], Lf[:, tau % 2, 1], Lf[:, (tau - 1) % 2, 0], Lf[:, (tau - 1) % 2, 1], sm["are"][:], sm["aim"][:])
                  sop(lambda e, tau=tau: e.tensor_copy(Lb[:, tau], Lf[:, tau % 2]))
              Cb = small("Cb", [128, 2, 16, 32], BF16)
              sop(lambda e: e.tensor_copy(Cb[:, 0], cblk[:, 0]))
              sop(lambda e: e.tensor_scalar(Cb[:, 1], cblk[:, 1], -1.0, None, ALU.mult))
              Kt = small("Kt", [128, 4, RS, 128], BF16)
              LT = small("LT", [128, 4, RS, 2, 128], BF16)
              tokK = Buf("tokK")
              tokLT = Buf("tokLT")
              S.add("dve", lambda e: e.memset(Kt[:], 0.0), writes=[tokK])
              pK, pKb = stg.ps([128, 4, 128])
              pKbm = [Buf("pK%d" % m_) for m_ in range(4)]
              pL = [stg.ps([128, 4, 128]) for _ in range(2)]
              pX = stg.ps([128, 4, 128])
              for q in range(4):
                for th in range(RS // 4):
                  for m in range(4):
                      i = 4 * q + m
                      ms = slice(m * 32, (m + 1) * 32)
                      for t4 in range(4):
                          tau = th * 4 + t4
                          S.add("pe", lambda e, tau=tau, t4=t4, i=i, ms=ms: e.matmul(pK[ms, t4, ms], Lb[:, tau, 0, i, :], Cb[:, 0, i, :], start=True, stop=False, tile_position=(0, ms.start)),
                                reads=[smb], writes=[pKbm[m]])
                          S.add("pe", lambda e, tau=tau, t4=t4, i=i, ms=ms: e.matmul(pK[ms, t4, ms], Lb[:, tau, 1, i, :], Cb[:, 1, i, :], start=False, stop=True, tile_position=(0, ms.start)),
                                reads=[smb], writes=[pKbm[m]])
                          for ri in range(2):
                              S.add("pe", lambda e, tau=tau, t4=t4, i=i, ms=ms, ri=ri: e.matmul(pL[ri][0][ms, t4, :], Lb[:, tau, ri, i, :], ident_b[:], start=True, stop=True, tile_position=(0, ms.start)),
                                    reads=[smb, Bc], writes=[pL[ri][1]])
                      S.add("act", lambda e, q=q, ms=ms, th=th: e.activation(out=Kt[ms, q, th * 4:(th + 1) * 4, ms], in_=pK[ms, :, ms], func=AF.Copy), reads=[pKbm[m]], writes=[tokK])
                  for ri in range(2):
                      S.add("act", lambda e, q=q, ri=ri, th=th: e.activation(out=LT[:, q, th * 4:(th + 1) * 4, ri, :], in_=pL[ri][0][:], func=AF.Copy), reads=[pL[ri][1]], writes=[tokLT])
              Gf = small("Gf", [128, 2, 16, 32])
              Gb = small("Gb", [128, RS, 2, 16, 32], BF16)
              tG1 = small("tG1", [128, 16, 32])
              tG2 = small("tG2", [128, 16, 32])
              tokG = Buf("tokG")

              def gop(fn):
                  S.add("dve", fn, reads=[smb, tokG], writes=[tokG])
              for j in range(RS):
                  sre, sim = apw[:, j, 0, :], apw[:, j, 1, :]
                  gop(lambda e, sre=sre: e.tensor_tensor(tG1[:], cblk[:, 0], bc(sre), ALU.mult))
                  gop(lambda e, sim=sim: e.tensor_tensor(tG2[:], cblk[:, 1], bc(sim), ALU.mult))
                  gop(lambda e: e.tensor_tensor(Gf[:, 0], tG1[:], tG2[:], ALU.subtract))
                  gop(lambda e, sim=sim: e.tensor_tensor(tG1[:], cblk[:, 0], bc(sim), ALU.mult))
                  gop(lambda e, sre=sre: e.tensor_tensor(tG2[:], cblk[:, 1], bc(sre), ALU.mult))
                  gop(lambda e: e.tensor_tensor(Gf[:, 1], tG1[:], tG2[:], ALU.add))
                  gop(lambda e, j=j: e.tensor_copy(Gb[:, j, 0], Gf[:, 0]))
                  gop(lambda e, j=j: e.tensor_scalar(Gb[:, j, 1], Gf[:, 1], -1.0, None, ALU.mult))

              dcol = VC["ssmd"]
              for q in range(4):
                  S.add("dve", lambda e, q=q: e.scalar_tensor_tensor(Kt[:, q, 0, :], ident_b[:], vt[:, dcol + q:dcol + q + 1], Kt[:, q, 0, :], ALU.mult, ALU.add),
                        reads=[tokK, vb, Bc], writes=[tokK])

              uT, uTb = stg.sb([128, 4, T], BF16)
              S.dma("sp", uT[:], kview(uT_d), reads=[scr_tok["uT"]], writes=[uTb])
              def fqk(e):
                  r = rank_of(e)
                  return e.dma_start(out=qk_mine.rearrange("(s j) t -> s j t", s=2),
                                     in_=qk_all.rearrange("(s j) t -> s j t", s=2)[:, bass.ds(r * 768, 768), :])
              S.dma_fn("sp", fqk, reads=[scr_tok["q_all"], scr_tok["k_all"]], writes=[scr_tok["q_mine"], scr_tok["k_mine"]])

              def fv(e):
                  r = rank_of(e)
                  return e.dma_start(out=v_mine, in_=v_all[bass.ds(r * 4 * T, 4 * T), :])
              S.dma_fn("pool", fv, reads=[scr_tok["v_all"]], writes=[scr_tok["v_mine"]])
              Vst = [stg.sb([128, 2, NK + 8], F32) for _ in range(2)]
              Eo, Eob = stg.sb([128, 2, 16], F32)
              pV = pL + [pX]
              Ssb, Ssbb = stg.sb([128, 16, 2, NK + 8], BF16)
              ra = [stg.sb([128, 2, 256], F32) for _ in range(2)]

              def compute_V(i, dst, dstb):
                  q, m = i // 4, i % 4
                  ms = slice(m * 32, (m + 1) * 32)
                  for ri in range(2):
                      p, pb = pV[(i * 2 + ri) % 3]
                      for j in range(RS):
                          S.add("pe", lambda e, p=p, j=j, ri=ri, q=q, ms=ms, m=m: e.matmul(
                              p[:].rearrange("p a b -> p (a b)")[:, 0:NK], LT[ms, q, RS - 1 - j, ri, :], uT[ms, q, j:T:RS],
                              start=(j == 0), stop=(j == RS - 1), tile_position=(m * 32, 0)), reads=[tokLT, uTb], writes=[pb])
                      S.add("act", lambda e, p=p, ri=ri, dst=dst: e.activation(out=dst[:, ri, 1:1 + NK], in_=p[:].rearrange("p a b -> p (a b)")[:, 0:NK], func=AF.Copy), reads=[pb], writes=[dstb])

              def interleave(lists):
                  n = max(len(x) for x in lists)
                  for k in range(n):
                      for x in lists:
                          if k < len(x):
                              eng, fn, rd_, wr_ = x[k]
                              S.add(eng, fn, reads=rd_, writes=wr_)

              def tree_ops(i, vs, vsb, ra_):
                  ops = []
                  src, srcb = vs, vsb
                  n = NK
                  off = 1
                  for d in range(TL):
                      n2 = n // 2
                      dstt, dsttb = ra_[d % 2]
                      ar = ald[:, d, 0, i:i + 1]
                      ai = ald[:, d, 1, i:i + 1]
                      nai = ald[:, d, 2, i:i + 1]
                      ev = lambda ri, src=src, off=off, n=n: src[:, ri, off:off + n:2]
                      od = lambda ri, src=src, off=off, n=n: src[:, ri, off + 1:off + n:2]
                      out_re = dstt[:, 0, 0:n2] if d < TL - 1 else Eo[:, 0, i:i + 1]
                      out_im = dstt[:, 1, 0:n2] if d < TL - 1 else Eo[:, 1, i:i + 1]
                      wtok = dsttb if d < TL - 1 else Eob
                      ops.append(("dve", lambda e, o=out_re, a=ev(0), b=od(0), ar=ar: e.scalar_tensor_tensor(o, a, ar, b, ALU.mult, ALU.add), [srcb, smb], [wtok]))
                      ops.append(("dve", lambda e, o=out_im, a=ev(0), b=od(1), ai=ai: e.scalar_tensor_tensor(o, a, ai, b, ALU.mult, ALU.add), [srcb, smb, wtok], [wtok]))
                      ops.append(("dve", lambda e, o=out_re, a=ev(1), nai=nai: e.scalar_tensor_tensor(o, a, nai, o, ALU.mult, ALU.add), [srcb, smb, wtok], [wtok]))
                      ops.append(("dve", lambda e, o=out_im, a=ev(1), ar=ar: e.scalar_tensor_tensor(o, a, ar, o, ALU.mult, ALU.add), [srcb, smb, wtok], [wtok]))
                      src, srcb = dstt, dsttb
                      n = n2
                      off = 0
                  return ops

              ra4 = [ra, [stg.sb([128, 2, 256], F32) for _ in range(2)]]
              for ip_ in range(8):
                  lists = []
                  for t_ in range(2):
                      i = 2 * ip_ + t_
                      vs, vsb = Vst[t_]
                      compute_V(i, vs, vsb)
                      lists.append(tree_ops(i, vs, vsb, ra4[t_]))
                  interleave(lists)
              S.dma("sp", e_own, Eo[:].rearrange("p a b -> p (a b)"), reads=[Eob], writes=[scr_tok["e_own"]])
              S.cc(lambda e: e.collective_compute("AllGather", ALU.bypass, replica_groups=RG, ins=[e_own.opt()], outs=[e_all.opt()]),
                   reads=[scr_tok["e_own"]], writes=[scr_tok["e_all"]])
              Ea, Eab = stg.sb([128, 4, 2, 16], F32)
              Fa, Fab = stg.sb([128, 4, 2, 16], F32)
              S.dma("sp", Ea[:], e_all.rearrange("(r p) (a b) -> p r a b", p=128, a=2), reads=[scr_tok["e_all"]], writes=[Eab])
              S.add("dve", lambda e: e.memset(Fa[:], 0.0), writes=[Fab])
              A9r, A9i = ald[:, NLV - 1, 0, :], ald[:, NLV - 1, 1, :]
              for rnk in range(1, 4):
                  S.add("dve", lambda e, rnk=rnk: e.tensor_tensor(tA[:], Fa[:, rnk - 1, 0, :], A9r, ALU.mult), reads=[Fab, smb], writes=[smb])
                  S.add("dve", lambda e, rnk=rnk: e.tensor_tensor(tB[:], Fa[:, rnk - 1, 1, :], A9i, ALU.mult), reads=[Fab, smb], writes=[smb])
                  S.add("dve", lambda e: e.tensor_tensor(tA[:], tA[:], tB[:], ALU.subtract), reads=[smb], writes=[smb])
                  S.add("dve", lambda e, rnk=rnk: e.tensor_tensor(Fa[:, rnk, 0, :], tA[:], Ea[:, rnk - 1, 0, :], ALU.add), reads=[smb, Eab], writes=[Fab])
                  S.add("dve", lambda e, rnk=rnk: e.tensor_tensor(tA[:], Fa[:, rnk - 1, 0, :], A9i, ALU.mult), reads=[Fab, smb], writes=[smb])
                  S.add("dve", lambda e, rnk=rnk: e.tensor_tensor(tB[:], Fa[:, rnk - 1, 1, :], A9r, ALU.mult), reads=[Fab, smb], writes=[smb])
                  S.add("dve", lambda e: e.tensor_tensor(tA[:], tA[:], tB[:], ALU.add), reads=[smb], writes=[smb])
                  S.add("dve", lambda e, rnk=rnk: e.tensor_tensor(Fa[:, rnk, 1, :], tA[:], Ea[:, rnk - 1, 1, :], ALU.add), reads=[smb, Eab], writes=[Fab])
              S.dma("sp", f_scr.rearrange("(r p) (a b) -> p r a b", p=128, a=2), Fa[:], reads=[Fab], writes=[scr_tok["f_scr"]])
              Fo, Fob = stg.sb([128, 2, 16], F32)

              def ff(e):
                  r = rank_of(e)
                  return e.dma_start(out=Fo[:].rearrange("p a b -> p (a b)"), in_=f_scr[bass.ds(r * 128, 128), :])
              S.dma_fn("pool", ff, reads=[scr_tok["f_scr"]], writes=[Fob])

              NX = NK + 1
              pp4 = [[stg.sb([128, 2, NK + 8], F32) for _ in range(2)] for _ in range(2)]

              def scan_ops(i, vs, vsb, pp):
                  ops = []
                  src, srcbs = vs, [vsb]
                  for d in range(NLV):
                      s_ = 1 << d
                      dstt, dsttb = pp[d % 2]
                      dpre = pp_pre[id(dsttb)]
                      ar = ald[:, d, 0, i:i + 1]
                      ai = ald[:, d, 1, i:i + 1]
                      nai = ald[:, d, 2, i:i + 1]
                      lo = lambda ri, src=src, s_=s_: src[:, ri, 0:NX - s_]
                      hi = lambda ri, src=src, s_=s_: src[:, ri, s_:NX]
                      o_re = dstt[:, 0, s_:NX]
                      o_im = dstt[:, 1, s_:NX]
                      ops.append(("pool", lambda e, dstt=dstt, src=src, s_=s_: e.tensor_copy(dstt[:, :, 0:s_], src[:, :, 0:s_]), list(srcbs), [dpre]))
                      ops.append(("dve", lambda e, o=o_re, a=lo(0), b=hi(0), ar=ar: e.scalar_tensor_tensor(o, a, ar, b, ALU.mult, ALU.add), srcbs + [smb], [dsttb]))
                      ops.append(("dve", lambda e, o=o_im, a=lo(0), b=hi(1), ai=ai: e.scalar_tensor_tensor(o, a, ai, b, ALU.mult, ALU.add), srcbs + [smb, dsttb], [dsttb]))
                      ops.append(("dve", lambda e, o=o_re, a=lo(1), nai=nai: e.scalar_tensor_tensor(o, a, nai, o, ALU.mult, ALU.add), srcbs + [smb, dsttb], [dsttb]))
                      ops.append(("dve", lambda e, o=o_im, a=lo(1), ar=ar: e.scalar_tensor_tensor(o, a, ar, o, ALU.mult, ALU.add), srcbs + [smb, dsttb], [dsttb]))
                      src, srcbs = dstt, [dsttb, dpre]
                  ops.append(("act", lambda e, i=i, src=src: e.activation(out=Ssb[:, i, :, 0:NX], in_=src[:, :, 0:NX], func=AF.Copy), list(srcbs), [Ssbb]))
                  return ops

              pp_pre = {}
              for pl in pp4:
                  for (_t, _b) in pl:
                      pp_pre[id(_b)] = Buf("pre")

              yg, ygb = stg.sb([128, 4, T], BF16)
              pY = [stg.ps() for _ in range(2)]
              glups = [stg.ps() for _ in range(2)]
              iy = [0]
              def emit_out(q):
                  for j in range(RS):
                      p, pb = pY[iy[0] % 2]
                      iy[0] += 1
                      first = True
                      for tau in range(j + 1):
                          S.add("pe", lambda e, p=p, q=q, tau=tau, j=j, first=first: e.matmul(p[:, 0:NK], Kt[:, q, tau, :], uT[:, q, (j - tau):T:RS], start=first, stop=False),
                                reads=[tokK, uTb], writes=[pb])
                          first = False
                      for m in range(4):
                          i = 4 * q + m
                          ms = slice(m * 32, (m + 1) * 32)
                          for ri in range(2):
                              lastmm = (m == 3 and ri == 1)
                              S.add("pe", lambda e, p=p, ms=ms, j=j, ri=ri, i=i, lastmm=lastmm: e.matmul(p[ms, 0:NK], Gb[:, j, ri, i, :], Ssb[:, i, ri, 0:NK], start=False, stop=lastmm, tile_position=(0, ms.start)),
                                    reads=[tokG, Ssbb], writes=[pb])
                      S.add("act", lambda e, p=p, q=q, j=j: e.activation(out=yg[:, q, j:T:RS], in_=p[:, 0:NK], func=AF.Gelu_apprx_tanh), reads=[pb], writes=[ygb])
              for ip_ in range(8):
                  lists = []
                  for t_ in range(2):
                      i = 2 * ip_ + t_
                      vs, vsb = Vst[t_]
                      compute_V(i, vs, vsb)
                      for ri in range(2):
                          S.add("pool", lambda e, vs=vs, ri=ri, i=i: e.tensor_copy(vs[:, ri, 0:1], Fo[:, ri, i:i + 1]), reads=[Fob], writes=[vsb])
                  if ip_ >= 2 and ip_ % 2 == 0:
                      emit_out((ip_ - 2) // 2)
                  for t_ in range(2):
                      i = 2 * ip_ + t_
                      vs, vsb = Vst[t_]
                      lists.append(scan_ops(i, vs, vsb, pp4[t_]))
                  interleave(lists)
              emit_out(3)

              gs, gsb = stg.sb([128, 4, T], BF16)
              sg = [stg.sb([128, 512], BF16) for _ in range(2)]
              gi = [0]
              bgl = VC["bglu"]

              def cons_glu(oc, m, tb, ps, psb):
                  s_, s_b = sg[gi[0] % 2]
                  gi[0] += 1
                  S.add("act", lambda e: e.activation(out=s_[:], in_=ps[:], func=AF.Sigmoid, bias=vt[:, bgl + oc:bgl + oc + 1]), reads=[psb, vb], writes=[s_b])
                  S.add("dve", lambda e: e.tensor_tensor(gs[:, oc, tb * 512:(tb + 1) * 512], yg[:, oc, tb * 512:(tb + 1) * 512], s_[:], ALU.mult),
                        reads=[s_b, ygb], writes=[gsb])
              linear_T(stg, yg, ygb, 4, kview(W["wglu"][l]), 0, 512, cons_glu, psums=glups)
              S.dma("sp", kview(gsT_d), gs[:], reads=[gsb], writes=[scr_tok["gsT"]])
              stg.done()

              chk("SSM%d" % l)
              stg = Stage()
              QT, QTb = stg.sb([96, 2, 4 * T], BF16)
              KT, KTb = stg.sb([96, 2, 4 * T], BF16)
              VA, VAb = stg.sb([128, 2, 64, 128], BF16)
              mskf, mskfb = stg.sb([128, 4, 512], F32)
              msk, mskb = stg.sb([128, 4, 512], BF16)
              S.dma("sp", mskf[:], masks_in, writes=[mskfb])
              S.add("dve", lambda e: e.tensor_copy(msk[:], mskf[:]), reads=[mskfb], writes=[mskb])
              QTbs = [Buf("QT%d" % i_) for i_ in range(4)]
              KTbs = [Buf("KT%d" % i_) for i_ in range(4)]
              VAbs = [Buf("VA%d" % i_) for i_ in range(4)]
              S.add("pool", lambda e: e.memset(VA[:], 1.0), writes=VAbs)
              for i in range(4):
                  for hh in range(2):
                      for (mine, dst, dstb, mname) in ((q_mine, QT, QTbs[i], "q_mine"), (k_mine, KT, KTbs[i], "k_mine")):
                          S.dma("sp", dst[0:96, hh, i * T:(i + 1) * T], mine[i * 192 + hh * 96:i * 192 + (hh + 1) * 96, :],
                                reads=[scr_tok[mname]], writes=[dstb])
                      for tq in range(4):
                          srcv = v_mine[i * T + tq * 512:i * T + (tq + 1) * 512, hh * 64:(hh + 1) * 64].rearrange("(t p) c -> p t c", p=128)
                          S.dma("sp", VA[:, hh, i * 16 + tq * 4:i * 16 + (tq + 1) * 4, 0:64], srcv, reads=[scr_tok["v_mine"]], writes=[VAbs[i]])
              if "qdbg" in dbg:
                  S.dma("sp", qdbg[0:96, :], QT[0:96, 0, :], reads=[QTb], writes=[scr_tok["qdbg"]])
                  S.dma("sp", qdbg[96:192, :], KT[0:96, 0, :], reads=[KTb], writes=[scr_tok["qdbg"]])
              pSa = [stg.ps([128, 1024]) for _ in range(3)]
              pOa = [stg.ps() for _ in range(2)]
              pTa = [stg.sb([128, 1024], BF16) for _ in range(4)]
              rden = [stg.sb([128, 512], F32) for _ in range(2)]
              ao = [stg.sb([128, 512], BF16) for _ in range(2)]
              tiles = []
              for qb in range(16):
                  for hh in range(2):
                      nkt = 4 * qb + 4
                      for kt in range(0, nkt, 2):
                          tiles.append((qb, hh, kt, nkt))
              LA = 2
              a_tok = [Buf("a_send%d" % j) for j in range(4)]
              msk2 = msk[:].rearrange("p a b -> p (a b)")

              def emit_qk(n):
                  qb, hh, kt, nkt = tiles[n]
                  qsl = slice(qb * 512, (qb + 1) * 512)
                  ps_, psb_ = pSa[n % 3]
                  pt, ptb = pTa[n % 4]
                  for u in range(2):
                      S.add("pe", lambda e, u=u: e.matmul(ps_[:, u * 512:(u + 1) * 512], KT[0:96, hh, (kt + u) * 128:(kt + u + 1) * 128], QT[0:96, hh, qsl], start=True, stop=True),
                            reads=[KTbs[(kt * 128) // T], QTbs[qb // 4]], writes=[psb_])
                  S.add("act", lambda e: e.activation(out=pt[:], in_=ps_[:], func=AF.Exp, scale=96 ** -0.5, bias=-6.0),
                        reads=[psb_], writes=[ptb])
                  if kt >= 4 * qb:
                      j = kt - 4 * qb
                      S.add("dve", lambda e: e.tensor_tensor(pt[:], pt[:], msk2[:, j * 512:(j + 2) * 512], ALU.mult), reads=[ptb, mskb], writes=[ptb])

              def emit_pv(n):
                  qb, hh, kt, nkt = tiles[n]
                  pt, ptb = pTa[n % 4]
                  po, pob = pOa[(qb * 2 + hh) % 2]
                  rdn, rdnb = rden[(qb * 2 + hh) % 2]
                  o, obf = ao[qb % 2]
                  for u in range(2):
                      S.add("pe", lambda e, u=u: e.matmul(po[:], VA[:, hh, kt + u, :], pt[:, u * 512:(u + 1) * 512], start=(kt + u == 0), stop=(kt + u == nkt - 1)),
                            reads=[VAbs[kt // 16], ptb], writes=[pob])
                  if kt + 2 == nkt:
                      S.add("dve", lambda e: e.reciprocal(rdn[64:128, :], po[64:128, :]), reads=[pob], writes=[rdnb])
                      S.add("dve", lambda e: e.tensor_tensor(o[hh * 64:(hh + 1) * 64, :], po[0:64, :], rdn[64:128, :], ALU.mult),
                            reads=[pob, rdnb], writes=[obf])
                      if hh == 1:
                          j = qb // 4
                          S.dma("sp", a_send[j * 128:(j + 1) * 128, (qb % 4) * 512:(qb % 4 + 1) * 512], o[:], reads=[obf], writes=[a_tok[j]])
                          if qb % 4 == 3:
                              S.cc(lambda e: e.collective_compute("AllGather", ALU.bypass, replica_groups=RG, ins=[a_send[j * 128:(j + 1) * 128, :].opt()],
                                                                  outs=[a_all[j * 512:(j + 1) * 512, :].opt()]),
                                   reads=[a_tok[j]], writes=[scr_tok["a_all"]], unit="cca")

              for n in range(len(tiles) + LA):
                  if n < len(tiles):
                      emit_qk(n)
                  if n >= LA:
                      emit_pv(n - LA)
              stg.done()

              chk("B1%d" % l)
              stg = Stage()
              vt, vb = load_vec(stg, l)
              hT, _ = stg.sb([128, 8, T], BF16)
              hTbs = [Buf("hTb%d" % i_) for i_ in range(NB)]
              acts = []
              for nm in ("attnT", "gsT", "xaT"):
                  t_, _ = stg.sb([128, 4, T], BF16)
                  acts.append((t_, [Buf("%s%d" % (nm, i_)) for i_ in range(NB)]))
              for i_ in range(NB):
                  csl = slice(i_ * 512, (i_ + 1) * 512)
                  S.dma("sp", hT[:, :, csl], kview(hT_d)[:, :, csl], reads=[scr_tok["hT"]], writes=[hTbs[i_]])
                  for nm, (t_, tbs_) in zip(("attnT", "gsT", "xaT"), acts):
                      if nm != "attnT":
                          S.dma("sp", t_[:, :, csl], kview(scr[nm])[:, :, csl], reads=[scr_tok[nm]], writes=[tbs_[i_]])
              def fa(e):
                  r = rank_of(e)
                  return e.dma_start(out=attnT_d, in_=a_all[bass.ds(r * 512, 512), :])
              S.dma_fn("sp", fa, reads=[scr_tok["a_all"]], writes=[scr_tok["attnT"]])
              for i_ in range(NB):
                  csl = slice(i_ * 512, (i_ + 1) * 512)
                  S.dma("sp", acts[0][0][:, :, csl], kview(scr["attnT"])[:, :, csl], reads=[scr_tok["attnT"]], writes=[acts[0][1][i_]])
              mg, _ = stg.sb([128, 8, T], BF16)
              mgbs = [[Buf("mg%d_%d" % (o_, i_)) for i_ in range(NB)] for o_ in range(8)]
              wo = [[stg.sb([128, 4, 128], BF16) for _ in range(3)] for _ in range(2)]
              wg = [[stg.sb([128, 8, 128], BF16) for _ in range(3)] for _ in range(2)]
              wov = [kview(W[n][l]) for n in ("womla", "wossm", "wocross")]
              pyy = [stg.ps() for _ in range(3)]
              pgg = [stg.ps() for _ in range(3)]
              sgt = [stg.sb([128, 512], F32) for _ in range(3)]
              tmpm = [stg.sb([128, 512], F32) for _ in range(4)]
              bg = VC["bgate"]
              imc = [0]

              def load_w(oc, brs):
                  for br in brs:
                      S.dma("pool", wo[oc % 2][br][0][:], wov[br][:, :, oc * 128:(oc + 1) * 128], writes=[wo[oc % 2][br][1]])
                      S.dma("pool", wg[oc % 2][br][0][:], win_v[:, :, 1696 + br * 1024 + oc * 128:1696 + br * 1024 + (oc + 1) * 128], writes=[wg[oc % 2][br][1]])

              def grp(oc, tb, br):
                  sl = slice(tb * 512, (tb + 1) * 512)
                  im = imc[0]
                  imc[0] += 1
                  py, pyb = pyy[im % 3]
                  pg, pgb = pgg[im % 3]
                  sgx, sgxb = sgt[im % 3]
                  wot, wotb = wo[oc % 2][br]
                  wgt, wgtb = wg[oc % 2][br]
                  at, atbs = acts[br]
                  for k in range(4):
                      S.add("pe", lambda e, k=k: e.matmul(py[:], wot[:, k, :], at[:, k, sl], start=(k == 0), stop=(k == 3)),
                            reads=[wotb, atbs[tb]], writes=[pyb])
                  for k in range(8):
                      S.add("pe", lambda e, k=k: e.matmul(pg[:], wgt[:, k, :], hT[:, k, sl], start=(k == 0), stop=(k == 7)),
                            reads=[wgtb, hTbs[tb]], writes=[pgb])
                  S.add("act", lambda e: e.activation(out=sgx[:], in_=pg[:], func=AF.Sigmoid, bias=vt[:, bg + br * 8 + oc:bg + br * 8 + oc + 1]),
                        reads=[pgb, vb], writes=[sgxb])
                  return py, pyb, sgx, sgxb

              itm = 0
              for oc in range(8):
                  load_w(oc, (1, 2))
                  for tb in range(NB):
                      sl = slice(tb * 512, (tb + 1) * 512)
                      t1, t1b = tmpm[itm % 4]
                      t2, t2b = tmpm[(itm + 1) % 4]
                      itm += 2
                      py, pyb, sgx, sgxb = grp(oc, tb, 1)
                      S.add("dve", lambda e, t1=t1, py=py, sgx=sgx: e.tensor_tensor(t1[:], py[:], sgx[:], ALU.mult), reads=[pyb, sgxb], writes=[t1b])
                      py, pyb, sgx, sgxb = grp(oc, tb, 2)
                      S.add("dve", lambda e, t2=t2, py=py, sgx=sgx: e.tensor_tensor(t2[:], py[:], sgx[:], ALU.mult), reads=[pyb, sgxb], writes=[t2b])
                      S.add("dve", lambda e, t1=t1, t2=t2, oc=oc, sl=sl: e.tensor_tensor(mg[:, oc, sl], t1[:], t2[:], ALU.add), reads=[t1b, t2b], writes=[mgbs[oc][tb]])
              for oc in range(8):
                  load_w(oc, (0,))
                  for tb in range(NB):
                      sl = slice(tb * 512, (tb + 1) * 512)
                      t1, t1b = tmpm[itm % 4]
                      itm += 1
                      py, pyb, sgx, sgxb = grp(oc, tb, 0)
                      S.add("dve", lambda e, t1=t1, py=py, sgx=sgx: e.tensor_tensor(t1[:], py[:], sgx[:], ALU.mult), reads=[pyb, sgxb], writes=[t1b])
                      S.add("dve", lambda e, t1=t1, oc=oc, sl=sl: e.tensor_tensor(mg[:, oc, sl], mg[:, oc, sl], t1[:], ALU.add), reads=[t1b, mgbs[oc][tb]], writes=[mgbs[oc][tb]])
              mgb = lambda k_, tb_: mgbs[k_][tb_]
              lin2 = [stg.ps() for _ in range(2)]
              xin = [stg.sb([128, 512], F32) for _ in range(6)]
              xo_ = [stg.sb([128, 512], F32) for _ in range(3)]
              ix = [0]
              xcv = kview(x_cur)
              x2v = kview(x2_d)

              NPF = 6

              def load_x(g):
                  oc_, tb_ = g // NB, g % NB
                  xi_, xib_ = xin[g % NPF]
                  S.dma("sp", xi_[:], xcv[:, oc_, tb_ * 512:(tb_ + 1) * 512], reads=[xb_cur], writes=[xib_])
              for g_ in range(NPF):
                  load_x(g_)

              def cons_out(oc, m, tb, ps, psb):
                  g = ix[0]
                  xi, xib = xin[g % NPF]
                  xo2, xo2b = xo_[g % 3]
                  ix[0] += 1
                  sl = slice(tb * 512, (tb + 1) * 512)
                  S.add("dve", lambda e: e.tensor_tensor(xo2[:], ps[:], xi[:], ALU.add), reads=[psb, xib], writes=[xo2b])
                  if g + NPF < 8 * NB:
                      load_x(g + NPF)
                  S.dma("sp", x2v[:, oc, sl], xo2[:], reads=[xo2b], writes=[scr_tok["x2"]])
                  if tb == NB - 1:
                      S.dma("sp", xh_own[oc * 128:(oc + 1) * 128, :], xo2[:, 510:512], reads=[xo2b], writes=[scr_tok["xh_own"]])
              linear_T(stg, mg, mgb, 8, kview(W["wout"][l]), 0, D, cons_out, psums=lin2 + pyy + pgg)
              stg.done()
              chk("B3%d" % l)
              S.cc(lambda e: e.collective_compute("AllGather", ALU.bypass, replica_groups=RG, ins=[xh_own.opt()], outs=[xh_all.opt()]),
                   reads=[scr_tok["xh_own"]], writes=[scr_tok["xh_all"]])

              stg = Stage()
              vt, vb = load_vec(stg, l)
              zt, ztb = stg.sb([128, 8, 2], F32)
              S.add("dve", lambda e: e.memset(zt[:], 0.0), writes=[ztb])
              if l == 0:
                  S.dma("sp", xh_ext[0:D, :].rearrange("(k p) t -> p k t", p=128), zt[:], reads=[ztb], writes=[scr_tok["xh_ext"]])
              S.dma("sp", xh_ext[D:5 * D, :], xh_all, reads=[scr_tok["xh_all"]], writes=[scr_tok["xh_ext"]])
              TH = T + 2
              x2v = kview(x2_d)
              x2, x2b = stg.sb([128, 8, TH], F32)
              h2, h2b = stg.sb([128, 8, TH], BF16)
              x2bs = [Buf("x2h")] + [Buf("x2b%d" % i_) for i_ in range(NB)]
              h2bs = [Buf("h2b%d" % i_) for i_ in range(NB + 1)]
              for i_ in range(NB):
                  S.dma("sp", x2[:, :, 2 + i_ * 512:2 + (i_ + 1) * 512], x2v[:, :, i_ * 512:(i_ + 1) * 512], reads=[scr_tok["x2"]], writes=[x2bs[1 + i_]])

              def fh(e):
                  r = rank_of(e)
                  return e.dma_start(out=x2[:, :, 0:2], in_=xh_ext[bass.ds(r * D, D), :].rearrange("(k p) t -> p k t", p=128))
              S.dma_fn("act", fh, reads=[scr_tok["xh_ext"]], writes=[x2bs[0]])
              sq, sqb = stg.sb([128, 8, 512], BF16)
              pss, pssb = stg.ps()
              rs, rsb = stg.sb([128, 512], F32)
              tm, tmb = stg.sb([128, 512], F32)
              fg = VC["ffng"]
              blocks = [(0, 2)] + [(2 + tb * 512, 2 + (tb + 1) * 512) for tb in range(NB)]
              for bi_, (c0, c1) in (list(enumerate(blocks))[1:] + [(0, blocks[0])]):
                  w_ = c1 - c0
                  x2b = x2bs[bi_]
                  for k in range(8):
                      S.add("act", lambda e, k=k, c0=c0, c1=c1, w_=w_: e.activation(out=sq[:, k, 0:w_], in_=x2[:, k, c0:c1], func=AF.Square), reads=[x2b], writes=[sqb])
                  for k in range(8):
                      S.add("pe", lambda e, k=k, w_=w_: e.matmul(pss[:, 0:w_], ones_b[:], sq[:, k, 0:w_], start=(k == 0), stop=(k == 7)), reads=[sqb, Bc], writes=[pssb])
                  rstd_from(pss[:, 0:w_], rs[:, 0:w_], D, [pssb], [rsb], tm[:, 0:w_], tmb)
                  for k in range(8):
                      S.add("dve", lambda e, k=k, c0=c0, c1=c1, w_=w_: e.scalar_tensor_tensor(h2[:, k, c0:c1], x2[:, k, c0:c1], vt[:, fg + k:fg + k + 1], rs[:, 0:w_], ALU.mult, ALU.mult),
                            reads=[x2b, rsb, vb], writes=[h2bs[bi_]])
                  S.dma("sp", kview(h2T_d)[:, :, c0:c1], h2[:, :, c0:c1], reads=[h2bs[bi_]], writes=[scr_tok["h2T"]])
              stg.done()
              chk("C1%d" % l)
              stg = Stage()
              vt, vb = load_vec(stg, l)
              h2, h2b = stg.sb([128, 8, TH], BF16)
              S.dma("sp", h2[:], kview(h2T_d), reads=[scr_tok["h2T"]], writes=[h2b])
              wupv = kview(W["wup"][l])
              wu = [stg.sb([128, 8, 128], BF16) for _ in range(4)]
              ups = [[stg.sb([128, TH], F32) for _ in range(2)] for _ in range(2)]
              cvs = [[stg.sb([128, T], F32) for _ in range(2)] for _ in range(2)]
              aos = [stg.sb([128, T], BF16) for _ in range(2)]
              pU = [stg.ps() for _ in range(4)]
              iu = 0
              def load_wu(fc_):
                  for half_ in range(2):
                      wt_, wtb_ = wu[(fc_ * 2 + half_) % 4]
                      col_ = half_ * DFF + fc_ * 128
                      S.dma("pool", wt_[:], wupv[:, :, col_:col_ + 128], writes=[wtb_])
              load_wu(0)
              for fc in range(22):
                  if fc + 1 < 22:
                      load_wu(fc + 1)
                  for half in range(2):
                      wt, wtb = wu[(fc * 2 + half) % 4]
                      up, upb = ups[fc % 2][half]
                      for (c0, c1) in blocks:
                          w_ = c1 - c0
                          p, pb = pU[iu % 4]
                          iu += 1
                          for k in range(8):
                              S.add("pe", lambda e, p=p, wt=wt, k=k, c0=c0, c1=c1, w_=w_: e.matmul(p[:, 0:w_], wt[:, k, :], h2[:, k, c0:c1], start=(k == 0), stop=(k == 7)),
                                    reads=[wtb, h2b], writes=[pb])
                          S.add("act", lambda e, p=p, up=up, c0=c0, c1=c1, w_=w_: e.activation(out=up[:, c0:c1], in_=p[:, 0:w_], func=AF.Copy), reads=[pb], writes=[upb])
                      cv, cvb = cvs[fc % 2][half]
                      ch = half * 22 + fc
                      w0, w1, w2, cbb = VC["cw0"] + ch, VC["cw1"] + ch, VC["cw2"] + ch, VC["cb"] + ch
                      S.add("dve", lambda e, cv=cv, up=up, w2=w2, cbb=cbb: e.tensor_scalar(cv[:], up[:, 2:TH], vt[:, w2:w2 + 1], vt[:, cbb:cbb + 1], ALU.mult, ALU.add),
                            reads=[upb, vb], writes=[cvb])
                      S.add("dve", lambda e, cv=cv, up=up, w1=w1: e.scalar_tensor_tensor(cv[:], up[:, 1:TH - 1], vt[:, w1:w1 + 1], cv[:], ALU.mult, ALU.add),
                            reads=[upb, vb, cvb], writes=[cvb])
                      S.add("dve", lambda e, cv=cv, up=up, w0=w0: e.scalar_tensor_tensor(cv[:], up[:, 0:TH - 2], vt[:, w0:w0 + 1], cv[:], ALU.mult, ALU.add),
                            reads=[upb, vb, cvb], writes=[cvb])
                  cg, cgb = cvs[fc % 2][0]
                  cv_, cv_b = cvs[fc % 2][1]
                  ao_, ao_b = aos[fc % 2]
                  S.add("act", lambda e, cg=cg: e.activation(out=cg[:], in_=cg[:], func=AF.Silu), reads=[cgb], writes=[cgb])
                  S.add("dve", lambda e, ao_=ao_, cg=cg, cv_=cv_: e.tensor_tensor(ao_[:], cg[:], cv_[:], ALU.mult), reads=[cgb, cv_b], writes=[ao_b])
                  S.dma("sp", actT_d[fc * 128:(fc + 1) * 128, :], ao_[:], reads=[ao_b], writes=[scr_tok["actT"]])
              stg.done()
              chk("C2%d" % l)
              stg = Stage()
              actT, actTb = stg.sb([128, 22, T], BF16)
              aTbs = [[Buf("aT%d_%d" % (i_, j_)) for j_ in range(NB)] for i_ in range(22)]
              for j_ in range(NB):
                  for i_ in range(22):
                      S.dma("sp", actT[:, i_, j_ * 512:(j_ + 1) * 512], actT_d[i_ * 128:(i_ + 1) * 128, j_ * 512:(j_ + 1) * 512],
                            reads=[scr_tok["actT"]], writes=[aTbs[i_][j_]])
              actTb = lambda k_, tb_: aTbs[k_][tb_]
              xo3 = [stg.sb([128, 512], F32) for _ in range(3)]
              xi3 = [stg.sb([128, 512], F32) for _ in range(3)]
              i3 = [0]
              xnv = kview(x_nxt)

              def cons_down(oc, m, tb, ps, psb):
                  xo2, xo2b = xo3[i3[0] % 3]
                  xi, xib = xi3[i3[0] % 3]
                  i3[0] += 1
                  sl = slice(tb * 512, (tb + 1) * 512)
                  S.dma("sp", xi[:], x2v[:, oc, sl], reads=[scr_tok["x2"]], writes=[xib])
                  S.add("dve", lambda e: e.tensor_tensor(xo2[:], ps[:], xi[:], ALU.add), reads=[psb, xib], writes=[xo2b])
                  S.dma("sp", xnv[:, oc, sl], xo2[:], reads=[xo2b], writes=[xb_nxt])
              linear_T(stg, actT, actTb, 22, W["wdown"][l].rearrange("(k p) n -> p k n", p=128), 0, D, cons_down)
              stg.done()

        except _Stop:
            pass

        S.barrier(final=True)
        S.flush(top, barrier=False)
    return nc


_NC_CACHE = {}


def make_in_maps(inp):
    x = np.asarray(inp["x"], np.float32)
    mem = np.asarray(inp["mem"], np.float32)
    pos = np.asarray(inp["positions"], np.int32)
    ident = np.eye(128, dtype=np.float32)
    masks = np.zeros((128, 4, 512), np.float32)
    kk = np.arange(128)[:, None]
    qq = np.arange(512)[None, :]
    for j in range(4):
        masks[:, j, :] = (qq >= kk + 128 * j).astype(np.float32)
    invf = np.zeros((128, 1), np.float32)
    f = (10000.0 ** (-np.arange(0, 32, 2, dtype=np.float32) / 32)).astype(np.float32)
    invf[0:16, 0] = f
    invf[16:32, 0] = f
    invf[32:48, 0] = -1.0
    invf[48:64, 0] = 1.0
    shared = {"ident": ident, "masks": masks, "invf": invf}
    sw = np.concatenate([np.arange(16, 32), np.arange(0, 16)])
    for l in range(2):
        w_in = np.asarray(inp["w_in"][l], np.float32)
        shared["win%d" % l] = np.ascontiguousarray(np.concatenate([w_in, w_in[:, 640 + sw]], axis=1))
        wq = np.asarray(inp["w_q_b"][l], np.float32)
        qsw_cols = np.concatenate([h * 96 + 64 + sw for h in range(8)])
        shared["wqb%d" % l] = np.ascontiguousarray(np.concatenate([wq, wq[:, qsw_cols]], axis=1))
        wkv = np.asarray(inp["w_kv_b"][l], np.float32).reshape(256, 8, 128)
        shared["wkvk%d" % l] = np.ascontiguousarray(wkv[:, :, :64].reshape(256, 512))
        shared["wkvv%d" % l] = np.ascontiguousarray(wkv[:, :, 64:].reshape(256, 512))
        for nm, key in (("womla", "w_o_mla"), ("wossm", "w_o_ssm"), ("wocross", "w_o_cross"), ("wglu", "w_glu"),
                        ("wmemkv", "w_mem_kv"), ("wout", "w_out"), ("wup", "w_up"), ("wdown", "w_down")):
            shared["%s%d" % (nm, l)] = np.ascontiguousarray(np.asarray(inp[key][l], np.float32))
        shared["vec%d" % l] = pack_vec(inp, l)
        bb, cb = pack_bc(inp, l)
        shared["bblk%d" % l] = bb.reshape(128, -1)
        shared["cblk%d" % l] = cb.reshape(128, -1)
    maps = []
    for c in range(8):
        b, r = c // 4, c % 4
        m = dict(shared)
        m["xT"] = np.ascontiguousarray(x[b, r * T:(r + 1) * T, :].T)
        m["memT"] = np.ascontiguousarray(mem[b].T)
        m["pos"] = np.ascontiguousarray(pos[b, r * T:(r + 1) * T][None, :])
        maps.append(m)
    return maps


def kernel(**inputs):
    if "nc" not in _NC_CACHE:
        _NC_CACHE["nc"] = build()
    nc = _NC_CACHE["nc"]
    maps = make_in_maps(inputs)
    res = run_bass_kernel_spmd(nc, maps, core_ids=list(range(8)))
    out = np.zeros((2, 4 * T, D), np.float32)
    for c in range(8):
        b, r = c // 4, c % 4
        out[b, r * T:(r + 1) * T, :] = np.asarray(res.results[c]["yT"]).T
    return out
```
